# Optimizing a Trainium2 kernel written in Bass

```python
import math
import jax, jax.numpy as jnp
from jax import lax
import numpy as np

D_MODEL = 2048
BATCH = 8
SEQ = 2048
DEPTH = 1
DEC_BATCH = 32
DEC_SEQ = 4
PAST_LEN = 16384
PAGE_SIZE = 128

MIX_WIDTH = D_MODEL
ATTN_WIDTH = MIX_WIDTH // 2
POOL_WIDTH = MIX_WIDTH - ATTN_WIDTH
HEAD_DIM = 64
N_HEADS_A = ATTN_WIDTH // HEAD_DIM
DILATED_CONFIGS = ((128, 1), (512, 4), (2048, 16))
MAX_WINDOW = max(w for w, _ in DILATED_CONFIGS)
BLOCK = 128
POOL_WINDOWS = (2, 4, 8, 16)
N_POOL_GROUPS = len(POOL_WINDOWS)
POOL_GROUP_DIM = POOL_WIDTH // N_POOL_GROUPS
POOL_STATE = max(POOL_WINDOWS) - 1
NUM_BUCKETS = 32
MAX_DISTANCE = MAX_WINDOW
D_FF = -(-8 * D_MODEL // (3 * 256)) * 256
IN_COLS = 3 * ATTN_WIDTH + POOL_WIDTH
EPS = 1e-6

kernel_name = "hymba_dilated_pool_decoder_step"


def rmsnorm(x, g):
    xf = x.astype(jnp.float32)
    y = xf * lax.rsqrt(jnp.mean(xf * xf, axis=-1, keepdims=True) + EPS)
    return (y * g.astype(jnp.float32)).astype(x.dtype)


def t5_bucket(dist):
    max_exact = NUM_BUCKETS // 2
    df = jnp.maximum(dist, 1).astype(jnp.float32)
    large = max_exact + (jnp.log(df / max_exact) / math.log(MAX_DISTANCE / max_exact)
                         * (NUM_BUCKETS - max_exact)).astype(jnp.int32)
    large = jnp.minimum(large, NUM_BUCKETS - 1)
    return jnp.where(dist < max_exact, dist, large)


def to_strided(t, dil):
    B, S = t.shape[0], t.shape[1]
    rest = t.shape[2:]
    L = S // dil
    return t.reshape((B, L, dil) + rest).swapaxes(1, 2).reshape((B * dil, L) + rest)


def from_strided(t, B, dil, L):
    rest = t.shape[2:]
    t = t[:, :L]
    return t.reshape((B, dil, L) + rest).swapaxes(1, 2).reshape((B, dil * L) + rest)


def dilated_attention_prompt(q, k, v, rel_bias):
    B, S, H, Dh = q.shape
    scale = HEAD_DIM ** -0.5
    nums, ms, ss = [], [], []
    for window, dil in DILATED_CONFIGS:
        sub_w = window // dil
        L = S // dil
        nb = -(-L // BLOCK)
        Lp = nb * BLOCK
        Bd = B * dil
        qs = jnp.pad(to_strided(q, dil), ((0, 0), (0, Lp - L), (0, 0), (0, 0)))
        kpad = ((0, 0), (BLOCK, Lp - L), (0, 0), (0, 0))
        ks = jnp.pad(to_strided(k, dil), kpad)
        vs = jnp.pad(to_strided(v, dil), kpad)
        qb = qs.reshape(Bd, nb, BLOCK, H, Dh)
        kb = jnp.concatenate([ks[:, :Lp].reshape(Bd, nb, BLOCK, H, Dh),
                              ks[:, BLOCK:].reshape(Bd, nb, BLOCK, H, Dh)], axis=2)
        vb = jnp.concatenate([vs[:, :Lp].reshape(Bd, nb, BLOCK, H, Dh),
                              vs[:, BLOCK:].reshape(Bd, nb, BLOCK, H, Dh)], axis=2)
        qi = jnp.arange(BLOCK)
        kj = jnp.arange(2 * BLOCK)
        dist = BLOCK + qi[:, None] - kj[None, :]
        key_pos = (jnp.arange(nb)[:, None] - 1) * BLOCK + kj[None, :]
        valid = ((dist >= 0) & (dist <= sub_w))[None] & (key_pos >= 0)[:, None, :]
        bias_sub = rel_bias[t5_bucket(dil * jnp.arange(sub_w + 1))].T
        bias = bias_sub[:, jnp.clip(dist, 0, sub_w)].astype(jnp.float32)
        logits = jnp.einsum('bnqhd,bnkhd->bnhqk', qb, kb,
                            preferred_element_type=jnp.float32) * scale + bias[None, None]
        logits = jnp.where(valid[None, :, None], logits, -jnp.inf)
        m = jnp.max(logits, axis=-1)
        p = jnp.exp(logits - m[..., None])
        s = jnp.sum(p, axis=-1)
        num = jnp.einsum('bnhqk,bnkhd->bnqhd', p, vb.astype(jnp.float32))
        nums.append(from_strided(num.reshape(Bd, Lp, H, Dh), B, dil, L))
        ms.append(from_strided(m.transpose(0, 1, 3, 2).reshape(Bd, Lp, H), B, dil, L))
        ss.append(from_strided(s.transpose(0, 1, 3, 2).reshape(Bd, Lp, H), B, dil, L))
    m_all = jnp.stack(ms, 0)
    wts = jnp.exp(m_all - jnp.max(m_all, axis=0))
    num = sum(wts[i][..., None] * nums[i] for i in range(len(nums)))
    den = jnp.sum(wts * jnp.stack(ss, 0), axis=0)
    out = num / den[..., None]
    return out.reshape(B, S, H * Dh).astype(q.dtype)


def dilated_attention_sample(q, k, v, ck, cv, rel_bias):
    DB, T, H, Dh = q.shape
    WC = ck.shape[1]
    scale = HEAD_DIM ** -0.5
    dists = jnp.concatenate([dil * jnp.arange(w // dil + 1) for w, dil in DILATED_CONFIGS])
    bias = rel_bias[t5_bucket(dists)].T.astype(jnp.float32)
    kcat = jnp.concatenate([ck.astype(k.dtype), k], axis=1)
    vcat = jnp.concatenate([cv.astype(v.dtype), v], axis=1)
    idx = WC + jnp.arange(T)[:, None] - dists[None, :]
    valid = idx >= 0
    idxc = jnp.maximum(idx, 0)
    kg = kcat[:, idxc]
    vg = vcat[:, idxc]
    logits = jnp.einsum('bthd,btkhd->bhtk', q, kg,
                        preferred_element_type=jnp.float32) * scale + bias[None, :, None, :]
    logits = jnp.where(valid[None, None], logits, -jnp.inf)
    p = jax.nn.softmax(logits, axis=-1)
    out = jnp.einsum('bhtk,btkhd->bthd', p, vg.astype(jnp.float32))
    return out.reshape(DB, T, H * Dh).astype(q.dtype)


def pool_mix(ucat, n_hist, w_pool, pool_scale):
    B, R, C = ucat.shape
    uf = ucat.astype(jnp.float32)
    cs = jnp.concatenate([jnp.zeros((B, 1, C), jnp.float32), jnp.cumsum(uf, axis=1)], axis=1)
    r = jnp.arange(n_hist, R)
    outs = []
    for g, w in enumerate(POOL_WINDOWS):
        sl = slice(g * POOL_GROUP_DIM, (g + 1) * POOL_GROUP_DIM)
        lo = jnp.maximum(r + 1 - w, 0)
        cnt = (r + 1 - lo).astype(jnp.float32)
        csg = cs[..., sl]
        pooled = (csg[:, r + 1] - csg[:, lo]) / cnt[None, :, None] - uf[:, n_hist:, sl]
        outs.append(pooled @ w_pool[g].astype(jnp.float32))
    y = jnp.concatenate(outs, axis=-1) * pool_scale.astype(jnp.float32)
    return y.astype(ucat.dtype)


def mix_input(x, norm_mix, w_in):
    B, T, _ = x.shape
    proj = rmsnorm(x, norm_mix) @ w_in
    q = proj[..., :ATTN_WIDTH].reshape(B, T, N_HEADS_A, HEAD_DIM)
    k = proj[..., ATTN_WIDTH:2 * ATTN_WIDTH].reshape(B, T, N_HEADS_A, HEAD_DIM)
    v = proj[..., 2 * ATTN_WIDTH:3 * ATTN_WIDTH].reshape(B, T, N_HEADS_A, HEAD_DIM)
    u = proj[..., 3 * ATTN_WIDTH:]
    return q, k, v, u


def layer_tail(x, attn_out, pool_out, w_out, norm_ffn, w_gate, w_up, w_down):
    x = x + jnp.concatenate([attn_out, pool_out], axis=-1) @ w_out
    h = rmsnorm(x, norm_ffn)
    return x + (jax.nn.silu(h @ w_gate) * (h @ w_up)) @ w_down


def setup_inputs(seed: int = 0) -> dict:
    key = jax.random.key(seed)
    ks = jax.random.split(key, 16)
    win_cache = min(MAX_WINDOW, PAST_LEN)
    nrm = jax.random.normal
    f32 = jnp.float32
    return {
        "x_prompt": nrm(ks[0], (BATCH, SEQ, D_MODEL), f32),
        "x_sample": nrm(ks[1], (DEC_BATCH, DEC_SEQ, D_MODEL), f32),
        "cache_k": nrm(ks[2], (DEPTH, DEC_BATCH, win_cache, N_HEADS_A, HEAD_DIM), f32),
        "cache_v": nrm(ks[3], (DEPTH, DEC_BATCH, win_cache, N_HEADS_A, HEAD_DIM), f32),
        "state_pool": nrm(ks[4], (DEPTH, DEC_BATCH, POOL_STATE, POOL_WIDTH), f32),
        "rel_bias": 0.5 * nrm(ks[5], (NUM_BUCKETS, N_HEADS_A), f32),
        "norm_mix": 1.0 + 0.05 * nrm(ks[6], (DEPTH, D_MODEL), f32),
        "w_in": nrm(ks[7], (DEPTH, D_MODEL, IN_COLS), f32) * D_MODEL ** -0.5,
        "w_pool": nrm(ks[8], (DEPTH, N_POOL_GROUPS, POOL_GROUP_DIM, POOL_GROUP_DIM), f32) * POOL_GROUP_DIM ** -0.5,
        "pool_scale": 1.0 + 0.05 * nrm(ks[9], (DEPTH, POOL_WIDTH), f32),
        "w_out": nrm(ks[10], (DEPTH, MIX_WIDTH, D_MODEL), f32) * MIX_WIDTH ** -0.5,
        "norm_ffn": 1.0 + 0.05 * nrm(ks[11], (DEPTH, D_MODEL), f32),
        "w_gate": nrm(ks[12], (DEPTH, D_MODEL, D_FF), f32) * D_MODEL ** -0.5,
        "w_up": nrm(ks[13], (DEPTH, D_MODEL, D_FF), f32) * D_MODEL ** -0.5,
        "w_down": nrm(ks[14], (DEPTH, D_FF, D_MODEL), f32) * D_FF ** -0.5,
        "norm_final": 1.0 + 0.05 * nrm(ks[15], (D_MODEL,), f32),
    }


def reference(x_prompt, x_sample, cache_k, cache_v, state_pool, rel_bias, norm_mix, w_in, w_pool,
              pool_scale, w_out, norm_ffn, w_gate, w_up, w_down, norm_final):
    xp, xs = x_prompt, x_sample
    kp_new, vp_new, pp_new, ks_new, vs_new, ps_new = [], [], [], [], [], []
    for l in range(DEPTH):
        qp, kp, vp, up = mix_input(xp, norm_mix[l], w_in[l])
        ap = dilated_attention_prompt(qp, kp, vp, rel_bias)
        pp = pool_mix(up, 0, w_pool[l], pool_scale[l])
        win = min(MAX_WINDOW, xp.shape[1])
        kp_new.append(kp[:, -win:])
        vp_new.append(vp[:, -win:])
        pp_new.append(up[:, -POOL_STATE:])
        xp = layer_tail(xp, ap, pp, w_out[l], norm_ffn[l], w_gate[l], w_up[l], w_down[l])
        qs, ksm, vsm, us = mix_input(xs, norm_mix[l], w_in[l])
        asm = dilated_attention_sample(qs, ksm, vsm, cache_k[l], cache_v[l], rel_bias)
        ucat = jnp.concatenate([state_pool[l].astype(us.dtype), us], axis=1)
        psm = pool_mix(ucat, POOL_STATE, w_pool[l], pool_scale[l])
        ks_new.append(ksm)
        vs_new.append(vsm)
        ps_new.append(ucat[:, -POOL_STATE:])
        xs = layer_tail(xs, asm, psm, w_out[l], norm_ffn[l], w_gate[l], w_up[l], w_down[l])
    y_prompt = rmsnorm(xp, norm_final)
    y_sample = rmsnorm(xs, norm_final)
    return (y_prompt, y_sample, jnp.stack(kp_new), jnp.stack(vp_new), jnp.stack(pp_new),
            jnp.stack(ks_new), jnp.stack(vs_new), jnp.stack(ps_new))
```

```python
import math
from contextlib import ExitStack

import numpy as np
import concourse.bass as bass
import concourse.mybir as mybir
from concourse.bass_utils import run_bass_kernel_spmd

F32 = mybir.dt.float32
BF16 = mybir.dt.bfloat16
AF = mybir.ActivationFunctionType
ALU = mybir.AluOpType

NCORES = 8
D = 2048
S = 2048
NS = 16
NT = S + NS
DFF = 5632
NJ = DFF // 128
EPS = 1e-6
NEG = -30000.0
SB_BASE = 16512
SB_END = 229376

ENGS = ("pe", "act", "dve", "pool", "sp")
SELF_SYNC = {"pe": False, "act": True, "dve": True, "pool": True, "sp": False}


def _is_psum_key(k):
    name = k[0] if isinstance(k, tuple) else k
    return name in ("PA", "PT", "PTb", "PS", "PO", "PON")


class Prog:
    NDMA = 8

    def __init__(self, nc, stack):
        self.nc = nc
        self.ops = {e: [] for e in ENGS}
        self.sem = {e: stack.enter_context(nc.semaphore("prog_" + e)) for e in ENGS}
        self.cnt = {e: 0 for e in ENGS}
        self.dsem = {e: [stack.enter_context(nc.semaphore("dma_%s_%d" % (e, i))) for i in range(self.NDMA)]
                     for e in ("sp", "pool", "act")}
        self.dcnt = {e: [0] * self.NDMA for e in self.dsem}
        self.dnext = {e: 0 for e in self.dsem}
        self.lastw = {}
        self.readers = {}
        self.waited = {e: {} for e in ENGS}
        self.out_tokens = []
        self.all_dma_tokens = {}
        self.total = 0
        self.maxops = None
        self.log = []

    def _deps(self, eng, reads, writes):
        toks = []
        for r in reads:
            w = self.lastw.get(r)
            if w is not None:
                toks.append(w)
            if _is_psum_key(r):
                toks.extend(self.readers.get(r, ()))
        for w_ in writes:
            w = self.lastw.get(w_)
            if w is not None:
                toks.append(w)
            toks.extend(self.readers.get(w_, ()))
        need = {}
        for (sname, sem, val, teng, is_dma) in toks:
            if (not is_dma) and teng == eng and not SELF_SYNC[eng]:
                continue
            if self.waited[eng].get(sname, 0) >= val:
                continue
            if need.get(sname, (None, 0))[1] < val:
                need[sname] = (sem, val)
        for sname, (sem, val) in need.items():
            self.waited[eng][sname] = val
        return list(need.values())

    def _record(self, tok, reads, writes):
        for r in reads:
            self.readers.setdefault(r, []).append(tok)
        for w in writes:
            self.lastw[w] = tok
            self.readers[w] = []

    def op(self, eng, fn, reads=(), writes=()):
        self.total += 1
        if self.maxops is not None and self.total > self.maxops:
            return None
        self.log.append((self.total, eng, "op", tuple(writes)))
        waits = self._deps(eng, reads, writes)
        self.cnt[eng] += 1
        val = self.cnt[eng]
        sem = self.sem[eng]

        def run(e, fn=fn, waits=waits, sem=sem):
            for (s, v) in waits:
                e.wait_ge(s, v)
            ins = fn(e)
            ins.then_inc(sem, 1)

        self.ops[eng].append(run)
        tok = ("prog_" + eng, sem, val, eng, False)
        self._record(tok, reads, writes)
        return tok

    def dma(self, eng, out, in_, reads=(), writes=(), is_output=False):
        self.total += 1
        if self.maxops is not None and self.total > self.maxops:
            return None
        self.log.append((self.total, eng, "dma", tuple(writes)))
        i = self.dnext[eng]
        self.dnext[eng] = (i + 1) % self.NDMA
        sem = self.dsem[eng][i]
        prev = self.dcnt[eng][i]
        self.dcnt[eng][i] = prev + 16
        val = prev + 16
        sname = "dma_%s_%d" % (eng, i)
        waits = self._deps(eng, reads, writes)
        if prev > 0 and self.waited[eng].get(sname, 0) < prev:
            waits.append((sem, prev))
            self.waited[eng][sname] = prev

        def run(e, waits=waits, sem=sem, out=out, in_=in_):
            for (s, v) in waits:
                e.wait_ge(s, v)
            e.dma_start(out=out, in_=in_).then_inc(sem, 16)

        self.ops[eng].append(run)
        tok = (sname, sem, val, eng, True)
        self._record(tok, reads, writes)
        self.all_dma_tokens[sname] = (sem, val)
        if is_output:
            self.out_tokens.append(tok)
        return tok

    def barrier(self):
        targets = {}
        for en in ENGS:
            if self.cnt[en] > 0:
                targets["prog_" + en] = (self.sem[en], self.cnt[en], en)
        for sname, (sem, val) in self.all_dma_tokens.items():
            targets[sname] = (sem, val, None)
        for en in ENGS:
            waits = []
            for sname, (sem, val, teng) in targets.items():
                if teng == en and not SELF_SYNC[en]:
                    continue
                if self.waited[en].get(sname, 0) >= val:
                    continue
                self.waited[en][sname] = val
                waits.append((sem, val))

            def run(e, waits=waits):
                for (s, v) in waits:
                    e.wait_ge(s, v)

            if waits:
                self.ops[en].append(run)

    def finish(self, block):
        self.maxops = None
        self.barrier()
        fin = {}
        for (sname, sem, val, teng, is_dma) in self.out_tokens:
            if fin.get(sname, (None, 0))[1] < val:
                fin[sname] = (sem, val)

        def run(e, fin=fin):
            for (s, v) in fin.values():
                e.wait_ge(s, v)

        self.ops["sp"].append(run)
        hmap = {"pe": block.tensor, "act": block.scalar, "dve": block.vector, "pool": block.gpsimd, "sp": block.sync}
        for en in ENGS:
            ops = self.ops[en]
            if not ops:
                continue

            def body(e, ops=ops):
                for f in ops:
                    f(e)

            hmap[en](body)


class Arena:
    def __init__(self, nc):
        self.nc = nc
        self.top = SB_BASE
        self.limit = SB_END
        self.n = 0

    def region(self, start, end):
        self.top = start
        self.limit = end

    def alloc(self, name, shape, dt):
        esz = 2 if dt == BF16 else 4
        nbytes = int(np.prod(shape[1:])) * esz
        off = (self.top + 31) // 32 * 32
        assert off + nbytes <= self.limit, ("SBUF overflow", name, off, nbytes, self.limit)
        self.top = off + nbytes
        self.n += 1
        return self.nc.alloc_sbuf_tensor_at("%s_%d" % (name, self.n), list(shape), dt, offset=off)

    def mark(self):
        return self.top

    def release(self, m):
        self.top = m


def ssl(start, count, step):
    return slice(start, start + (count - 1) * step + 1, step)


def _mm(P, calls, reads, writes):
    def fn(e, calls=calls):
        ins = None
        for c in calls:
            (o, l, r, s, t) = c[:5]
            if len(c) > 5 and c[5]:
                ins = e.matmul(o, lhsT=l, rhs=r, start=s, stop=t, skip_group_check=True)
            else:
                ins = e.matmul(o, lhsT=l, rhs=r, start=s, stop=t)
        return ins
    return P.op("pe", fn, reads, writes)


def _tr(P, calls, reads, writes):
    def fn(e, calls=calls):
        ins = None
        for (o, i, idn) in calls:
            ins = e.transpose(o, i, idn)
        return ins
    return P.op("pe", fn, reads, writes)


def _trf(P, calls, reads, writes):
    def fn(e, calls=calls):
        ins = None
        for (o, i, idn) in calls:
            ins = e.matmul(o, lhsT=i, rhs=idn, start=True, stop=True)
        return ins
    return P.op("pe", fn, reads, writes)


def build_program(stop_after=None):
    nc = bass.Bass("TRN2", target_bir_lowering=False)

    def din(name, shape):
        return nc.dram_tensor(name, list(shape), F32, kind="ExternalInput").ap()

    def dout(name, shape):
        return nc.dram_tensor(name, list(shape), F32, kind="ExternalOutput").ap()

    xp = din("xp", [S, D])
    xs = din("xs", [NS, D])
    ck = din("ck", [4, 2048, 1024])
    cv = din("cv", [4, 2048, 1024])
    spst_d = din("spst", [4, 15, 1024])
    w_in = din("w_in", [D, 4096])
    w_pool = din("w_pool", [4, 256, 256])
    w_out = din("w_out", [D, D])
    w_gate = din("w_gate", [D, DFF])
    w_up = din("w_up", [D, DFF])
    w_down = din("w_down", [DFF, D])
    g_mix = din("g_mix", [128, D])
    g_ffn = din("g_ffn", [128, D])
    g_fin = din("g_fin", [128, D])
    pscale = din("pscale", [128, 8])
    btab = din("btab", [8, 128, 2 * 3 * 256])
    sbias_d = din("sbias", [128, 12 * 256])
    ident_d = din("ident", [128, 128])
    invc_d = din("invc", [128, 16])
    selab_d = din("selab", [128, 256])

    yp = dout("yp", [S, D])
    ys = dout("ys", [NS, D])
    pk = dout("pk", [S, 1024])
    pv = dout("pv", [S, 1024])
    ppool = dout("ppool", [15, 1024])
    sk = dout("sk", [NS, 1024])
    sv = dout("sv", [NS, 1024])
    spool = dout("spool", [4, 15, 1024])

    with ExitStack() as st:
        P = Prog(nc, st)
        import os as _os
        if _os.environ.get("PROG_MAXOPS"):
            P.maxops = int(_os.environ["PROG_MAXOPS"])
        A = Arena(nc)
        psum = lambda name, shape, dt: st.enter_context(nc.psum_tensor(name, shape, dt))

        RING = 4
        ring = [A.alloc("ring", [128, 4096], BF16) for _ in range(RING)]
        v16 = lambda t: t[:, :].rearrange("p (k n) -> p k n", k=16)
        v8 = lambda t: t[:, :].rearrange("p (k n) -> p k n", k=8)
        ident_f = A.alloc("identf", [128, 128], F32)
        ident_b = A.alloc("identb", [128, 128], BF16)
        ones_b = A.alloc("onesb", [128, 128], BF16)
        invc = A.alloc("invc", [128, 16], F32)
        psc = A.alloc("psc", [128, 8], F32)
        selab = A.alloc("selab", [128, 256], F32)
        QTs = A.alloc("QTs", [128, 8, 2, NS], BF16)
        KTs = A.alloc("KTs", [128, 8, NS], BF16)
        VTs = A.alloc("VTs", [128, 8, NS], BF16)
        mixA = A.alloc("mixA", [128, 8, NT], BF16)
        m_after_mixA = A.mark()

        PA = [psum("PA%d" % i, [128, 512], F32) for i in range(2)]
        PT = psum("PT", [128, 4, 128], F32)
        PTb = psum("PTb", [128, 8, 128], BF16)
        PS = [psum("PS%d" % i, [128, 512], F32) for i in range(2)]
        PO = [psum("PO%d" % i, [128, 512], F32) for i in range(2)]

        wstate = {"next": 0, "seq": []}
        widx = {}

        def wreg(key, dstf, src):
            widx[key] = len(wstate["seq"])
            wstate["seq"].append((dstf, src))

        def wissue_upto(n):
            while wstate["next"] < min(n, len(wstate["seq"])):
                i = wstate["next"]
                dstf, src = wstate["seq"][i]
                sk_ = wstate.get("scr_keys", {}).get(i)
                P.dma("pool", dstf(ring[i % RING]), src, reads=([sk_] if sk_ is not None else []), writes=[("w", i % RING)])
                wstate["next"] += 1

        def wget(key):
            i = widx[key]
            assert i < wstate["next"], ("weight tile not issued", key)
            return ring[i % RING], ("w", i % RING)

        def wdone(key):
            wissue_upto(widx[key] + RING + 1)

        w_in_v = w_in.rearrange("(kc p) n -> p kc n", p=128)
        w_out_v = w_out.rearrange("(kc p) n -> p kc n", p=128)
        w_gate_v = w_gate.rearrange("(kc p) n -> p kc n", p=128)
        w_up_v = w_up.rearrange("(kc p) n -> p kc n", p=128)
        w_down_v = w_down.rearrange("(j p) n -> p j n", p=128)
        for hp in range(8):
            for g in range(3):
                wreg(("in", hp, g), (lambda t: v16(t)[:, :, 0:128]),
                     w_in_v[:, :, g * 1024 + hp * 128: g * 1024 + (hp + 1) * 128])
        for c in range(8):
            wreg(("u", c), (lambda t: v16(t)[:, :, 0:128]), w_in_v[:, :, 3072 + c * 128: 3072 + (c + 1) * 128])
        PARTS = [(0, 8), (8, 8), (16, 8), (24, 8), (32, 8), (40, 4)]
        NG = 4
        p2tiles = []
        for cb in range(4):
            for half in range(2):
                p2tiles.append((("out", cb, half), (lambda t: t.rearrange("p (k n) -> p k n", k=8)),
                                w_out_v[:, half * 8:(half + 1) * 8, cb * 512:(cb + 1) * 512], 4096))
        for pi, (j0, nj) in enumerate(PARTS):
            for q in range(nj // 2):
                c0 = (j0 + q * 2) * 128
                p2tiles.append((("gate", pi, q), (lambda t: t.rearrange("p (k n) -> p k n", k=16)), w_gate_v[:, :, c0:c0 + 256], 4096))
                p2tiles.append((("up", pi, q), (lambda t: t.rearrange("p (k n) -> p k n", k=16)), w_up_v[:, :, c0:c0 + 256], 4096))
            for cb in range(4):
                p2tiles.append((("down", pi, cb), (lambda t, nj=nj: t[:, 0:nj * 512].rearrange("p (k n) -> p k n", k=nj)),
                                w_down_v[:, j0:j0 + nj, cb * 512:(cb + 1) * 512], nj * 512))
        NP2 = len(p2tiles)
        wscr = nc.dram_tensor("wscr", [NP2, 128, 4096], BF16, kind="Internal").ap()
        for g in range(NG):
            for ti, (ks, vf, src, nval) in enumerate(p2tiles):
                wreg((ks[0], g) + ks[1:], (lambda t, nval=nval: t[:, 0:nval]), wscr[ti][:, 0:nval])
        wstate["scr_keys"] = {}
        for g in range(NG):
            for ti, (ks, vf, src, nval) in enumerate(p2tiles):
                wstate["scr_keys"][widx[(ks[0], g) + ks[1:]]] = ("scr", ti)
        pc_state = {"next": 0}

        def precast(n):
            while n > 0 and pc_state["next"] < NP2:
                ti = pc_state["next"]
                ks, vf, src, nval = p2tiles[ti]
                P.dma("pool", vf(wscr[ti]), src, writes=[("scr", ti)])
                pc_state["next"] += 1
                n -= 1

        with nc.Block() as block:
          for _once in (0,):
              P.dma("sp", ident_f[:], ident_d, writes=["identf"])
              P.dma("sp", invc[:], invc_d, writes=["invc"])
              P.dma("sp", psc[:], pscale, writes=["psc"])
              P.dma("sp", selab[:], selab_d, writes=["selab"])
              P.op("dve", lambda e: e.tensor_copy(out=ident_b[:], in_=ident_f[:]), reads=["identf"], writes=["identb"])
              P.op("pool", lambda e: e.memset(ones_b[:], 1.0), writes=["onesb"])
              wissue_upto(RING)
              P.op("pool", lambda e: e.memset(QTs[64:128, :, 0, :], 0.0), writes=["QTs0a"])
              P.op("pool", lambda e: e.memset(QTs[0:64, :, 1, :], 0.0), writes=["QTs0b"])

              HT_off = (A.mark() + 31) // 32 * 32
              HT = A.alloc("HT", [128, 16, NT], BF16)
              m_after_HT = A.mark()
              xst = [A.alloc("xst", [128, D], F32) for _ in range(2)]
              hbt = [A.alloc("hbt", [128, D], BF16) for _ in range(2)]
              junk = A.alloc("junk", [128, D], BF16)
              gmix_b = A.alloc("gmixb", [128, D], F32)
              ssq = A.alloc("ssq", [128, 20], F32)
              rstd = A.alloc("rstd", [128, 20], F32)
              P.dma("sp", gmix_b[:], g_mix, writes=["gmixb"])
              P.op("pool", lambda e: e.memset(ssq[:], 0.0), writes=[("ssq", t) for t in range(17)])
              xst.append(A.alloc("xst", [128, D], F32))

              def p1a_stage_a(t):
                  n = 128 if t < 16 else NS
                  b = t % 3
                  src = xp[t * 128:(t + 1) * 128, :] if t < 16 else xs[:, :]
                  P.dma("sp", xst[b][0:n, :], src, writes=[("xst", b)])
                  P.op("act", lambda e, b=b, n=n, t=t: e.activation(out=junk[0:n, :], in_=xst[b][0:n, :], func=AF.Square,
                                                                    accum_out=ssq[0:n, t:t + 1]),
                       reads=[("xst", b)], writes=["junk", ("ssq", t)])
                  P.op("act", lambda e, n=n, t=t: e.activation(out=rstd[0:n, t:t + 1], in_=ssq[0:n, t:t + 1], func=AF.Sqrt,
                                                               scale=1.0 / D, bias=EPS),
                       reads=[("ssq", t)], writes=[("rstd", t)])

              def p1a_stage_b(t):
                  n = 128 if t < 16 else NS
                  b = t % 3
                  hb_ = t % 2
                  P.op("dve", lambda e, n=n, t=t: e.reciprocal(out=rstd[0:n, t:t + 1], in_=rstd[0:n, t:t + 1]),
                       reads=[("rstd", t)], writes=[("rstd", t)])
                  P.op("dve", lambda e, b=b, hb_=hb_, n=n, t=t: e.scalar_tensor_tensor(
                      out=hbt[hb_][0:n, :], in0=xst[b][0:n, :], scalar=rstd[0:n, t:t + 1], in1=gmix_b[0:n, :],
                      op0=ALU.mult, op1=ALU.mult), reads=[("xst", b), ("rstd", t), "gmixb"], writes=[("hbt", hb_)])
                  for half in range(2):
                      calls = [(PTb[:, k, 0:n], hbt[hb_][0:n, (half * 8 + k) * 128:(half * 8 + k + 1) * 128], ident_b[0:n, 0:n])
                               for k in range(8)]
                      _tr(P, calls, reads=[("hbt", hb_), "identb"], writes=["PTb"])
                      c0 = t * 128 if t < 16 else S
                      if half == 0:
                          P.op("act", lambda e, half=half, c0=c0, n=n: e.copy(out=HT[:, half * 8:(half + 1) * 8, c0:c0 + n],
                                                                              in_=PTb[:, :, 0:n]),
                               reads=["PTb"], writes=[("HT", t)])
                      else:
                          P.op("dve", lambda e, half=half, c0=c0, n=n: e.tensor_copy(out=HT[:, half * 8:(half + 1) * 8, c0:c0 + n],
                                                                                     in_=PTb[:, :, 0:n]),
                               reads=["PTb"], writes=[("HT", t)])

              p1a_stage_a(0)
              for t in range(17):
                  if t + 1 < 17:
                      p1a_stage_a(t + 1)
                  p1a_stage_b(t)
              P.barrier()
              precast(6)
              if stop_after == "p1a":
                  break
              A.release(m_after_HT)
              HT_ALL = [("HT", t) for t in range(17)]

              TGS = [(i * 512, 512) for i in range(4)] + [(S, NS)]

              QTz = A.alloc("QTz", [128, 2, NT], BF16)
              P.op("pool", lambda e: e.memset(QTz[64:128, 0, :], 0.0), writes=["QTz0a"])
              P.op("pool", lambda e: e.memset(QTz[0:64, 1, :], 0.0), writes=["QTz0b"])
              KT = A.alloc("KT", [128, NT], BF16)
              VT = A.alloc("VT", [128, NT], BF16)
              V1 = A.alloc("V1", [128, 16, 192], BF16)
              VX = [A.alloc("VX", [128, 16, 192], BF16) for _ in range(2)]
              for vi, vb in enumerate((V1, VX[0], VX[1])):
                  P.op("pool", lambda e, vb=vb: e.memset(vb[:, :, 64:128], 1.0), writes=[("Vones", vi)])
              rdt = [A.alloc("rdt", [128, 512], F32) for _ in range(2)]
              ftmp = [A.alloc("ftmp", [128, 512], F32) for _ in range(2)]
              kvst = [A.alloc("kvst", [128, 4, 128], F32) for _ in range(2)]
              ACC = A.alloc("ACC", [128, 2, S], F32)
              BT = [A.alloc("BT", [128, 2, 3, 256], BF16) for _ in range(2)]
              PE_ = [A.alloc("Pexp", [128, 512], BF16) for _ in range(4)]
              PSB = [PS[0], PS[1], PT[:, :, :].rearrange("p a b -> p (a b)")]
              PSK = [("PS", 0), ("PS", 1), "PT"]
              kvs_tmp = A.alloc("kvstmp", [128, 2, NS], F32)
              kvss = [A.alloc("kvss", [NS, 128], F32) for _ in range(4)]
              kvss_i = [0]

              ftmp_i = [0]
              kvst_i = [0]

              for hp in range(8):
                  bt = BT[hp % 2]
                  P.dma("pool", bt[:], btab[hp].rearrange("p (h b q) -> p h b q", h=2, b=3), writes=[("BT", hp % 2)])
                  pa_i = 0
                  for gi, gname in enumerate(("Q", "K", "V")):
                      slot_t, wkey = wget(("in", hp, gi))
                      slot = v16(slot_t)
                      for tgi, (c0, n) in enumerate(TGS):
                          pa = PA[pa_i % 2]
                          pkey = ("PA", pa_i % 2)
                          pa_i += 1
                          calls = [(pa[:, 0:n], slot[:, kc, 0:128], HT[:, kc, c0:c0 + n], kc == 0, kc == 15)
                                   for kc in range(16)]
                          _mm(P, calls, reads=[wkey] + HT_ALL, writes=[pkey])
                          if tgi == len(TGS) - 1:
                              wdone(("in", hp, gi))
                          if gname == "Q":
                              if tgi < 4:
                                  P.op("act", lambda e, pa=pa, c0=c0, n=n: e.mul(out=QTz[0:64, 0, c0:c0 + n], in_=pa[0:64, 0:n], mul=0.125),
                                       reads=[pkey, "QTz0a", "QTz0b"], writes=[("QT", tgi)])
                                  P.op("act", lambda e, pa=pa, c0=c0, n=n: e.mul(out=QTz[64:128, 1, c0:c0 + n], in_=pa[64:128, 0:n],
                                                                                 mul=0.125),
                                       reads=[pkey, "QTz0a", "QTz0b"], writes=[("QT", tgi)])
                              else:
                                  P.op("act", lambda e, pa=pa, hp=hp: e.mul(out=QTs[0:64, hp, 0, :], in_=pa[0:64, 0:NS], mul=0.125),
                                       reads=[pkey, "QTs0a", "QTs0b"], writes=[("QTs", hp)])
                                  P.op("act", lambda e, pa=pa, hp=hp: e.mul(out=QTs[64:128, hp, 1, :], in_=pa[64:128, 0:NS], mul=0.125),
                                       reads=[pkey, "QTs0a", "QTs0b"], writes=[("QTs", hp)])
                              continue
                          Tb = KT if gname == "K" else VT
                          Ts = KTs if gname == "K" else VTs
                          outd = pk if gname == "K" else pv
                          if tgi < 4:
                              f = ftmp[ftmp_i[0] % 2]
                              fkey = ("ftmp", ftmp_i[0] % 2)
                              ftmp_i[0] += 1
                              P.op("dve", lambda e, Tb=Tb, pa=pa, c0=c0, n=n: e.tensor_copy(out=Tb[:, c0:c0 + n], in_=pa[:, 0:n]),
                                   reads=[pkey], writes=[(gname + "T", tgi)])
                              P.op("act", lambda e, f=f, pa=pa, n=n: e.mul(out=f[:, 0:n], in_=pa[:, 0:n], mul=1.0),
                                   reads=[pkey, (gname + "T", tgi)], writes=[fkey])
                              calls = [(PT[:, i, :], f[:, i * 128:(i + 1) * 128], ident_f[:, :]) for i in range(4)]
                              _trf(P, calls, reads=[fkey, "identf"], writes=["PT"])
                              stg = kvst[kvst_i[0] % 2]
                              skey = ("kvst", kvst_i[0] % 2)
                              kvst_i[0] += 1
                              P.op("dve", lambda e, stg=stg: e.tensor_copy(out=stg[:], in_=PT[:]), reads=["PT"], writes=[skey])
                              if gname == "V":
                                  P.op("act", lambda e, tgi=tgi: e.mul(
                                      out=V1[:, tgi * 4:(tgi + 1) * 4, :].rearrange("p t (a c) -> p t a c", a=3)[:, :, 0:3:2, :],
                                      in_=PT[:].rearrange("p t (a c) -> p t a c", a=2), mul=1.0),
                                       reads=["PT", ("Vones", 0)], writes=[("V1", tgi)])
                              dview = outd[tgi * 512:(tgi + 1) * 512, hp * 128:(hp + 1) * 128].rearrange("(t p) c -> p t c", p=128)
                              P.dma("sp", dview, stg[:], reads=[skey], is_output=True)
                          else:
                              stg_s = kvss[kvss_i[0] % 4]
                              sskey = ("kvss", kvss_i[0] % 4)
                              kvss_i[0] += 1
                              outs = sk if gname == "K" else sv
                              gsel = 0 if gname == "K" else 1
                              P.op("dve", lambda e, Ts=Ts, pa=pa, hp=hp: e.tensor_copy(out=Ts[:, hp, :], in_=pa[:, 0:NS]),
                                   reads=[pkey], writes=[(gname + "Ts", hp)])
                              P.op("act", lambda e, pa=pa, gsel=gsel: e.mul(out=kvs_tmp[:, gsel, :], in_=pa[:, 0:NS], mul=1.0),
                                   reads=[pkey], writes=[("kvstmp", gsel)])
                              _trf(P, [(PT[0:NS, 0, :], kvs_tmp[:, gsel, :], ident_f[:, :])], reads=[("kvstmp", gsel), "identf"],
                                  writes=["PT"])
                              P.op("dve", lambda e, stg_s=stg_s: e.tensor_copy(out=stg_s[:, :], in_=PT[0:NS, 0, :]),
                                   reads=["PT"], writes=[sskey])
                              P.dma("sp", outs[:, hp * 128:(hp + 1) * 128], stg_s[:, :], reads=[sskey], is_output=True)

                  if stop_after == "p1b:proj":
                      break
                  QK_R = [("QT", i) for i in range(4)] + [("KT", i) for i in range(4)]
                  tiles = []
                  POB = [PO[0], PO[1], PA[0], PA[1]]
                  POK = [("PO", 0), ("PO", 1), ("PA", 0), ("PA", 1)]
                  for bi, d in enumerate((1, 4, 16)):
                      nb = 16 // d
                      for r in range(d):
                          for j in range(nb):
                              tiles.append((bi, d, nb, r, j))

                  def build_vx(bi, d, buf):
                      nb = 16 // d
                      for q4 in range(2):
                          calls = []
                          for k in range(8):
                              ti = q4 * 8 + k
                              r, j = ti // nb, ti % nb
                              st0 = r + d * 128 * j
                              calls.append((PTb[:, k, :], VT[:, ssl(st0, 128, d)], ident_b[:, :]))
                          _tr(P, calls, reads=[("VT", i) for i in range(4)] + ["identb"], writes=["PTb"])
                          P.op("dve", lambda e, buf=buf, q4=q4: e.tensor_copy(
                              out=VX[buf][:, q4 * 8:(q4 + 1) * 8, :].rearrange("p t (a c) -> p t a c", a=3)[:, :, 0:3:2, :],
                              in_=PTb[:].rearrange("p t (a c) -> p t a c", a=2)),
                               reads=["PTb", ("Vones", buf + 1)], writes=[("VX", buf, q4)])

                  build_vx(1, 4, 0)
                  build_vx(2, 16, 1)
                  if stop_after == "p1b:vx":
                      break

                  def emit_qk(k):
                      bi, d, nb, r, j = tiles[k]
                      nq = 2 if j + 1 < nb else 1
                      W = 128 * nq
                      st0 = r + d * 128 * j
                      psv = PSB[k % 3][:, 0:2 * W].rearrange("p (h w) -> p h w", h=2)
                      calls = [(psv, ident_b[:, :], bt[:, :, bi, 0:W], True, False),
                               (psv, KT[:, ssl(st0, 128, d)], QTz[:, :, ssl(st0, W, d)], False, True)]
                      _mm(P, calls, reads=QK_R + [("BT", hp % 2), "identb"], writes=[PSK[k % 3]])
                      P.op("act", lambda e, psv=psv, W=W, k=k: e.activation(
                          out=PE_[k % 4][:, 0:2 * W].rearrange("p (h w) -> p h w", h=2), in_=psv, func=AF.Exp),
                           reads=[PSK[k % 3]], writes=[("Pexp", k % 4)])

                  def emit_pv(k):
                      bi, d, nb, r, j = tiles[k]
                      nq = 2 if j + 1 < nb else 1
                      ti = r * nb + j
                      st0 = r + d * 128 * j
                      for hh in range(2):
                          if bi == 0:
                              vt = V1[:, ti, hh * 64:hh * 64 + 128]
                              vkeys = [("V1", i) for i in range(4)] + [("Vones", 0)]
                          else:
                              vt = VX[bi - 1][:, ti, hh * 64:hh * 64 + 128]
                              vkeys = [("VX", bi - 1, 0), ("VX", bi - 1, 1), ("Vones", bi)]
                          pe_ = PE_[k % 4][:, 0:256 * nq].rearrange("p (h w) -> p h w", h=2)
                          cb_ = 2 * hh + (k % 2)
                          nb_ = 2 * hh + ((k + 1) % 2)
                          cur, nxt = POB[cb_], POB[nb_]
                          curk, nxtk = POK[cb_], POK[nb_]
                          calls = [(cur[:, 0:128], vt, pe_[:, hh, 0:128], j == 0, True, True)]
                          wr = [curk]
                          if nq == 2:
                              calls += [(nxt[:, 0:128], vt, pe_[:, hh, 128:256], True, False, True)]
                              wr.append(nxtk)
                          _mm(P, calls, reads=[("Pexp", k % 4)] + vkeys, writes=wr)
                          dst = ACC[:, hh, ssl(st0, 128, d)]
                          srcv = cur[:, 0:128]
                          if d == 1:
                              akeys = [("ACC", hh, j // 4)]
                          elif d == 4:
                              akeys = [("ACC", hh, j)]
                          else:
                              akeys = [("ACC", hh, i) for i in range(4)]
                          if bi == 0 and hh == 1:
                              P.op("act", lambda e, dst=dst, srcv=srcv: e.mul(out=dst, in_=srcv, mul=1.0),
                                   reads=[curk], writes=akeys)
                          elif bi == 0:
                              P.op("dve", lambda e, dst=dst, srcv=srcv: e.tensor_copy(out=dst, in_=srcv),
                                   reads=[curk], writes=akeys)
                          else:
                              P.op("dve", lambda e, dst=dst, srcv=srcv: e.tensor_tensor(out=dst, in0=dst, in1=srcv, op=ALU.add),
                                   reads=[curk] + akeys, writes=akeys)

                  LOOK = 3
                  for k in range(min(LOOK, len(tiles))):
                      emit_qk(k)
                  for k in range(len(tiles)):
                      emit_pv(k)
                      if k + LOOK < len(tiles):
                          emit_qk(k + LOOK)
                  if stop_after == "p1b:att":
                      break
                  precast(10)
                  for tg4 in range(4):
                      cs = slice(tg4 * 512, (tg4 + 1) * 512)
                      AKg = [("ACC", 0, tg4), ("ACC", 1, tg4)]
                      pf = POB[2 + tg4 % 2]
                      pfk = POK[2 + tg4 % 2]
                      _mm(P, [(pf[:, 0:512], selab[:, 0:128], ACC[:, 0, cs], True, False),
                              (pf[:, 0:512], selab[:, 128:256], ACC[:, 1, cs], False, True)], reads=AKg + ["selab"], writes=[pfk])
                      rd = rdt[tg4 % 2]
                      P.op("dve", lambda e, rd=rd, pf=pf: e.reciprocal(out=rd[:, :], in_=pf[:, 0:512]), reads=[pfk],
                           writes=[("rdt", tg4 % 2)])
                      P.op("pool", lambda e, rd=rd, cs=cs, hp=hp: e.tensor_tensor(out=mixA[0:64, hp, cs], in0=ACC[0:64, 0, cs],
                                                                                 in1=rd[0:64, :], op=ALU.mult),
                           reads=AKg + [("rdt", tg4 % 2)], writes=[("mixA", hp, tg4, 0)])
                      P.op("pool", lambda e, rd=rd, cs=cs, hp=hp: e.tensor_tensor(out=mixA[64:128, hp, cs], in0=ACC[64:128, 1, cs],
                                                                                 in1=rd[64:128, :], op=ALU.mult),
                           reads=AKg + [("rdt", tg4 % 2)], writes=[("mixA", hp, tg4, 1)])

              precast(NP2)
              if stop_after is not None and stop_after.startswith("p1b:"):
                  break
              P.barrier()
              if stop_after == "p1b":
                  break
              A.release(m_after_HT)

              mixP = A.alloc("mixP", [128, 8, NT], BF16)
              m_after_mixP = A.mark()
              UT = A.alloc("UT", [128, 16 + S], F32)
              S1 = A.alloc("S1", [128, 16 + S], F32)
              S2 = A.alloc("S2", [128, 16 + S], F32)
              pooled = A.alloc("pooled", [128, 2, NT], BF16)
              wp = A.alloc("wp", [128, 4, 2, 256], BF16)
              spst = [A.alloc("spst", [64, 128], F32) for _ in range(2)]
              ucs = [A.alloc("ucs", [128, 4, 19], F32) for _ in range(3)]
              ptmp = A.alloc("ptmp", [128, 16], F32)
              ppst = [A.alloc("ppst", [16, 128], F32) for _ in range(2)]
              usst = [A.alloc("usst", [NS, 128], F32) for _ in range(2)]
              utail = A.alloc("utail", [128, 32], F32)
              P.dma("pool", wp[:], w_pool.rearrange("g (cc p) o -> p g cc o", p=128), writes=["wp"])
              P.op("pool", lambda e: e.memset(UT[:, 0:16], 0.0), writes=["UTpad"])
              P.op("pool", lambda e: e.memset(S1[:, 0:16], 0.0), writes=["S1pad"])
              P.op("pool", lambda e: e.memset(S2[:, 0:16], 0.0), writes=["S2pad"])
              for b in range(4):
                  P.dma("sp", spool[b, 0:11, :], spst_d[b, 4:15, :], is_output=True)
              pa_i = 0
              PCB = [PA[0], PA[1], PO[0], PO[1]]
              PCK = [("PA", 0), ("PA", 1), ("PO", 0), ("PO", 1)]
              for c in range(8):
                  slot_t, wkey = wget(("u", c))
                  slot = v16(slot_t)
                  g = c // 2
                  w = (2, 4, 8, 16)[g]
                  for tgi, (c0, n) in enumerate(TGS):
                      pa = PCB[pa_i % 4]
                      pkey = PCK[pa_i % 4]
                      pa_i += 1
                      calls = [(pa[:, 0:n], slot[:, kc, 0:128], HT[:, kc, c0:c0 + n], kc == 0, kc == 15)
                               for kc in range(16)]
                      _mm(P, calls, reads=[wkey] + HT_ALL, writes=[pkey])
                      if tgi == len(TGS) - 1:
                          wdone(("u", c))
                      if tgi < 4:
                          P.op("act", lambda e, pa=pa, c0=c0: e.mul(out=UT[:, 16 + c0:16 + c0 + 512], in_=pa[:, :], mul=1.0),
                               reads=[pkey, "UTpad"], writes=[("UT", tgi)])
                      else:
                          P.op("act", lambda e, pa=pa: e.mul(out=ucs[0][:, :, 15:19],
                                                             in_=pa[:, 0:NS].rearrange("p (b t) -> p b t", b=4), mul=1.0),
                               reads=[pkey], writes=["ucs0n"])
                          P.op("dve", lambda e, pa=pa: e.tensor_copy(out=utail[:, 0:NS], in_=pa[:, 0:NS]),
                               reads=[pkey], writes=["utailS"])
                  UTK = [("UT", i) for i in range(4)]
                  P.dma("sp", spst[c % 2][0:60, :], spst_d.rearrange("b r c -> (b r) c")[:, c * 128:(c + 1) * 128],
                        writes=[("spst", c % 2)])
                  _trf(P, [(PT[:, 0, 0:60], spst[c % 2][0:60, :], ident_f[0:60, 0:60])], reads=[("spst", c % 2), "identf"],
                      writes=["PT"])
                  P.op("dve", lambda e: e.tensor_copy(out=ucs[0][:, :, 0:15], in_=PT[:, 0, 0:60].rearrange("p (b r) -> p b r", b=4)),
                       reads=["PT"], writes=["ucs0s"])
                  P.op("dve", lambda e: e.tensor_copy(out=utail[:, 16:31], in_=UT[:, 16 + S - 15:16 + S]),
                       reads=UTK, writes=["utailP"])
                  _trf(P, [(PT[0:15, 1, :], utail[:, 16:31], ident_f[:, :]), (PT[0:NS, 2, :], utail[:, 0:NS], ident_f[:, :])],
                      reads=["utailP", "utailS", "identf"], writes=["PT"])
                  P.op("dve", lambda e, c=c: e.tensor_copy(out=ppst[c % 2][0:15, :], in_=PT[0:15, 1, :]),
                       reads=["PT"], writes=[("ppst", c % 2)])
                  P.op("dve", lambda e, c=c: e.tensor_copy(out=usst[c % 2][0:NS, :], in_=PT[0:NS, 2, :]),
                       reads=["PT"], writes=[("usst", c % 2)])
                  P.dma("sp", ppool[:, c * 128:(c + 1) * 128], ppst[c % 2][0:15, :], reads=[("ppst", c % 2)], is_output=True)
                  for b in range(4):
                      P.dma("sp", spool[b, 11:15, c * 128:(c + 1) * 128], usst[c % 2][4 * b:4 * b + 4, :],
                            reads=[("usst", c % 2)], is_output=True)
                  bufs = [UT, S1, S2]
                  cur_i = 0
                  steps = int(math.log2(w))
                  src_keys = UTK + ["UTpad"]
                  for si in range(steps):
                      sh = 1 << si
                      dst_i = 1 if cur_i != 1 else 2
                      srcb, dstb = bufs[cur_i], bufs[dst_i]
                      dkey = "S%d" % dst_i
                      P.op("pool", lambda e, srcb=srcb, dstb=dstb, sh=sh: e.tensor_tensor(
                          out=dstb[:, 16:16 + S], in0=srcb[:, 16:16 + S], in1=srcb[:, 16 - sh:16 + S - sh], op=ALU.add),
                          reads=src_keys + [dkey + "pad"], writes=[dkey])
                      cur_i = dst_i
                      src_keys = [dkey, dkey + "pad"]
                  sw = bufs[cur_i]
                  P.op("dve", lambda e, sw=sw, w=w, c=c: e.scalar_tensor_tensor(
                      out=pooled[:, c % 2, 0:S], in0=sw[:, 16:16 + S], scalar=1.0 / w, in1=UT[:, 16:16 + S],
                      op0=ALU.mult, op1=ALU.subtract), reads=src_keys + UTK, writes=[("pooled", c % 2)])
                  P.op("dve", lambda e, sw=sw, w=w: e.tensor_tensor(out=ptmp[:, 0:w - 1], in0=sw[:, 16:16 + w - 1],
                                                                    in1=invc[:, 0:w - 1], op=ALU.mult),
                       reads=src_keys + ["invc"], writes=["ptmp"])
                  P.op("dve", lambda e, w=w, c=c: e.tensor_tensor(out=pooled[:, c % 2, 0:w - 1], in0=ptmp[:, 0:w - 1],
                                                                  in1=UT[:, 16:16 + w - 1], op=ALU.subtract),
                       reads=["ptmp"] + UTK + [("pooled", c % 2)], writes=[("pooled", c % 2)])
                  cur = 0
                  lo = 0
                  rk = ["ucs0n", "ucs0s"]
                  for si in range(steps):
                      sh = 1 << si
                      dst = 1 if cur != 1 else 2
                      nlo = lo + sh
                      wk = "ucs%d" % dst
                      P.op("dve", lambda e, cur=cur, dst=dst, nlo=nlo, sh=sh: e.tensor_tensor(
                          out=ucs[dst][:, :, nlo:19], in0=ucs[cur][:, :, nlo:19], in1=ucs[cur][:, :, nlo - sh:19 - sh], op=ALU.add),
                          reads=rk, writes=[wk])
                      cur, lo, rk = dst, nlo, [wk]
                  P.op("dve", lambda e, cur=cur, w=w, c=c: e.scalar_tensor_tensor(
                      out=pooled[:, c % 2, S:NT].rearrange("p (b t) -> p b t", b=4), in0=ucs[cur][:, :, 15:19], scalar=1.0 / w,
                      in1=ucs[0][:, :, 15:19], op0=ALU.mult, op1=ALU.subtract),
                      reads=rk + ["ucs0n", "ucs0s", ("pooled", c % 2)], writes=[("pooled", c % 2)])
                  if c % 2 == 1:
                      for co in range(2):
                          for tgi, (c0, n) in enumerate(TGS):
                              pa = PCB[pa_i % 4]
                              pkey = PCK[pa_i % 4]
                              pa_i += 1
                              calls = [(pa[:, 0:n], wp[:, g, ci, co * 128:(co + 1) * 128], pooled[:, ci, c0:c0 + n], ci == 0, ci == 1)
                                       for ci in range(2)]
                              _mm(P, calls, reads=["wp", ("pooled", 0), ("pooled", 1)], writes=[pkey])
                              ch = 2 * g + co
                              P.op("dve", lambda e, pa=pa, ch=ch, c0=c0, n=n: e.tensor_scalar_mul(
                                  out=mixP[:, ch, c0:c0 + n], in0=pa[:, 0:n], scalar1=psc[:, ch:ch + 1]),
                                  reads=[pkey, "psc"], writes=[("mixP", ch)])
              P.barrier()
              if stop_after == "p1c":
                  break
              R2_start = m_after_mixP
              A.region(HT_off, m_after_HT)

              sbias = A.alloc("sbias", [128, 12, 256], BF16)
              P.dma("pool", sbias[:], sbias_d.rearrange("p (t c) -> p t c", t=12), writes=["sbias"])
              kc_t = [A.alloc("kct", [128, 1024], BF16) for _ in range(4)]
              vc_t = [A.alloc("vct", [128, 1024], BF16) for _ in range(4)]
              kT = [A.alloc("kT", [128, 8, 128], BF16) for _ in range(2)]
              Psm = [A.alloc("Psm", [128, 256], BF16) for _ in range(2)]
              vnew = A.alloc("vnew", [NS, 8, 128], BF16)
              srd = A.alloc("srd", [128, 256], F32)
              PON, POD = PO[0], PO[1]
              _tr(P, [(PTb[0:NS, hp, :], VTs[:, hp, :], ident_b[:, :]) for hp in range(8)],
                  reads=[("VTs", hp) for hp in range(8)] + ["identb"], writes=["PTb"])
              P.op("dve", lambda e: e.tensor_copy(out=vnew[:], in_=PTb[0:NS, :, :]), reads=["PTb"], writes=["vnew"])
              QTS_K = [("QTs", hp) for hp in range(8)]
              kT.append(A.alloc("kT", [128, 8, 128], BF16))
              ttypes = [("A", 0)] + [("B", t) for t in range(4)] + [("C", t) for t in range(4)] + [("N", i) for i in range(3)]
              tl = []
              ld = 0
              for b in range(4):
                  for ti, (kind, t) in enumerate(ttypes):
                      d_ = {"b": b, "ti": ti, "kind": kind, "t": t, "first": ti == 0, "last": ti == len(ttypes) - 1, "k": len(tl)}
                      if kind != "N":
                          d_["sl"] = ld % 4
                          ld += 1
                      tl.append(d_)

              def s1(k):
                  d_ = tl[k]
                  if d_["kind"] == "N":
                      return
                  b, t, sl = d_["b"], d_["t"], d_["sl"]
                  if d_["kind"] == "A":
                      rows = slice(1920, 2048)
                  elif d_["kind"] == "B":
                      rows = slice(1536 + t, 2048, 4)
                  else:
                      rows = slice(t, 2048, 16)
                  P.dma("pool", kc_t[sl][:], ck[b, rows, :], writes=[("kct", sl)])
                  P.dma("pool", vc_t[sl][:], cv[b, rows, :], writes=[("vct", sl)])
                  kt = kT[k % 3]
                  _tr(P, [(PTb[:, hp, :], kc_t[sl][:, hp * 128:(hp + 1) * 128], ident_b[:, :]) for hp in range(8)],
                      reads=[("kct", sl), "identb"], writes=["PTb"])
                  P.op("dve", lambda e, kt=kt: e.tensor_copy(out=kt[:], in_=PTb[:]), reads=["PTb"], writes=[("kT", k % 3)])

              def s2(k):
                  d_ = tl[k]
                  ti = d_["ti"]
                  psb = PS[k % 2]
                  pskey = ("PS", k % 2)
                  psm = Psm[k % 2]
                  pmkey = ("Psm", k % 2)
                  if d_["kind"] != "N":
                      nk = 128
                      kt = kT[k % 3]
                      calls = [(psb[0:nk, 0:256], ident_b[:, :], sbias[:, ti, :], True, False, True)]
                      for hp in range(8):
                          calls.append((psb[0:nk, hp * 32:(hp + 1) * 32], kt[:, hp, :],
                                        QTs[:, hp, :, :].rearrange("p h q -> p (h q)"), False, hp == 7, True))
                      _mm(P, calls, reads=[("kT", k % 3), "sbias", "identb"] + QTS_K, writes=[pskey])
                  else:
                      nk = NS
                      calls = [(psb[0:nk, 0:256], ident_b[0:NS, 0:NS], sbias[0:NS, ti, :], True, False, True)]
                      for hp in range(8):
                          calls.append((psb[0:nk, hp * 32:(hp + 1) * 32], KTs[:, hp, :],
                                        QTs[:, hp, :, :].rearrange("p h q -> p (h q)"), False, hp == 7, True))
                      _mm(P, calls, reads=["sbias", "identb"] + QTS_K + [("KTs", hp) for hp in range(8)], writes=[pskey])
                  P.op("act", lambda e, psm=psm, psb=psb, nk=nk: e.activation(out=psm[0:nk, :], in_=psb[0:nk, 0:256], func=AF.Exp),
                       reads=[pskey], writes=[pmkey])

              def s3(k):
                  d_ = tl[k]
                  b, first, last = d_["b"], d_["first"], d_["last"]
                  psm = Psm[k % 2]
                  pmkey = ("Psm", k % 2)
                  if d_["kind"] != "N":
                      nk = 128
                      sl = d_["sl"]
                      vsrc = lambda hp, sl=sl: vc_t[sl][:, hp * 128:(hp + 1) * 128]
                      vkeys = [("vct", sl)]
                  else:
                      nk = NS
                      vsrc = lambda hp: vnew[0:NS, hp, :]
                      vkeys = ["vnew"]
                  calls = [(PON[:, hp * 32:(hp + 1) * 32], vsrc(hp), psm[0:nk, hp * 32:(hp + 1) * 32], first and hp == 0, last, True)
                           for hp in range(8)]
                  calls.append((POD[:, 0:256], ones_b[0:nk, :], psm[0:nk, :], first, last, True))
                  _mm(P, calls, reads=[pmkey, "onesb"] + vkeys, writes=["PON"])
                  if last:
                      P.op("dve", lambda e: e.reciprocal(out=srd[:], in_=POD[:, 0:256]), reads=["PON"], writes=["srd"])
                      for hh in range(2):
                          hs = slice(hh * 64, (hh + 1) * 64)
                          cs = slice(hh * NS + 4 * b, hh * NS + 4 * b + 4)
                          P.op("dve", lambda e, hs=hs, hh=hh, cs=cs, b=b: e.tensor_tensor(
                              out=mixA[hs, :, S + 4 * b:S + 4 * b + 4],
                              in0=PON[hs, 0:256].rearrange("p (h c) -> p h c", h=8)[:, :, cs],
                              in1=srd[hs, 0:256].rearrange("p (h c) -> p h c", h=8)[:, :, cs], op=ALU.mult),
                              reads=["PON", "srd"], writes=[("mixAs", b, hh)])

              NTL = len(tl)
              s1(0)
              s1(1)
              s2(0)
              for k in range(NTL):
                  if k + 2 < NTL:
                      s1(k + 2)
                  if k + 1 < NTL:
                      s2(k + 1)
                  s3(k)
              P.barrier()
              if stop_after == "p1d":
                  break
              A.region(HT_off, m_after_HT)

              NW = 528
              NJP = 8
              xg = A.alloc("xg", [128, 5, D], F32)
              H2T = A.alloc("H2T", [128, 16, NW], BF16)
              A.region(R2_start, SB_END)
              aT = A.alloc("aT", [128, NJP, NW], BF16)
              gffn_b = A.alloc("gffnb", [128, D], F32)
              gfin_b = A.alloc("gfinb", [128, D], F32)
              h2s = A.alloc("h2s", [128, D], BF16)
              sgt = [A.alloc("sgt", [128, NW], F32) for _ in range(2)]
              ssq2 = A.alloc("ssq2", [128, 64], F32)
              rstd2 = A.alloc("rstd2", [128, 64], F32)
              junk2 = A.alloc("junk2", [128, D], BF16)
              P.dma("sp", gffn_b[:], g_ffn, writes=["gffnb"])
              P.dma("sp", gfin_b[:], g_fin, writes=["gfinb"])
              P.op("pool", lambda e: e.memset(ssq2[:], 0.0), writes=[("ssq2", i) for i in range(64)])
              PG, PU, PD = PA, PS, PO
              MIXK = [("mixA", hp, t4, h2) for hp in range(8) for t4 in range(4) for h2 in range(2)] + [("mixAs", b, hh) for b in range(4) for hh in range(2)] + \
                     [("mixP", ch) for ch in range(8)]
              stat_i = 0
              pd_i = 0
              gi_ = 0
              for g in range(NG):
                  subs = [(g * 512 + i * 128, 128, xp[g * 512 + i * 128:g * 512 + (i + 1) * 128, :],
                           yp[g * 512 + i * 128:g * 512 + (i + 1) * 128, :]) for i in range(4)]
                  if g == NG - 1:
                      subs.append((S, NS, xs[:, :], ys[:, :]))
                  ncol = sum(s_[1] for s_ in subs)
                  for si, (c0, n, xsrc, ydst) in enumerate(subs):
                      P.dma("sp", xg[0:n, si, :], xsrc, writes=[("xg", si)])
                  for cb in range(4):
                      t0, k0 = wget(("out", g, cb, 0))
                      t1, k1 = wget(("out", g, cb, 1))
                      wv = (v8(t0), v8(t1))
                      for si, (c0, n, xsrc, ydst) in enumerate(subs):
                          pd = PD[pd_i % 2]
                          pdk = ("PO", pd_i % 2)
                          pd_i += 1
                          calls = []
                          for kc in range(16):
                              src = mixA if kc < 8 else mixP
                              calls.append((pd[0:n, :], src[:, kc % 8, c0:c0 + n], wv[kc // 8][:, kc % 8, :], kc == 0, kc == 15))
                          _mm(P, calls, reads=[k0, k1] + MIXK, writes=[pdk])
                          P.op("dve", lambda e, pd=pd, n=n, si=si, cb=cb: e.tensor_tensor(
                              out=xg[0:n, si, cb * 512:(cb + 1) * 512], in0=xg[0:n, si, cb * 512:(cb + 1) * 512], in1=pd[0:n, :],
                              op=ALU.add), reads=[pdk, ("xg", si)], writes=[("xg", si)])
                      wdone(("out", g, cb, 0))
                      wdone(("out", g, cb, 1))
                  col = 0
                  for si, (c0, n, xsrc, ydst) in enumerate(subs):
                      sc = stat_i
                      stat_i += 1
                      P.op("act", lambda e, n=n, si=si, sc=sc: e.activation(out=junk2[0:n, :], in_=xg[0:n, si, :], func=AF.Square,
                                                                            accum_out=ssq2[0:n, sc:sc + 1]),
                           reads=[("xg", si)], writes=["junk2", ("ssq2", sc)])
                      P.op("act", lambda e, n=n, sc=sc: e.activation(out=rstd2[0:n, sc:sc + 1], in_=ssq2[0:n, sc:sc + 1], func=AF.Sqrt,
                                                                     scale=1.0 / D, bias=EPS), reads=[("ssq2", sc)], writes=[("rstd2", sc)])
                      P.op("dve", lambda e, n=n, sc=sc: e.reciprocal(out=rstd2[0:n, sc:sc + 1], in_=rstd2[0:n, sc:sc + 1]),
                           reads=[("rstd2", sc)], writes=[("rstd2", sc)])
                      P.op("dve", lambda e, n=n, si=si, sc=sc: e.scalar_tensor_tensor(
                          out=h2s[0:n, :], in0=xg[0:n, si, :], scalar=rstd2[0:n, sc:sc + 1], in1=gffn_b[0:n, :],
                          op0=ALU.mult, op1=ALU.mult), reads=[("xg", si), ("rstd2", sc), "gffnb"], writes=["h2s"])
                      for half in range(2):
                          calls = [(PTb[:, k, 0:n], h2s[0:n, (half * 8 + k) * 128:(half * 8 + k + 1) * 128], ident_b[0:n, 0:n])
                                   for k in range(8)]
                          _tr(P, calls, reads=["h2s", "identb"], writes=["PTb"])
                          if half == 0:
                              P.op("act", lambda e, col=col, n=n: e.copy(out=H2T[:, 0:8, col:col + n], in_=PTb[:, :, 0:n]),
                                   reads=["PTb"], writes=[("H2T", si)])
                          else:
                              P.op("dve", lambda e, col=col, n=n: e.tensor_copy(out=H2T[:, 8:16, col:col + n], in_=PTb[:, :, 0:n]),
                                   reads=["PTb"], writes=[("H2T", si)])
                      col += n
                  H2K = [("H2T", si) for si in range(len(subs))]
                  for pi, (j0, nj) in enumerate(PARTS):
                      for q in range(nj // 2):
                          gt, gkey = wget(("gate", g, pi, q))
                          ut, ukey = wget(("up", g, pi, q))
                          gslot, uslot = v16(gt), v16(ut)
                          for jj in range(2):
                              jl = q * 2 + jj
                              pg = PG[gi_ % 2]
                              pu = PU[gi_ % 2]
                              pgk, puk = ("PA", gi_ % 2), ("PS", gi_ % 2)
                              sg = sgt[gi_ % 2]
                              sgk = ("sgt", gi_ % 2)
                              gi_ += 1
                              for (wslot, wk, pp, ppk) in ((gslot, gkey, pg, pgk), (uslot, ukey, pu, puk)):
                                  calls = [(pp[:, 0:512], wslot[:, kc, jj * 128:(jj + 1) * 128], H2T[:, kc, 0:512], kc == 0, kc == 15)
                                           for kc in range(16)]
                                  _mm(P, calls, reads=[wk] + H2K, writes=[ppk])
                              P.op("act", lambda e, sg=sg, pg=pg: e.activation(out=sg[:, 0:512], in_=pg[:, 0:512], func=AF.Silu),
                                   reads=[pgk], writes=[sgk])
                              P.op("dve", lambda e, sg=sg, pu=pu, jl=jl: e.tensor_tensor(out=aT[:, jl, 0:512], in0=sg[:, 0:512],
                                                                                         in1=pu[:, 0:512], op=ALU.mult),
                                   reads=[sgk, puk], writes=[("aT", jl)])
                              if ncol > 512:
                                  for wi, (wslot, wk) in enumerate(((gslot, gkey), (uslot, ukey))):
                                      calls = [(PT[:, wi, 0:NS], wslot[:, kc, jj * 128:(jj + 1) * 128], H2T[:, kc, 512:NW],
                                                kc == 0, kc == 15, True) for kc in range(16)]
                                      _mm(P, calls, reads=[wk] + H2K, writes=["PT"])
                                  P.op("act", lambda e, sg=sg: e.activation(out=sg[:, 512:NW], in_=PT[:, 0, 0:NS], func=AF.Silu),
                                       reads=["PT"], writes=[sgk])
                                  P.op("dve", lambda e, sg=sg, jl=jl: e.tensor_tensor(out=aT[:, jl, 512:NW], in0=sg[:, 512:NW],
                                                                                      in1=PT[:, 1, 0:NS], op=ALU.mult),
                                       reads=[sgk, "PT"], writes=[("aT", jl)])
                          wdone(("gate", g, pi, q))
                          wdone(("up", g, pi, q))
                      ATK = [("aT", jl) for jl in range(nj)]
                      for cb in range(4):
                          dt_, wkey = wget(("down", g, pi, cb))
                          slot = v8(dt_)
                          col = 0
                          for si, (c0, n, xsrc, ydst) in enumerate(subs):
                              pd = PD[pd_i % 2]
                              pdk = ("PO", pd_i % 2)
                              pd_i += 1
                              calls = [(pd[0:n, :], aT[:, jl, col:col + n], slot[:, jl, :], jl == 0, jl == nj - 1) for jl in range(nj)]
                              _mm(P, calls, reads=[wkey] + ATK, writes=[pdk])
                              P.op("dve", lambda e, pd=pd, n=n, si=si, cb=cb: e.tensor_tensor(
                                  out=xg[0:n, si, cb * 512:(cb + 1) * 512], in0=xg[0:n, si, cb * 512:(cb + 1) * 512], in1=pd[0:n, :],
                                  op=ALU.add), reads=[pdk, ("xg", si)], writes=[("xg", si)])
                              col += n
                          wdone(("down", g, pi, cb))
                  for si, (c0, n, xsrc, ydst) in enumerate(subs):
                      sc = stat_i
                      stat_i += 1
                      P.op("act", lambda e, n=n, si=si, sc=sc: e.activation(out=junk2[0:n, :], in_=xg[0:n, si, :], func=AF.Square,
                                                                            accum_out=ssq2[0:n, sc:sc + 1]),
                           reads=[("xg", si)], writes=["junk2", ("ssq2", sc)])
                      P.op("act", lambda e, n=n, sc=sc: e.activation(out=rstd2[0:n, sc:sc + 1], in_=ssq2[0:n, sc:sc + 1], func=AF.Sqrt,
                                                                     scale=1.0 / D, bias=EPS), reads=[("ssq2", sc)], writes=[("rstd2", sc)])
                      P.op("dve", lambda e, n=n, sc=sc: e.reciprocal(out=rstd2[0:n, sc:sc + 1], in_=rstd2[0:n, sc:sc + 1]),
                           reads=[("rstd2", sc)], writes=[("rstd2", sc)])
                      P.op("dve", lambda e, n=n, si=si, sc=sc: e.scalar_tensor_tensor(
                          out=xg[0:n, si, :], in0=xg[0:n, si, :], scalar=rstd2[0:n, sc:sc + 1], in1=gfin_b[0:n, :],
                          op0=ALU.mult, op1=ALU.mult), reads=[("xg", si), ("rstd2", sc), "gfinb"], writes=[("xg", si)])
                      P.dma("sp", ydst, xg[0:n, si, :], reads=[("xg", si)], is_output=True)
          P.finish(block)
        if _os.environ.get("PROG_LOG"):
            with open(_os.environ["PROG_LOG"], "w") as fh:
                for rec in P.log:
                    fh.write(repr(rec) + "\n")
    return nc


def _t5_bucket(dist):
    dist = np.asarray(dist)
    df = np.maximum(dist, 1).astype(np.float32)
    large = 16 + (np.log(df / np.float32(16)) / np.float32(math.log(2048 / 16)) * np.float32(16)).astype(np.int32)
    large = np.minimum(large, 31)
    return np.where(dist < 16, dist, large)


def _bias_tables(rel_bias):
    rb = np.asarray(rel_bias, dtype=np.float32)
    ki = np.arange(128)[:, None]
    qc = np.arange(256)[None, :]
    dist = np.where(qc < 128, qc - ki, 128 + (qc - 128) - ki)
    valid = (dist >= 0) & (dist <= 128)
    distc = np.clip(dist, 0, 128)
    btab = np.empty((8, 128, 2, 3, 256), np.float32)
    for bi, d in enumerate((1, 4, 16)):
        bucket = _t5_bucket(d * distc)
        for h in range(16):
            vals = rb[bucket, h]
            btab[h // 2, :, h % 2, bi, :] = np.where(valid, vals, np.float32(NEG))
    btab = btab.reshape(8, 128, 2 * 3 * 256)
    sb = np.full((12, 128, 8, 2, 4, 4), NEG, np.float32)
    i = np.arange(128)
    for h in range(16):
        hp, hh = h // 2, h % 2
        for t in range(4):
            dA = 128 + t - i
            sb[0, :, hp, hh, :, t] = np.where(i >= t, rb[_t5_bucket(np.clip(dA, 0, 128)), h], np.float32(NEG))[:, None]
            sb[1 + t, :, hp, hh, :, t] = rb[_t5_bucket(4 * (128 - i)), h][:, None]
            sb[5 + t, :, hp, hh, :, t] = rb[_t5_bucket(16 * (128 - i)), h][:, None]
            for bq in range(4):
                for tp in range(t + 1):
                    sb[9, 4 * bq + tp, hp, hh, bq, t] = rb[_t5_bucket(np.array(t - tp)), h]
                sb[10, 4 * bq + t, hp, hh, bq, t] = rb[0, h]
                sb[11, 4 * bq + t, hp, hh, bq, t] = rb[0, h]
    sbias = np.ascontiguousarray(sb.reshape(12, 128, 256).transpose(1, 0, 2)).reshape(128, 12 * 256)
    return btab, sbias


def _selab():
    sel = np.zeros((128, 256), np.float32)
    for m in range(64):
        sel[m + 64, m] = 1.0
    for m in range(64, 128):
        sel[m - 64, 128 + m] = 1.0
    return sel


_NC_CACHE = {}


def kernel(x_prompt, x_sample, cache_k, cache_v, state_pool, rel_bias, norm_mix, w_in, w_pool, pool_scale,
           w_out, norm_ffn, w_gate, w_up, w_down, norm_final):
    f32 = lambda a: np.ascontiguousarray(np.asarray(a, dtype=np.float32))
    x_prompt, x_sample = f32(x_prompt), f32(x_sample)
    cache_k, cache_v, state_pool = f32(cache_k), f32(cache_v), f32(state_pool)
    btab, sbias = _bias_tables(rel_bias)
    shared = {
        "w_in": f32(w_in)[0], "w_pool": f32(w_pool)[0], "w_out": f32(w_out)[0], "w_gate": f32(w_gate)[0],
        "w_up": f32(w_up)[0], "w_down": f32(w_down)[0],
        "g_mix": np.ascontiguousarray(np.broadcast_to(f32(norm_mix).reshape(1, D), (128, D))),
        "g_ffn": np.ascontiguousarray(np.broadcast_to(f32(norm_ffn).reshape(1, D), (128, D))),
        "g_fin": np.ascontiguousarray(np.broadcast_to(f32(norm_final).reshape(1, D), (128, D))),
        "pscale": np.ascontiguousarray(f32(pool_scale).reshape(8, 128).T),
        "btab": btab, "sbias": sbias,
        "ident": np.eye(128, dtype=np.float32),
        "selab": _selab(),
        "invc": np.ascontiguousarray(np.broadcast_to((1.0 / np.arange(1, 17, dtype=np.float32))[None, :], (128, 16))),
    }
    in_maps = []
    for c in range(NCORES):
        m = dict(shared)
        m["xp"] = x_prompt[c]
        m["xs"] = np.ascontiguousarray(x_sample[4 * c:4 * c + 4].reshape(NS, D))
        m["ck"] = np.ascontiguousarray(cache_k[0, 4 * c:4 * c + 4].reshape(4, 2048, 1024))
        m["cv"] = np.ascontiguousarray(cache_v[0, 4 * c:4 * c + 4].reshape(4, 2048, 1024))
        m["spst"] = np.ascontiguousarray(state_pool[0, 4 * c:4 * c + 4])
        in_maps.append(m)
    if "nc" not in _NC_CACHE:
        _NC_CACHE["nc"] = build_program()
    nc = _NC_CACHE["nc"]
    res = run_bass_kernel_spmd(nc, in_maps, core_ids=list(range(NCORES)))
    R = res.results
    y_prompt = np.stack([R[c]["yp"] for c in range(NCORES)], 0)
    y_sample = np.concatenate([R[c]["ys"].reshape(4, 4, D) for c in range(NCORES)], 0)
    prompt_k = np.stack([R[c]["pk"].reshape(S, 16, 64) for c in range(NCORES)], 0)[None]
    prompt_v = np.stack([R[c]["pv"].reshape(S, 16, 64) for c in range(NCORES)], 0)[None]
    prompt_pool = np.stack([R[c]["ppool"] for c in range(NCORES)], 0)[None]
    sample_k = np.concatenate([R[c]["sk"].reshape(4, 4, 16, 64) for c in range(NCORES)], 0)[None]
    sample_v = np.concatenate([R[c]["sv"].reshape(4, 4, 16, 64) for c in range(NCORES)], 0)[None]
    sample_pool = np.concatenate([R[c]["spool"] for c in range(NCORES)], 0)[None]
    return (y_prompt.astype(np.float32), y_sample.astype(np.float32), prompt_k.astype(np.float32),
            prompt_v.astype(np.float32), prompt_pool.astype(np.float32), sample_k.astype(np.float32),
            sample_v.astype(np.float32), sample_pool.astype(np.float32))
```

```python
import math
from contextlib import ExitStack

import numpy as np
import concourse.bass as bass
import concourse.mybir as mybir
from concourse.bass_utils import run_bass_kernel_spmd

F32 = mybir.dt.float32
BF16 = mybir.dt.bfloat16
AF = mybir.ActivationFunctionType
ALU = mybir.AluOpType

NCORES = 8
D = 2048
S = 2048
NS = 16
NT = S + NS
DFF = 5632
NJ = DFF // 128
EPS = 1e-6
NEG = -30000.0
SB_BASE = 16512
SB_END = 229376

ENGS = ("pe", "act", "dve", "pool", "sp")
SELF_SYNC = {"pe": False, "act": True, "dve": True, "pool": True, "sp": False}


def _is_psum_key(k):
    name = k[0] if isinstance(k, tuple) else k
    return name in ("PA", "PT", "PTb", "PS", "PO", "PON")


class Prog:
    NDMA = 8

    def __init__(self, nc, stack):
        self.nc = nc
        self.ops = {e: [] for e in ENGS}
        self.sem = {e: stack.enter_context(nc.semaphore("prog_" + e)) for e in ENGS}
        self.cnt = {e: 0 for e in ENGS}
        self.dsem = {e: [stack.enter_context(nc.semaphore("dma_%s_%d" % (e, i))) for i in range(self.NDMA)]
                     for e in ("sp", "pool", "act")}
        self.dcnt = {e: [0] * self.NDMA for e in self.dsem}
        self.dnext = {e: 0 for e in self.dsem}
        self.lastw = {}
        self.readers = {}
        self.waited = {e: {} for e in ENGS}
        self.out_tokens = []
        self.all_dma_tokens = {}
        self.total = 0
        self.maxops = None
        self.log = []

    def _deps(self, eng, reads, writes):
        toks = []
        for r in reads:
            w = self.lastw.get(r)
            if w is not None:
                toks.append(w)
            if _is_psum_key(r):
                toks.extend(self.readers.get(r, ()))
        for w_ in writes:
            w = self.lastw.get(w_)
            if w is not None:
                toks.append(w)
            toks.extend(self.readers.get(w_, ()))
        need = {}
        for (sname, sem, val, teng, is_dma) in toks:
            if (not is_dma) and teng == eng and not SELF_SYNC[eng]:
                continue
            if self.waited[eng].get(sname, 0) >= val:
                continue
            if need.get(sname, (None, 0))[1] < val:
                need[sname] = (sem, val)
        for sname, (sem, val) in need.items():
            self.waited[eng][sname] = val
        return list(need.values())

    def _record(self, tok, reads, writes):
        for r in reads:
            self.readers.setdefault(r, []).append(tok)
        for w in writes:
            self.lastw[w] = tok
            self.readers[w] = []

    def op(self, eng, fn, reads=(), writes=()):
        self.total += 1
        if self.maxops is not None and self.total > self.maxops:
            return None
        self.log.append((self.total, eng, "op", tuple(writes)))
        waits = self._deps(eng, reads, writes)
        self.cnt[eng] += 1
        val = self.cnt[eng]
        sem = self.sem[eng]

        def run(e, fn=fn, waits=waits, sem=sem):
            for (s, v) in waits:
                e.wait_ge(s, v)
            ins = fn(e)
            ins.then_inc(sem, 1)

        self.ops[eng].append(run)
        tok = ("prog_" + eng, sem, val, eng, False)
        self._record(tok, reads, writes)
        return tok

    def dma(self, eng, out, in_, reads=(), writes=(), is_output=False):
        self.total += 1
        if self.maxops is not None and self.total > self.maxops:
            return None
        self.log.append((self.total, eng, "dma", tuple(writes)))
        i = self.dnext[eng]
        self.dnext[eng] = (i + 1) % self.NDMA
        sem = self.dsem[eng][i]
        prev = self.dcnt[eng][i]
        self.dcnt[eng][i] = prev + 16
        val = prev + 16
        sname = "dma_%s_%d" % (eng, i)
        waits = self._deps(eng, reads, writes)
        if prev > 0 and self.waited[eng].get(sname, 0) < prev:
            waits.append((sem, prev))
            self.waited[eng][sname] = prev

        def run(e, waits=waits, sem=sem, out=out, in_=in_):
            for (s, v) in waits:
                e.wait_ge(s, v)
            e.dma_start(out=out, in_=in_).then_inc(sem, 16)

        self.ops[eng].append(run)
        tok = (sname, sem, val, eng, True)
        self._record(tok, reads, writes)
        self.all_dma_tokens[sname] = (sem, val)
        if is_output:
            self.out_tokens.append(tok)
        return tok

    def barrier(self):
        targets = {}
        for en in ENGS:
            if self.cnt[en] > 0:
                targets["prog_" + en] = (self.sem[en], self.cnt[en], en)
        for sname, (sem, val) in self.all_dma_tokens.items():
            targets[sname] = (sem, val, None)
        for en in ENGS:
            waits = []
            for sname, (sem, val, teng) in targets.items():
                if teng == en and not SELF_SYNC[en]:
                    continue
                if self.waited[en].get(sname, 0) >= val:
                    continue
                self.waited[en][sname] = val
                waits.append((sem, val))

            def run(e, waits=waits):
                for (s, v) in waits:
                    e.wait_ge(s, v)

            if waits:
                self.ops[en].append(run)

    def finish(self, block):
        self.maxops = None
        self.barrier()
        fin = {}
        for (sname, sem, val, teng, is_dma) in self.out_tokens:
            if fin.get(sname, (None, 0))[1] < val:
                fin[sname] = (sem, val)

        def run(e, fin=fin):
            for (s, v) in fin.values():
                e.wait_ge(s, v)

        self.ops["sp"].append(run)
        hmap = {"pe": block.tensor, "act": block.scalar, "dve": block.vector, "pool": block.gpsimd, "sp": block.sync}
        for en in ENGS:
            ops = self.ops[en]
            if not ops:
                continue

            def body(e, ops=ops):
                for f in ops:
                    f(e)

            hmap[en](body)


class Arena:
    def __init__(self, nc):
        self.nc = nc
        self.top = SB_BASE
        self.limit = SB_END
        self.n = 0

    def region(self, start, end):
        self.top = start
        self.limit = end

    def alloc(self, name, shape, dt):
        esz = 2 if dt == BF16 else 4
        nbytes = int(np.prod(shape[1:])) * esz
        off = (self.top + 31) // 32 * 32
        assert off + nbytes <= self.limit, ("SBUF overflow", name, off, nbytes, self.limit)
        self.top = off + nbytes
        self.n += 1
        return self.nc.alloc_sbuf_tensor_at("%s_%d" % (name, self.n), list(shape), dt, offset=off)

    def mark(self):
        return self.top

    def release(self, m):
        self.top = m


def ssl(start, count, step):
    return slice(start, start + (count - 1) * step + 1, step)


def _mm(P, calls, reads, writes):
    def fn(e, calls=calls):
        ins = None
        for c in calls:
            (o, l, r, s, t) = c[:5]
            if len(c) > 5 and c[5]:
                ins = e.matmul(o, lhsT=l, rhs=r, start=s, stop=t, skip_group_check=True)
            else:
                ins = e.matmul(o, lhsT=l, rhs=r, start=s, stop=t)
        return ins
    return P.op("pe", fn, reads, writes)


def _tr(P, calls, reads, writes):
    def fn(e, calls=calls):
        ins = None
        for (o, i, idn) in calls:
            ins = e.transpose(o, i, idn)
        return ins
    return P.op("pe", fn, reads, writes)


def _trf(P, calls, reads, writes):
    def fn(e, calls=calls):
        ins = None
        for (o, i, idn) in calls:
            ins = e.matmul(o, lhsT=i, rhs=idn, start=True, stop=True)
        return ins
    return P.op("pe", fn, reads, writes)


def build_program(stop_after=None):
    nc = bass.Bass("TRN2", target_bir_lowering=False)

    def din(name, shape):
        return nc.dram_tensor(name, list(shape), F32, kind="ExternalInput").ap()

    def dout(name, shape):
        return nc.dram_tensor(name, list(shape), F32, kind="ExternalOutput").ap()

    xp = din("xp", [S, D])
    xs = din("xs", [NS, D])
    ck = din("ck", [4, 2048, 1024])
    cv = din("cv", [4, 2048, 1024])
    spst_d = din("spst", [4, 15, 1024])
    w_in = din("w_in", [D, 4096])
    w_pool = din("w_pool", [4, 256, 256])
    w_out = din("w_out", [D, D])
    w_gate = din("w_gate", [D, DFF])
    w_up = din("w_up", [D, DFF])
    w_down = din("w_down", [DFF, D])
    g_mix = din("g_mix", [128, D])
    g_ffn = din("g_ffn", [128, D])
    g_fin = din("g_fin", [128, D])
    pscale = din("pscale", [128, 8])
    btab = din("btab", [8, 128, 2 * 3 * 256])
    sbias_d = din("sbias", [128, 12 * 256])
    ident_d = din("ident", [128, 128])
    invc_d = din("invc", [128, 16])
    selab_d = din("selab", [128, 256])

    yp = dout("yp", [S, D])
    ys = dout("ys", [NS, D])
    pk = dout("pk", [S, 1024])
    pv = dout("pv", [S, 1024])
    ppool = dout("ppool", [15, 1024])
    sk = dout("sk", [NS, 1024])
    sv = dout("sv", [NS, 1024])
    spool = dout("spool", [4, 15, 1024])

    with ExitStack() as st:
        P = Prog(nc, st)
        import os as _os
        if _os.environ.get("PROG_MAXOPS"):
            P.maxops = int(_os.environ["PROG_MAXOPS"])
        A = Arena(nc)
        psum = lambda name, shape, dt: st.enter_context(nc.psum_tensor(name, shape, dt))

        RING = 4
        ring = [A.alloc("ring", [128, 4096], BF16) for _ in range(RING)]
        v16 = lambda t: t[:, :].rearrange("p (k n) -> p k n", k=16)
        v8 = lambda t: t[:, :].rearrange("p (k n) -> p k n", k=8)
        ident_f = A.alloc("identf", [128, 128], F32)
        ident_b = A.alloc("identb", [128, 128], BF16)
        ones_b = A.alloc("onesb", [128, 128], BF16)
        invc = A.alloc("invc", [128, 16], F32)
        psc = A.alloc("psc", [128, 8], F32)
        selab = A.alloc("selab", [128, 256], F32)
        QTs = A.alloc("QTs", [128, 8, 2, NS], BF16)
        KTs = A.alloc("KTs", [128, 8, NS], BF16)
        VTs = A.alloc("VTs", [128, 8, NS], BF16)
        mixA = A.alloc("mixA", [128, 8, NT], BF16)
        m_after_mixA = A.mark()

        PA = [psum("PA%d" % i, [128, 512], F32) for i in range(2)]
        PT = psum("PT", [128, 4, 128], F32)
        PTb = psum("PTb", [128, 8, 128], BF16)
        PS = [psum("PS%d" % i, [128, 512], F32) for i in range(2)]
        PO = [psum("PO%d" % i, [128, 512], F32) for i in range(2)]

        wstate = {"next": 0, "seq": []}
        widx = {}

        def wreg(key, dstf, src):
            widx[key] = len(wstate["seq"])
            wstate["seq"].append((dstf, src))

        def wissue_upto(n):
            while wstate["next"] < min(n, len(wstate["seq"])):
                i = wstate["next"]
                dstf, src = wstate["seq"][i]
                sk_ = wstate.get("scr_keys", {}).get(i)
                P.dma("pool", dstf(ring[i % RING]), src, reads=([sk_] if sk_ is not None else []), writes=[("w", i % RING)])
                wstate["next"] += 1

        def wget(key):
            i = widx[key]
            assert i < wstate["next"], ("weight tile not issued", key)
            return ring[i % RING], ("w", i % RING)

        def wdone(key):
            wissue_upto(widx[key] + RING + 1)

        w_in_v = w_in.rearrange("(kc p) n -> p kc n", p=128)
        w_out_v = w_out.rearrange("(kc p) n -> p kc n", p=128)
        w_gate_v = w_gate.rearrange("(kc p) n -> p kc n", p=128)
        w_up_v = w_up.rearrange("(kc p) n -> p kc n", p=128)
        w_down_v = w_down.rearrange("(j p) n -> p j n", p=128)
        for hp in range(8):
            for g in range(3):
                wreg(("in", hp, g), (lambda t: v16(t)[:, :, 0:128]),
                     w_in_v[:, :, g * 1024 + hp * 128: g * 1024 + (hp + 1) * 128])
        for c in range(8):
            wreg(("u", c), (lambda t: v16(t)[:, :, 0:128]), w_in_v[:, :, 3072 + c * 128: 3072 + (c + 1) * 128])
        PARTS = [(0, 8), (8, 8), (16, 8), (24, 8), (32, 8), (40, 4)]
        NG = 4
        p2tiles = []
        for cb in range(4):
            for half in range(2):
                p2tiles.append((("out", cb, half), (lambda t: t.rearrange("p (k n) -> p k n", k=8)),
                                w_out_v[:, half * 8:(half + 1) * 8, cb * 512:(cb + 1) * 512], 4096))
        for pi, (j0, nj) in enumerate(PARTS):
            for q in range(nj // 2):
                c0 = (j0 + q * 2) * 128
                p2tiles.append((("gate", pi, q), (lambda t: t.rearrange("p (k n) -> p k n", k=16)), w_gate_v[:, :, c0:c0 + 256], 4096))
                p2tiles.append((("up", pi, q), (lambda t: t.rearrange("p (k n) -> p k n", k=16)), w_up_v[:, :, c0:c0 + 256], 4096))
            for cb in range(4):
                p2tiles.append((("down", pi, cb), (lambda t, nj=nj: t[:, 0:nj * 512].rearrange("p (k n) -> p k n", k=nj)),
                                w_down_v[:, j0:j0 + nj, cb * 512:(cb + 1) * 512], nj * 512))
        NP2 = len(p2tiles)
        wscr = nc.dram_tensor("wscr", [NP2, 128, 4096], BF16, kind="Internal").ap()
        for g in range(NG):
            for ti, (ks, vf, src, nval) in enumerate(p2tiles):
                wreg((ks[0], g) + ks[1:], (lambda t, nval=nval: t[:, 0:nval]), wscr[ti][:, 0:nval])
        wstate["scr_keys"] = {}
        for g in range(NG):
            for ti, (ks, vf, src, nval) in enumerate(p2tiles):
                wstate["scr_keys"][widx[(ks[0], g) + ks[1:]]] = ("scr", ti)
        pc_state = {"next": 0}

        def precast(n):
            while n > 0 and pc_state["next"] < NP2:
                ti = pc_state["next"]
                ks, vf, src, nval = p2tiles[ti]
                P.dma("pool", vf(wscr[ti]), src, writes=[("scr", ti)])
                pc_state["next"] += 1
                n -= 1

        with nc.Block() as block:
          for _once in (0,):
              P.dma("sp", ident_f[:], ident_d, writes=["identf"])
              P.dma("sp", invc[:], invc_d, writes=["invc"])
              P.dma("sp", psc[:], pscale, writes=["psc"])
              P.dma("sp", selab[:], selab_d, writes=["selab"])
              P.op("dve", lambda e: e.tensor_copy(out=ident_b[:], in_=ident_f[:]), reads=["identf"], writes=["identb"])
              P.op("pool", lambda e: e.memset(ones_b[:], 1.0), writes=["onesb"])
              wissue_upto(RING)
              P.op("pool", lambda e: e.memset(QTs[64:128, :, 0, :], 0.0), writes=["QTs0a"])
              P.op("pool", lambda e: e.memset(QTs[0:64, :, 1, :], 0.0), writes=["QTs0b"])

              HT_off = (A.mark() + 31) // 32 * 32
              HT = A.alloc("HT", [128, 16, NT], BF16)
              m_after_HT = A.mark()
              xst = [A.alloc("xst", [128, D], F32) for _ in range(2)]
              hbt = [A.alloc("hbt", [128, D], BF16) for _ in range(2)]
              junk = A.alloc("junk", [128, D], BF16)
              gmix_b = A.alloc("gmixb", [128, D], F32)
              ssq = A.alloc("ssq", [128, 20], F32)
              rstd = A.alloc("rstd", [128, 20], F32)
              P.dma("sp", gmix_b[:], g_mix, writes=["gmixb"])
              P.op("pool", lambda e: e.memset(ssq[:], 0.0), writes=[("ssq", t) for t in range(17)])
              xst.append(A.alloc("xst", [128, D], F32))

              def p1a_stage_a(t):
                  n = 128 if t < 16 else NS
                  b = t % 3
                  src = xp[t * 128:(t + 1) * 128, :] if t < 16 else xs[:, :]
                  P.dma("sp", xst[b][0:n, :], src, writes=[("xst", b)])
                  P.op("act", lambda e, b=b, n=n, t=t: e.activation(out=junk[0:n, :], in_=xst[b][0:n, :], func=AF.Square,
                                                                    accum_out=ssq[0:n, t:t + 1]),
                       reads=[("xst", b)], writes=["junk", ("ssq", t)])
                  P.op("act", lambda e, n=n, t=t: e.activation(out=rstd[0:n, t:t + 1], in_=ssq[0:n, t:t + 1], func=AF.Sqrt,
                                                               scale=1.0 / D, bias=EPS),
                       reads=[("ssq", t)], writes=[("rstd", t)])

              def p1a_stage_b(t):
                  n = 128 if t < 16 else NS
                  b = t % 3
                  hb_ = t % 2
                  P.op("dve", lambda e, n=n, t=t: e.reciprocal(out=rstd[0:n, t:t + 1], in_=rstd[0:n, t:t + 1]),
                       reads=[("rstd", t)], writes=[("rstd", t)])
                  P.op("dve", lambda e, b=b, hb_=hb_, n=n, t=t: e.scalar_tensor_tensor(
                      out=hbt[hb_][0:n, :], in0=xst[b][0:n, :], scalar=rstd[0:n, t:t + 1], in1=gmix_b[0:n, :],
                      op0=ALU.mult, op1=ALU.mult), reads=[("xst", b), ("rstd", t), "gmixb"], writes=[("hbt", hb_)])
                  for half in range(2):
                      calls = [(PTb[:, k, 0:n], hbt[hb_][0:n, (half * 8 + k) * 128:(half * 8 + k + 1) * 128], ident_b[0:n, 0:n])
                               for k in range(8)]
                      _tr(P, calls, reads=[("hbt", hb_), "identb"], writes=["PTb"])
                      c0 = t * 128 if t < 16 else S
                      if half == 0:
                          P.op("act", lambda e, half=half, c0=c0, n=n: e.copy(out=HT[:, half * 8:(half + 1) * 8, c0:c0 + n],
                                                                              in_=PTb[:, :, 0:n]),
                               reads=["PTb"], writes=[("HT", t)])
                      else:
                          P.op("dve", lambda e, half=half, c0=c0, n=n: e.tensor_copy(out=HT[:, half * 8:(half + 1) * 8, c0:c0 + n],
                                                                                     in_=PTb[:, :, 0:n]),
                               reads=["PTb"], writes=[("HT", t)])

              p1a_stage_a(0)
              for t in range(17):
                  if t + 1 < 17:
                      p1a_stage_a(t + 1)
                  p1a_stage_b(t)
              P.barrier()
              precast(6)
              if stop_after == "p1a":
                  break
              A.release(m_after_HT)
              HT_ALL = [("HT", t) for t in range(17)]

              TGS = [(i * 512, 512) for i in range(4)] + [(S, NS)]

              QTz = A.alloc("QTz", [128, 2, NT], BF16)
              P.op("pool", lambda e: e.memset(QTz[64:128, 0, :], 0.0), writes=["QTz0a"])
              P.op("pool", lambda e: e.memset(QTz[0:64, 1, :], 0.0), writes=["QTz0b"])
              KT = A.alloc("KT", [128, NT], BF16)
              VT = A.alloc("VT", [128, NT], BF16)
              V1 = A.alloc("V1", [128, 16, 192], BF16)
              VX = [A.alloc("VX", [128, 16, 192], BF16) for _ in range(2)]
              for vi, vb in enumerate((V1, VX[0], VX[1])):
                  P.op("pool", lambda e, vb=vb: e.memset(vb[:, :, 64:128], 1.0), writes=[("Vones", vi)])
              rdt = [A.alloc("rdt", [128, 512], F32) for _ in range(2)]
              ftmp = [A.alloc("ftmp", [128, 512], F32) for _ in range(2)]
              kvst = [A.alloc("kvst", [128, 4, 128], F32) for _ in range(2)]
              ACC = A.alloc("ACC", [128, 2, S], F32)
              BT = [A.alloc("BT", [128, 2, 3, 256], BF16) for _ in range(2)]
              PE_ = [A.alloc("Pexp", [128, 512], BF16) for _ in range(4)]
              PSB = [PS[0], PS[1], PT[:, :, :].rearrange("p a b -> p (a b)")]
              PSK = [("PS", 0), ("PS", 1), "PT"]
              kvs_tmp = A.alloc("kvstmp", [128, 2, NS], F32)
              kvss = [A.alloc("kvss", [NS, 128], F32) for _ in range(4)]
              kvss_i = [0]

              ftmp_i = [0]
              kvst_i = [0]

              for hp in range(8):
                  bt = BT[hp % 2]
                  P.dma("pool", bt[:], btab[hp].rearrange("p (h b q) -> p h b q", h=2, b=3), writes=[("BT", hp % 2)])
                  pa_i = 0
                  for gi, gname in enumerate(("Q", "K", "V")):
                      slot_t, wkey = wget(("in", hp, gi))
                      slot = v16(slot_t)
                      for tgi, (c0, n) in enumerate(TGS):
                          pa = PA[pa_i % 2]
                          pkey = ("PA", pa_i % 2)
                          pa_i += 1
                          calls = [(pa[:, 0:n], slot[:, kc, 0:128], HT[:, kc, c0:c0 + n], kc == 0, kc == 15)
                                   for kc in range(16)]
                          _mm(P, calls, reads=[wkey] + HT_ALL, writes=[pkey])
                          if tgi == len(TGS) - 1:
                              wdone(("in", hp, gi))
                          if gname == "Q":
                              if tgi < 4:
                                  P.op("act", lambda e, pa=pa, c0=c0, n=n: e.mul(out=QTz[0:64, 0, c0:c0 + n], in_=pa[0:64, 0:n], mul=0.125),
                                       reads=[pkey, "QTz0a", "QTz0b"], writes=[("QT", tgi)])
                                  P.op("act", lambda e, pa=pa, c0=c0, n=n: e.mul(out=QTz[64:128, 1, c0:c0 + n], in_=pa[64:128, 0:n],
                                                                                 mul=0.125),
                                       reads=[pkey, "QTz0a", "QTz0b"], writes=[("QT", tgi)])
                              else:
                                  P.op("act", lambda e, pa=pa, hp=hp: e.mul(out=QTs[0:64, hp, 0, :], in_=pa[0:64, 0:NS], mul=0.125),
                                       reads=[pkey, "QTs0a", "QTs0b"], writes=[("QTs", hp)])
                                  P.op("act", lambda e, pa=pa, hp=hp: e.mul(out=QTs[64:128, hp, 1, :], in_=pa[64:128, 0:NS], mul=0.125),
                                       reads=[pkey, "QTs0a", "QTs0b"], writes=[("QTs", hp)])
                              continue
                          Tb = KT if gname == "K" else VT
                          Ts = KTs if gname == "K" else VTs
                          outd = pk if gname == "K" else pv
                          if tgi < 4:
                              f = ftmp[ftmp_i[0] % 2]
                              fkey = ("ftmp", ftmp_i[0] % 2)
                              ftmp_i[0] += 1
                              P.op("dve", lambda e, Tb=Tb, pa=pa, c0=c0, n=n: e.tensor_copy(out=Tb[:, c0:c0 + n], in_=pa[:, 0:n]),
                                   reads=[pkey], writes=[(gname + "T", tgi)])
                              P.op("act", lambda e, f=f, pa=pa, n=n: e.mul(out=f[:, 0:n], in_=pa[:, 0:n], mul=1.0),
                                   reads=[pkey, (gname + "T", tgi)], writes=[fkey])
                              calls = [(PT[:, i, :], f[:, i * 128:(i + 1) * 128], ident_f[:, :]) for i in range(4)]
                              _trf(P, calls, reads=[fkey, "identf"], writes=["PT"])
                              stg = kvst[kvst_i[0] % 2]
                              skey = ("kvst", kvst_i[0] % 2)
                              kvst_i[0] += 1
                              P.op("dve", lambda e, stg=stg: e.tensor_copy(out=stg[:], in_=PT[:]), reads=["PT"], writes=[skey])
                              if gname == "V":
                                  P.op("act", lambda e, tgi=tgi: e.mul(
                                      out=V1[:, tgi * 4:(tgi + 1) * 4, :].rearrange("p t (a c) -> p t a c", a=3)[:, :, 0:3:2, :],
                                      in_=PT[:].rearrange("p t (a c) -> p t a c", a=2), mul=1.0),
                                       reads=["PT", ("Vones", 0)], writes=[("V1", tgi)])
                              dview = outd[tgi * 512:(tgi + 1) * 512, hp * 128:(hp + 1) * 128].rearrange("(t p) c -> p t c", p=128)
                              P.dma("sp", dview, stg[:], reads=[skey], is_output=True)
                          else:
                              stg_s = kvss[kvss_i[0] % 4]
                              sskey = ("kvss", kvss_i[0] % 4)
                              kvss_i[0] += 1
                              outs = sk if gname == "K" else sv
                              gsel = 0 if gname == "K" else 1
                              P.op("dve", lambda e, Ts=Ts, pa=pa, hp=hp: e.tensor_copy(out=Ts[:, hp, :], in_=pa[:, 0:NS]),
                                   reads=[pkey], writes=[(gname + "Ts", hp)])
                              P.op("act", lambda e, pa=pa, gsel=gsel: e.mul(out=kvs_tmp[:, gsel, :], in_=pa[:, 0:NS], mul=1.0),
                                   reads=[pkey], writes=[("kvstmp", gsel)])
                              _trf(P, [(PT[0:NS, 0, :], kvs_tmp[:, gsel, :], ident_f[:, :])], reads=[("kvstmp", gsel), "identf"],
                                  writes=["PT"])
                              P.op("dve", lambda e, stg_s=stg_s: e.tensor_copy(out=stg_s[:, :], in_=PT[0:NS, 0, :]),
                                   reads=["PT"], writes=[sskey])
                              P.dma("sp", outs[:, hp * 128:(hp + 1) * 128], stg_s[:, :], reads=[sskey], is_output=True)

                  if stop_after == "p1b:proj":
                      break
                  QK_R = [("QT", i) for i in range(4)] + [("KT", i) for i in range(4)]
                  tiles = []
                  POB = [PO[0], PO[1], PA[0], PA[1]]
                  POK = [("PO", 0), ("PO", 1), ("PA", 0), ("PA", 1)]
                  for bi, d in enumerate((1, 4, 16)):
                      nb = 16 // d
                      for r in range(d):
                          for j in range(nb):
                              tiles.append((bi, d, nb, r, j))

                  def build_vx(bi, d, buf):
                      nb = 16 // d
                      for q4 in range(2):
                          calls = []
                          for k in range(8):
                              ti = q4 * 8 + k
                              r, j = ti // nb, ti % nb
                              st0 = r + d * 128 * j
                              calls.append((PTb[:, k, :], VT[:, ssl(st0, 128, d)], ident_b[:, :]))
                          _tr(P, calls, reads=[("VT", i) for i in range(4)] + ["identb"], writes=["PTb"])
                          P.op("dve", lambda e, buf=buf, q4=q4: e.tensor_copy(
                              out=VX[buf][:, q4 * 8:(q4 + 1) * 8, :].rearrange("p t (a c) -> p t a c", a=3)[:, :, 0:3:2, :],
                              in_=PTb[:].rearrange("p t (a c) -> p t a c", a=2)),
                               reads=["PTb", ("Vones", buf + 1)], writes=[("VX", buf, q4)])

                  build_vx(1, 4, 0)
                  build_vx(2, 16, 1)
                  if stop_after == "p1b:vx":
                      break

                  def emit_qk(k):
                      bi, d, nb, r, j = tiles[k]
                      nq = 2 if j + 1 < nb else 1
                      W = 128 * nq
                      st0 = r + d * 128 * j
                      psv = PSB[k % 3][:, 0:2 * W].rearrange("p (h w) -> p h w", h=2)
                      calls = [(psv, ident_b[:, :], bt[:, :, bi, 0:W], True, False),
                               (psv, KT[:, ssl(st0, 128, d)], QTz[:, :, ssl(st0, W, d)], False, True)]
                      _mm(P, calls, reads=QK_R + [("BT", hp % 2), "identb"], writes=[PSK[k % 3]])
                      P.op("act", lambda e, psv=psv, W=W, k=k: e.activation(
                          out=PE_[k % 4][:, 0:2 * W].rearrange("p (h w) -> p h w", h=2), in_=psv, func=AF.Exp),
                           reads=[PSK[k % 3]], writes=[("Pexp", k % 4)])

                  def emit_pv(k):
                      bi, d, nb, r, j = tiles[k]
                      nq = 2 if j + 1 < nb else 1
                      ti = r * nb + j
                      st0 = r + d * 128 * j
                      for hh in range(2):
                          if bi == 0:
                              vt = V1[:, ti, hh * 64:hh * 64 + 128]
                              vkeys = [("V1", i) for i in range(4)] + [("Vones", 0)]
                          else:
                              vt = VX[bi - 1][:, ti, hh * 64:hh * 64 + 128]
                              vkeys = [("VX", bi - 1, 0), ("VX", bi - 1, 1), ("Vones", bi)]
                          pe_ = PE_[k % 4][:, 0:256 * nq].rearrange("p (h w) -> p h w", h=2)
                          cb_ = 2 * hh + (k % 2)
                          nb_ = 2 * hh + ((k + 1) % 2)
                          cur, nxt = POB[cb_], POB[nb_]
                          curk, nxtk = POK[cb_], POK[nb_]
                          calls = [(cur[:, 0:128], vt, pe_[:, hh, 0:128], j == 0, True, True)]
                          wr = [curk]
                          if nq == 2:
                              calls += [(nxt[:, 0:128], vt, pe_[:, hh, 128:256], True, False, True)]
                              wr.append(nxtk)
                          _mm(P, calls, reads=[("Pexp", k % 4)] + vkeys, writes=wr)
                          dst = ACC[:, hh, ssl(st0, 128, d)]
                          srcv = cur[:, 0:128]
                          if d == 1:
                              akeys = [("ACC", hh, j // 4)]
                          elif d == 4:
                              akeys = [("ACC", hh, j)]
                          else:
                              akeys = [("ACC", hh, i) for i in range(4)]
                          if bi == 0 and hh == 1:
                              P.op("act", lambda e, dst=dst, srcv=srcv: e.mul(out=dst, in_=srcv, mul=1.0),
                                   reads=[curk], writes=akeys)
                          elif bi == 0:
                              P.op("dve", lambda e, dst=dst, srcv=srcv: e.tensor_copy(out=dst, in_=srcv),
                                   reads=[curk], writes=akeys)
                          else:
                              P.op("dve", lambda e, dst=dst, srcv=srcv: e.tensor_tensor(out=dst, in0=dst, in1=srcv, op=ALU.add),
                                   reads=[curk] + akeys, writes=akeys)

                  LOOK = 3
                  for k in range(min(LOOK, len(tiles))):
                      emit_qk(k)
                  for k in range(len(tiles)):
                      emit_pv(k)
                      if k + LOOK < len(tiles):
                          emit_qk(k + LOOK)
                  if stop_after == "p1b:att":
                      break
                  precast(10)
                  for tg4 in range(4):
                      cs = slice(tg4 * 512, (tg4 + 1) * 512)
                      AKg = [("ACC", 0, tg4), ("ACC", 1, tg4)]
                      pf = POB[2 + tg4 % 2]
                      pfk = POK[2 + tg4 % 2]
                      _mm(P, [(pf[:, 0:512], selab[:, 0:128], ACC[:, 0, cs], True, False),
                              (pf[:, 0:512], selab[:, 128:256], ACC[:, 1, cs], False, True)], reads=AKg + ["selab"], writes=[pfk])
                      rd = rdt[tg4 % 2]
                      P.op("dve", lambda e, rd=rd, pf=pf: e.reciprocal(out=rd[:, :], in_=pf[:, 0:512]), reads=[pfk],
                           writes=[("rdt", tg4 % 2)])
                      P.op("pool", lambda e, rd=rd, cs=cs, hp=hp: e.tensor_tensor(out=mixA[0:64, hp, cs], in0=ACC[0:64, 0, cs],
                                                                                 in1=rd[0:64, :], op=ALU.mult),
                           reads=AKg + [("rdt", tg4 % 2)], writes=[("mixA", hp, tg4, 0)])
                      P.op("pool", lambda e, rd=rd, cs=cs, hp=hp: e.tensor_tensor(out=mixA[64:128, hp, cs], in0=ACC[64:128, 1, cs],
                                                                                 in1=rd[64:128, :], op=ALU.mult),
                           reads=AKg + [("rdt", tg4 % 2)], writes=[("mixA", hp, tg4, 1)])

              precast(NP2)
              if stop_after is not None and stop_after.startswith("p1b:"):
                  break
              P.barrier()
              if stop_after == "p1b":
                  break
              A.release(m_after_HT)

              mixP = A.alloc("mixP", [128, 8, NT], BF16)
              m_after_mixP = A.mark()
              UT = A.alloc("UT", [128, 16 + S], F32)
              S1 = A.alloc("S1", [128, 16 + S], F32)
              S2 = A.alloc("S2", [128, 16 + S], F32)
              pooled = A.alloc("pooled", [128, 2, NT], BF16)
              wp = A.alloc("wp", [128, 4, 2, 256], BF16)
              spst = [A.alloc("spst", [64, 128], F32) for _ in range(2)]
              ucs = [A.alloc("ucs", [128, 4, 19], F32) for _ in range(3)]
              ptmp = A.alloc("ptmp", [128, 16], F32)
              ppst = [A.alloc("ppst", [16, 128], F32) for _ in range(2)]
              usst = [A.alloc("usst", [NS, 128], F32) for _ in range(2)]
              utail = A.alloc("utail", [128, 32], F32)
              P.dma("pool", wp[:], w_pool.rearrange("g (cc p) o -> p g cc o", p=128), writes=["wp"])
              P.op("pool", lambda e: e.memset(UT[:, 0:16], 0.0), writes=["UTpad"])
              P.op("pool", lambda e: e.memset(S1[:, 0:16], 0.0), writes=["S1pad"])
              P.op("pool", lambda e: e.memset(S2[:, 0:16], 0.0), writes=["S2pad"])
              for b in range(4):
                  P.dma("sp", spool[b, 0:11, :], spst_d[b, 4:15, :], is_output=True)
              pa_i = 0
              PCB = [PA[0], PA[1], PO[0], PO[1]]
              PCK = [("PA", 0), ("PA", 1), ("PO", 0), ("PO", 1)]
              for c in range(8):
                  slot_t, wkey = wget(("u", c))
                  slot = v16(slot_t)
                  g = c // 2
                  w = (2, 4, 8, 16)[g]
                  for tgi, (c0, n) in enumerate(TGS):
                      pa = PCB[pa_i % 4]
                      pkey = PCK[pa_i % 4]
                      pa_i += 1
                      calls = [(pa[:, 0:n], slot[:, kc, 0:128], HT[:, kc, c0:c0 + n], kc == 0, kc == 15)
                               for kc in range(16)]
                      _mm(P, calls, reads=[wkey] + HT_ALL, writes=[pkey])
                      if tgi == len(TGS) - 1:
                          wdone(("u", c))
                      if tgi < 4:
                          P.op("act", lambda e, pa=pa, c0=c0: e.mul(out=UT[:, 16 + c0:16 + c0 + 512], in_=pa[:, :], mul=1.0),
                               reads=[pkey, "UTpad"], writes=[("UT", tgi)])
                      else:
                          P.op("act", lambda e, pa=pa: e.mul(out=ucs[0][:, :, 15:19],
                                                             in_=pa[:, 0:NS].rearrange("p (b t) -> p b t", b=4), mul=1.0),
                               reads=[pkey], writes=["ucs0n"])
                          P.op("dve", lambda e, pa=pa: e.tensor_copy(out=utail[:, 0:NS], in_=pa[:, 0:NS]),
                               reads=[pkey], writes=["utailS"])
                  UTK = [("UT", i) for i in range(4)]
                  P.dma("sp", spst[c % 2][0:60, :], spst_d.rearrange("b r c -> (b r) c")[:, c * 128:(c + 1) * 128],
                        writes=[("spst", c % 2)])
                  _trf(P, [(PT[:, 0, 0:60], spst[c % 2][0:60, :], ident_f[0:60, 0:60])], reads=[("spst", c % 2), "identf"],
                      writes=["PT"])
                  P.op("dve", lambda e: e.tensor_copy(out=ucs[0][:, :, 0:15], in_=PT[:, 0, 0:60].rearrange("p (b r) -> p b r", b=4)),
                       reads=["PT"], writes=["ucs0s"])
                  P.op("dve", lambda e: e.tensor_copy(out=utail[:, 16:31], in_=UT[:, 16 + S - 15:16 + S]),
                       reads=UTK, writes=["utailP"])
                  _trf(P, [(PT[0:15, 1, :], utail[:, 16:31], ident_f[:, :]), (PT[0:NS, 2, :], utail[:, 0:NS], ident_f[:, :])],
                      reads=["utailP", "utailS", "identf"], writes=["PT"])
                  P.op("dve", lambda e, c=c: e.tensor_copy(out=ppst[c % 2][0:15, :], in_=PT[0:15, 1, :]),
                       reads=["PT"], writes=[("ppst", c % 2)])
                  P.op("dve", lambda e, c=c: e.tensor_copy(out=usst[c % 2][0:NS, :], in_=PT[0:NS, 2, :]),
                       reads=["PT"], writes=[("usst", c % 2)])
                  P.dma("sp", ppool[:, c * 128:(c + 1) * 128], ppst[c % 2][0:15, :], reads=[("ppst", c % 2)], is_output=True)
                  for b in range(4):
                      P.dma("sp", spool[b, 11:15, c * 128:(c + 1) * 128], usst[c % 2][4 * b:4 * b + 4, :],
                            reads=[("usst", c % 2)], is_output=True)
                  bufs = [UT, S1, S2]
                  cur_i = 0
                  steps = int(math.log2(w))
                  src_keys = UTK + ["UTpad"]
                  for si in range(steps):
                      sh = 1 << si
                      dst_i = 1 if cur_i != 1 else 2
                      srcb, dstb = bufs[cur_i], bufs[dst_i]
                      dkey = "S%d" % dst_i
                      P.op("pool", lambda e, srcb=srcb, dstb=dstb, sh=sh: e.tensor_tensor(
                          out=dstb[:, 16:16 + S], in0=srcb[:, 16:16 + S], in1=srcb[:, 16 - sh:16 + S - sh], op=ALU.add),
                          reads=src_keys + [dkey + "pad"], writes=[dkey])
                      cur_i = dst_i
                      src_keys = [dkey, dkey + "pad"]
                  sw = bufs[cur_i]
                  P.op("dve", lambda e, sw=sw, w=w, c=c: e.scalar_tensor_tensor(
                      out=pooled[:, c % 2, 0:S], in0=sw[:, 16:16 + S], scalar=1.0 / w, in1=UT[:, 16:16 + S],
                      op0=ALU.mult, op1=ALU.subtract), reads=src_keys + UTK, writes=[("pooled", c % 2)])
                  P.op("dve", lambda e, sw=sw, w=w: e.tensor_tensor(out=ptmp[:, 0:w - 1], in0=sw[:, 16:16 + w - 1],
                                                                    in1=invc[:, 0:w - 1], op=ALU.mult),
                       reads=src_keys + ["invc"], writes=["ptmp"])
                  P.op("dve", lambda e, w=w, c=c: e.tensor_tensor(out=pooled[:, c % 2, 0:w - 1], in0=ptmp[:, 0:w - 1],
                                                                  in1=UT[:, 16:16 + w - 1], op=ALU.subtract),
                       reads=["ptmp"] + UTK + [("pooled", c % 2)], writes=[("pooled", c % 2)])
                  cur = 0
                  lo = 0
                  rk = ["ucs0n", "ucs0s"]
                  for si in range(steps):
                      sh = 1 << si
                      dst = 1 if cur != 1 else 2
                      nlo = lo + sh
                      wk = "ucs%d" % dst
                      P.op("dve", lambda e, cur=cur, dst=dst, nlo=nlo, sh=sh: e.tensor_tensor(
                          out=ucs[dst][:, :, nlo:19], in0=ucs[cur][:, :, nlo:19], in1=ucs[cur][:, :, nlo - sh:19 - sh], op=ALU.add),
                          reads=rk, writes=[wk])
                      cur, lo, rk = dst, nlo, [wk]
                  P.op("dve", lambda e, cur=cur, w=w, c=c: e.scalar_tensor_tensor(
                      out=pooled[:, c % 2, S:NT].rearrange("p (b t) -> p b t", b=4), in0=ucs[cur][:, :, 15:19], scalar=1.0 / w,
                      in1=ucs[0][:, :, 15:19], op0=ALU.mult, op1=ALU.subtract),
                      reads=rk + ["ucs0n", "ucs0s", ("pooled", c % 2)], writes=[("pooled", c % 2)])
                  if c % 2 == 1:
                      for co in range(2):
                          for tgi, (c0, n) in enumerate(TGS):
                              pa = PCB[pa_i % 4]
                              pkey = PCK[pa_i % 4]
                              pa_i += 1
                              calls = [(pa[:, 0:n], wp[:, g, ci, co * 128:(co + 1) * 128], pooled[:, ci, c0:c0 + n], ci == 0, ci == 1)
                                       for ci in range(2)]
                              _mm(P, calls, reads=["wp", ("pooled", 0), ("pooled", 1)], writes=[pkey])
                              ch = 2 * g + co
                              P.op("dve", lambda e, pa=pa, ch=ch, c0=c0, n=n: e.tensor_scalar_mul(
                                  out=mixP[:, ch, c0:c0 + n], in0=pa[:, 0:n], scalar1=psc[:, ch:ch + 1]),
                                  reads=[pkey, "psc"], writes=[("mixP", ch)])
              P.barrier()
              if stop_after == "p1c":
                  break
              R2_start = m_after_mixP
              A.region(HT_off, m_after_HT)

              sbias = A.alloc("sbias", [128, 12, 256], BF16)
              P.dma("pool", sbias[:], sbias_d.rearrange("p (t c) -> p t c", t=12), writes=["sbias"])
              kc_t = [A.alloc("kct", [128, 1024], BF16) for _ in range(4)]
              vc_t = [A.alloc("vct", [128, 1024], BF16) for _ in range(4)]
              kT = [A.alloc("kT", [128, 8, 128], BF16) for _ in range(2)]
              Psm = [A.alloc("Psm", [128, 256], BF16) for _ in range(2)]
              vnew = A.alloc("vnew", [NS, 8, 128], BF16)
              srd = A.alloc("srd", [128, 256], F32)
              PON, POD = PO[0], PO[1]
              _tr(P, [(PTb[0:NS, hp, :], VTs[:, hp, :], ident_b[:, :]) for hp in range(8)],
                  reads=[("VTs", hp) for hp in range(8)] + ["identb"], writes=["PTb"])
              P.op("dve", lambda e: e.tensor_copy(out=vnew[:], in_=PTb[0:NS, :, :]), reads=["PTb"], writes=["vnew"])
              QTS_K = [("QTs", hp) for hp in range(8)]
              kT.append(A.alloc("kT", [128, 8, 128], BF16))
              ttypes = [("A", 0)] + [("B", t) for t in range(4)] + [("C", t) for t in range(4)] + [("N", i) for i in range(3)]
              tl = []
              ld = 0
              for b in range(4):
                  for ti, (kind, t) in enumerate(ttypes):
                      d_ = {"b": b, "ti": ti, "kind": kind, "t": t, "first": ti == 0, "last": ti == len(ttypes) - 1, "k": len(tl)}
                      if kind != "N":
                          d_["sl"] = ld % 4
                          ld += 1
                      tl.append(d_)

              def s1(k):
                  d_ = tl[k]
                  if d_["kind"] == "N":
                      return
                  b, t, sl = d_["b"], d_["t"], d_["sl"]
                  if d_["kind"] == "A":
                      rows = slice(1920, 2048)
                  elif d_["kind"] == "B":
                      rows = slice(1536 + t, 2048, 4)
                  else:
                      rows = slice(t, 2048, 16)
                  P.dma("pool", kc_t[sl][:], ck[b, rows, :], writes=[("kct", sl)])
                  P.dma("pool", vc_t[sl][:], cv[b, rows, :], writes=[("vct", sl)])
                  kt = kT[k % 3]
                  _tr(P, [(PTb[:, hp, :], kc_t[sl][:, hp * 128:(hp + 1) * 128], ident_b[:, :]) for hp in range(8)],
                      reads=[("kct", sl), "identb"], writes=["PTb"])
                  P.op("dve", lambda e, kt=kt: e.tensor_copy(out=kt[:], in_=PTb[:]), reads=["PTb"], writes=[("kT", k % 3)])

              def s2(k):
                  d_ = tl[k]
                  ti = d_["ti"]
                  psb = PS[k % 2]
                  pskey = ("PS", k % 2)
                  psm = Psm[k % 2]
                  pmkey = ("Psm", k % 2)
                  if d_["kind"] != "N":
                      nk = 128
                      kt = kT[k % 3]
                      calls = [(psb[0:nk, 0:256], ident_b[:, :], sbias[:, ti, :], True, False, True)]
                      for hp in range(8):
                          calls.append((psb[0:nk, hp * 32:(hp + 1) * 32], kt[:, hp, :],
                                        QTs[:, hp, :, :].rearrange("p h q -> p (h q)"), False, hp == 7, True))
                      _mm(P, calls, reads=[("kT", k % 3), "sbias", "identb"] + QTS_K, writes=[pskey])
                  else:
                      nk = NS
                      calls = [(psb[0:nk, 0:256], ident_b[0:NS, 0:NS], sbias[0:NS, ti, :], True, False, True)]
                      for hp in range(8):
                          calls.append((psb[0:nk, hp * 32:(hp + 1) * 32], KTs[:, hp, :],
                                        QTs[:, hp, :, :].rearrange("p h q -> p (h q)"), False, hp == 7, True))
                      _mm(P, calls, reads=["sbias", "identb"] + QTS_K + [("KTs", hp) for hp in range(8)], writes=[pskey])
                  P.op("act", lambda e, psm=psm, psb=psb, nk=nk: e.activation(out=psm[0:nk, :], in_=psb[0:nk, 0:256], func=AF.Exp),
                       reads=[pskey], writes=[pmkey])

              def s3(k):
                  d_ = tl[k]
                  b, first, last = d_["b"], d_["first"], d_["last"]
                  psm = Psm[k % 2]
                  pmkey = ("Psm", k % 2)
                  if d_["kind"] != "N":
                      nk = 128
                      sl = d_["sl"]
                      vsrc = lambda hp, sl=sl: vc_t[sl][:, hp * 128:(hp + 1) * 128]
                      vkeys = [("vct", sl)]
                  else:
                      nk = NS
                      vsrc = lambda hp: vnew[0:NS, hp, :]
                      vkeys = ["vnew"]
                  calls = [(PON[:, hp * 32:(hp + 1) * 32], vsrc(hp), psm[0:nk, hp * 32:(hp + 1) * 32], first and hp == 0, last, True)
                           for hp in range(8)]
                  calls.append((POD[:, 0:256], ones_b[0:nk, :], psm[0:nk, :], first, last, True))
                  _mm(P, calls, reads=[pmkey, "onesb"] + vkeys, writes=["PON"])
                  if last:
                      P.op("dve", lambda e: e.reciprocal(out=srd[:], in_=POD[:, 0:256]), reads=["PON"], writes=["srd"])
                      for hh in range(2):
                          hs = slice(hh * 64, (hh + 1) * 64)
                          cs = slice(hh * NS + 4 * b, hh * NS + 4 * b + 4)
                          P.op("dve", lambda e, hs=hs, hh=hh, cs=cs, b=b: e.tensor_tensor(
                              out=mixA[hs, :, S + 4 * b:S + 4 * b + 4],
                              in0=PON[hs, 0:256].rearrange("p (h c) -> p h c", h=8)[:, :, cs],
                              in1=srd[hs, 0:256].rearrange("p (h c) -> p h c", h=8)[:, :, cs], op=ALU.mult),
                              reads=["PON", "srd"], writes=[("mixAs", b, hh)])

              NTL = len(tl)
              s1(0)
              s1(1)
              s2(0)
              for k in range(NTL):
                  if k + 2 < NTL:
                      s1(k + 2)
                  if k + 1 < NTL:
                      s2(k + 1)
                  s3(k)
              P.barrier()
              if stop_after == "p1d":
                  break
              A.region(HT_off, m_after_HT)

              NW = 528
              NJP = 8
              xg = A.alloc("xg", [128, 5, D], F32)
              H2T = A.alloc("H2T", [128, 16, NW], BF16)
              A.region(R2_start, SB_END)
              aT = A.alloc("aT", [128, NJP, NW], BF16)
              gffn_b = A.alloc("gffnb", [128, D], F32)
              gfin_b = A.alloc("gfinb", [128, D], F32)
              h2sb = [A.alloc("h2s", [128, D], BF16) for _ in range(2)]
              sgt = [A.alloc("sgt", [128, NW], F32) for _ in range(2)]
              ssq2 = A.alloc("ssq2", [128, 64], F32)
              rstd2 = A.alloc("rstd2", [128, 64], F32)
              junk2 = A.alloc("junk2", [128, D], BF16)
              P.dma("sp", gffn_b[:], g_ffn, writes=["gffnb"])
              P.dma("sp", gfin_b[:], g_fin, writes=["gfinb"])
              P.op("pool", lambda e: e.memset(ssq2[:], 0.0), writes=[("ssq2", i) for i in range(64)])
              PG, PU, PD = PA, PS, PO
              MIXK = [("mixA", hp, t4, h2) for hp in range(8) for t4 in range(4) for h2 in range(2)] + [("mixAs", b, hh) for b in range(4) for hh in range(2)] + \
                     [("mixP", ch) for ch in range(8)]
              stat_i = 0
              pd_i = 0
              gi_ = 0
              for g in range(NG):
                  subs = [(g * 512 + i * 128, 128, xp[g * 512 + i * 128:g * 512 + (i + 1) * 128, :],
                           yp[g * 512 + i * 128:g * 512 + (i + 1) * 128, :]) for i in range(4)]
                  if g == NG - 1:
                      subs.append((S, NS, xs[:, :], ys[:, :]))
                  ncol = sum(s_[1] for s_ in subs)
                  for si, (c0, n, xsrc, ydst) in enumerate(subs):
                      P.dma("sp", xg[0:n, si, :], xsrc, writes=[("xg", si)])
                  for cb in range(4):
                      t0, k0 = wget(("out", g, cb, 0))
                      t1, k1 = wget(("out", g, cb, 1))
                      wv = (v8(t0), v8(t1))
                      for si, (c0, n, xsrc, ydst) in enumerate(subs):
                          pd = PD[pd_i % 2]
                          pdk = ("PO", pd_i % 2)
                          pd_i += 1
                          calls = []
                          for kc in range(16):
                              src = mixA if kc < 8 else mixP
                              calls.append((pd[0:n, :], src[:, kc % 8, c0:c0 + n], wv[kc // 8][:, kc % 8, :], kc == 0, kc == 15))
                          _mm(P, calls, reads=[k0, k1] + MIXK, writes=[pdk])
                          P.op("dve", lambda e, pd=pd, n=n, si=si, cb=cb: e.tensor_tensor(
                              out=xg[0:n, si, cb * 512:(cb + 1) * 512], in0=xg[0:n, si, cb * 512:(cb + 1) * 512], in1=pd[0:n, :],
                              op=ALU.add), reads=[pdk, ("xg", si)], writes=[("xg", si)])
                      wdone(("out", g, cb, 0))
                      wdone(("out", g, cb, 1))
                  cols_ = []
                  cacc_ = 0
                  for (c0, n, xsrc, ydst) in subs:
                      cols_.append(cacc_)
                      cacc_ += n

                  def h2_stage_a(si):
                      nonlocal_stat = h2_stat
                      c0, n, xsrc, ydst = subs[si]
                      sc = nonlocal_stat[0]
                      nonlocal_stat[0] += 1
                      hb = h2sb[si % 2]
                      hk = ("h2s", si % 2)
                      P.op("act", lambda e, n=n, si=si, sc=sc: e.activation(out=junk2[0:n, :], in_=xg[0:n, si, :], func=AF.Square,
                                                                            accum_out=ssq2[0:n, sc:sc + 1]),
                           reads=[("xg", si)], writes=["junk2", ("ssq2", sc)])
                      P.op("act", lambda e, n=n, sc=sc: e.activation(out=rstd2[0:n, sc:sc + 1], in_=ssq2[0:n, sc:sc + 1], func=AF.Sqrt,
                                                                     scale=1.0 / D, bias=EPS), reads=[("ssq2", sc)], writes=[("rstd2", sc)])
                      P.op("dve", lambda e, n=n, sc=sc: e.reciprocal(out=rstd2[0:n, sc:sc + 1], in_=rstd2[0:n, sc:sc + 1]),
                           reads=[("rstd2", sc)], writes=[("rstd2", sc)])
                      P.op("dve", lambda e, n=n, si=si, sc=sc, hb=hb: e.scalar_tensor_tensor(
                          out=hb[0:n, :], in0=xg[0:n, si, :], scalar=rstd2[0:n, sc:sc + 1], in1=gffn_b[0:n, :],
                          op0=ALU.mult, op1=ALU.mult), reads=[("xg", si), ("rstd2", sc), "gffnb"], writes=[hk])

                  def h2_stage_b(si):
                      c0, n, xsrc, ydst = subs[si]
                      hb = h2sb[si % 2]
                      hk = ("h2s", si % 2)
                      col = cols_[si]
                      for half in range(2):
                          calls = [(PTb[:, k, 0:n], hb[0:n, (half * 8 + k) * 128:(half * 8 + k + 1) * 128], ident_b[0:n, 0:n])
                                   for k in range(8)]
                          _tr(P, calls, reads=[hk, "identb"], writes=["PTb"])
                          if half == 0:
                              P.op("act", lambda e, col=col, n=n: e.copy(out=H2T[:, 0:8, col:col + n], in_=PTb[:, :, 0:n]),
                                   reads=["PTb"], writes=[("H2T", si)])
                          else:
                              P.op("dve", lambda e, col=col, n=n: e.tensor_copy(out=H2T[:, 8:16, col:col + n], in_=PTb[:, :, 0:n]),
                                   reads=["PTb"], writes=[("H2T", si)])

                  h2_stat = [stat_i]
                  h2_stage_a(0)
                  for si in range(len(subs)):
                      if si + 1 < len(subs):
                          h2_stage_a(si + 1)
                      h2_stage_b(si)
                  stat_i = h2_stat[0]
                  H2K = [("H2T", si) for si in range(len(subs))]
                  for pi, (j0, nj) in enumerate(PARTS):
                      for q in range(nj // 2):
                          gt, gkey = wget(("gate", g, pi, q))
                          ut, ukey = wget(("up", g, pi, q))
                          gslot, uslot = v16(gt), v16(ut)
                          for jj in range(2):
                              jl = q * 2 + jj
                              pg = PG[gi_ % 2]
                              pu = PU[gi_ % 2]
                              pgk, puk = ("PA", gi_ % 2), ("PS", gi_ % 2)
                              sg = sgt[gi_ % 2]
                              sgk = ("sgt", gi_ % 2)
                              gi_ += 1
                              for (wslot, wk, pp, ppk) in ((gslot, gkey, pg, pgk), (uslot, ukey, pu, puk)):
                                  calls = [(pp[:, 0:512], wslot[:, kc, jj * 128:(jj + 1) * 128], H2T[:, kc, 0:512], kc == 0, kc == 15)
                                           for kc in range(16)]
                                  _mm(P, calls, reads=[wk] + H2K, writes=[ppk])
                              P.op("act", lambda e, sg=sg, pg=pg: e.activation(out=sg[:, 0:512], in_=pg[:, 0:512], func=AF.Silu),
                                   reads=[pgk], writes=[sgk])
                              P.op("dve", lambda e, sg=sg, pu=pu, jl=jl: e.tensor_tensor(out=aT[:, jl, 0:512], in0=sg[:, 0:512],
                                                                                         in1=pu[:, 0:512], op=ALU.mult),
                                   reads=[sgk, puk], writes=[("aT", jl)])
                              if ncol > 512:
                                  for wi, (wslot, wk) in enumerate(((gslot, gkey), (uslot, ukey))):
                                      calls = [(PT[:, wi, 0:NS], wslot[:, kc, jj * 128:(jj + 1) * 128], H2T[:, kc, 512:NW],
                                                kc == 0, kc == 15, True) for kc in range(16)]
                                      _mm(P, calls, reads=[wk] + H2K, writes=["PT"])
                                  P.op("act", lambda e, sg=sg: e.activation(out=sg[:, 512:NW], in_=PT[:, 0, 0:NS], func=AF.Silu),
                                       reads=["PT"], writes=[sgk])
                                  P.op("dve", lambda e, sg=sg, jl=jl: e.tensor_tensor(out=aT[:, jl, 512:NW], in0=sg[:, 512:NW],
                                                                                      in1=PT[:, 1, 0:NS], op=ALU.mult),
                                       reads=[sgk, "PT"], writes=[("aT", jl)])
                          wdone(("gate", g, pi, q))
                          wdone(("up", g, pi, q))
                      ATK = [("aT", jl) for jl in range(nj)]
                      for cb in range(4):
                          dt_, wkey = wget(("down", g, pi, cb))
                          slot = v8(dt_)
                          col = 0
                          for si, (c0, n, xsrc, ydst) in enumerate(subs):
                              pd = PD[pd_i % 2]
                              pdk = ("PO", pd_i % 2)
                              pd_i += 1
                              calls = [(pd[0:n, :], aT[:, jl, col:col + n], slot[:, jl, :], jl == 0, jl == nj - 1) for jl in range(nj)]
                              _mm(P, calls, reads=[wkey] + ATK, writes=[pdk])
                              P.op("dve", lambda e, pd=pd, n=n, si=si, cb=cb: e.tensor_tensor(
                                  out=xg[0:n, si, cb * 512:(cb + 1) * 512], in0=xg[0:n, si, cb * 512:(cb + 1) * 512], in1=pd[0:n, :],
                                  op=ALU.add), reads=[pdk, ("xg", si)], writes=[("xg", si)])
                              col += n
                          wdone(("down", g, pi, cb))
                  for si, (c0, n, xsrc, ydst) in enumerate(subs):
                      sc = stat_i
                      stat_i += 1
                      P.op("act", lambda e, n=n, si=si, sc=sc: e.activation(out=junk2[0:n, :], in_=xg[0:n, si, :], func=AF.Square,
                                                                            accum_out=ssq2[0:n, sc:sc + 1]),
                           reads=[("xg", si)], writes=["junk2", ("ssq2", sc)])
                      P.op("act", lambda e, n=n, sc=sc: e.activation(out=rstd2[0:n, sc:sc + 1], in_=ssq2[0:n, sc:sc + 1], func=AF.Sqrt,
                                                                     scale=1.0 / D, bias=EPS), reads=[("ssq2", sc)], writes=[("rstd2", sc)])
                      P.op("dve", lambda e, n=n, sc=sc: e.reciprocal(out=rstd2[0:n, sc:sc + 1], in_=rstd2[0:n, sc:sc + 1]),
                           reads=[("rstd2", sc)], writes=[("rstd2", sc)])
                      P.op("dve", lambda e, n=n, si=si, sc=sc: e.scalar_tensor_tensor(
                          out=xg[0:n, si, :], in0=xg[0:n, si, :], scalar=rstd2[0:n, sc:sc + 1], in1=gfin_b[0:n, :],
                          op0=ALU.mult, op1=ALU.mult), reads=[("xg", si), ("rstd2", sc), "gfinb"], writes=[("xg", si)])
                      P.dma("sp", ydst, xg[0:n, si, :], reads=[("xg", si)], is_output=True)
          P.finish(block)
        if _os.environ.get("PROG_LOG"):
            with open(_os.environ["PROG_LOG"], "w") as fh:
                for rec in P.log:
                    fh.write(repr(rec) + "\n")
    return nc


def _t5_bucket(dist):
    dist = np.asarray(dist)
    df = np.maximum(dist, 1).astype(np.float32)
    large = 16 + (np.log(df / np.float32(16)) / np.float32(math.log(2048 / 16)) * np.float32(16)).astype(np.int32)
    large = np.minimum(large, 31)
    return np.where(dist < 16, dist, large)


def _bias_tables(rel_bias):
    rb = np.asarray(rel_bias, dtype=np.float32)
    ki = np.arange(128)[:, None]
    qc = np.arange(256)[None, :]
    dist = np.where(qc < 128, qc - ki, 128 + (qc - 128) - ki)
    valid = (dist >= 0) & (dist <= 128)
    distc = np.clip(dist, 0, 128)
    btab = np.empty((8, 128, 2, 3, 256), np.float32)
    for bi, d in enumerate((1, 4, 16)):
        bucket = _t5_bucket(d * distc)
        for h in range(16):
            vals = rb[bucket, h]
            btab[h // 2, :, h % 2, bi, :] = np.where(valid, vals, np.float32(NEG))
    btab = btab.reshape(8, 128, 2 * 3 * 256)
    sb = np.full((12, 128, 8, 2, 4, 4), NEG, np.float32)
    i = np.arange(128)
    for h in range(16):
        hp, hh = h // 2, h % 2
        for t in range(4):
            dA = 128 + t - i
            sb[0, :, hp, hh, :, t] = np.where(i >= t, rb[_t5_bucket(np.clip(dA, 0, 128)), h], np.float32(NEG))[:, None]
            sb[1 + t, :, hp, hh, :, t] = rb[_t5_bucket(4 * (128 - i)), h][:, None]
            sb[5 + t, :, hp, hh, :, t] = rb[_t5_bucket(16 * (128 - i)), h][:, None]
            for bq in range(4):
                for tp in range(t + 1):
                    sb[9, 4 * bq + tp, hp, hh, bq, t] = rb[_t5_bucket(np.array(t - tp)), h]
                sb[10, 4 * bq + t, hp, hh, bq, t] = rb[0, h]
                sb[11, 4 * bq + t, hp, hh, bq, t] = rb[0, h]
    sbias = np.ascontiguousarray(sb.reshape(12, 128, 256).transpose(1, 0, 2)).reshape(128, 12 * 256)
    return btab, sbias


def _selab():
    sel = np.zeros((128, 256), np.float32)
    for m in range(64):
        sel[m + 64, m] = 1.0
    for m in range(64, 128):
        sel[m - 64, 128 + m] = 1.0
    return sel


_NC_CACHE = {}


def kernel(x_prompt, x_sample, cache_k, cache_v, state_pool, rel_bias, norm_mix, w_in, w_pool, pool_scale,
           w_out, norm_ffn, w_gate, w_up, w_down, norm_final):
    f32 = lambda a: np.ascontiguousarray(np.asarray(a, dtype=np.float32))
    x_prompt, x_sample = f32(x_prompt), f32(x_sample)
    cache_k, cache_v, state_pool = f32(cache_k), f32(cache_v), f32(state_pool)
    btab, sbias = _bias_tables(rel_bias)
    shared = {
        "w_in": f32(w_in)[0], "w_pool": f32(w_pool)[0], "w_out": f32(w_out)[0], "w_gate": f32(w_gate)[0],
        "w_up": f32(w_up)[0], "w_down": f32(w_down)[0],
        "g_mix": np.ascontiguousarray(np.broadcast_to(f32(norm_mix).reshape(1, D), (128, D))),
        "g_ffn": np.ascontiguousarray(np.broadcast_to(f32(norm_ffn).reshape(1, D), (128, D))),
        "g_fin": np.ascontiguousarray(np.broadcast_to(f32(norm_final).reshape(1, D), (128, D))),
        "pscale": np.ascontiguousarray(f32(pool_scale).reshape(8, 128).T),
        "btab": btab, "sbias": sbias,
        "ident": np.eye(128, dtype=np.float32),
        "selab": _selab(),
        "invc": np.ascontiguousarray(np.broadcast_to((1.0 / np.arange(1, 17, dtype=np.float32))[None, :], (128, 16))),
    }
    in_maps = []
    for c in range(NCORES):
        m = dict(shared)
        m["xp"] = x_prompt[c]
        m["xs"] = np.ascontiguousarray(x_sample[4 * c:4 * c + 4].reshape(NS, D))
        m["ck"] = np.ascontiguousarray(cache_k[0, 4 * c:4 * c + 4].reshape(4, 2048, 1024))
        m["cv"] = np.ascontiguousarray(cache_v[0, 4 * c:4 * c + 4].reshape(4, 2048, 1024))
        m["spst"] = np.ascontiguousarray(state_pool[0, 4 * c:4 * c + 4])
        in_maps.append(m)
    if "nc" not in _NC_CACHE:
        _NC_CACHE["nc"] = build_program()
    nc = _NC_CACHE["nc"]
    res = run_bass_kernel_spmd(nc, in_maps, core_ids=list(range(NCORES)))
    R = res.results
    y_prompt = np.stack([R[c]["yp"] for c in range(NCORES)], 0)
    y_sample = np.concatenate([R[c]["ys"].reshape(4, 4, D) for c in range(NCORES)], 0)
    prompt_k = np.stack([R[c]["pk"].reshape(S, 16, 64) for c in range(NCORES)], 0)[None]
    prompt_v = np.stack([R[c]["pv"].reshape(S, 16, 64) for c in range(NCORES)], 0)[None]
    prompt_pool = np.stack([R[c]["ppool"] for c in range(NCORES)], 0)[None]
    sample_k = np.concatenate([R[c]["sk"].reshape(4, 4, 16, 64) for c in range(NCORES)], 0)[None]
    sample_v = np.concatenate([R[c]["sv"].reshape(4, 4, 16, 64) for c in range(NCORES)], 0)[None]
    sample_pool = np.concatenate([R[c]["spool"] for c in range(NCORES)], 0)[None]
    return (y_prompt.astype(np.float32), y_sample.astype(np.float32), prompt_k.astype(np.float32),
            prompt_v.astype(np.float32), prompt_pool.astype(np.float32), sample_k.astype(np.float32),
            sample_v.astype(np.float32), sample_pool.astype(np.float32))
```

```python
import math
from contextlib import ExitStack

import numpy as np
import concourse.bass as bass
import concourse.mybir as mybir
from concourse.bass_utils import run_bass_kernel_spmd

F32 = mybir.dt.float32
BF16 = mybir.dt.bfloat16
AF = mybir.ActivationFunctionType
ALU = mybir.AluOpType

NCORES = 8
D = 2048
S = 2048
NS = 16
NT = S + NS
DFF = 5632
NJ = DFF // 128
EPS = 1e-6
NEG = -30000.0
SB_BASE = 16512
SB_END = 229376

ENGS = ("pe", "act", "dve", "pool", "sp")
SELF_SYNC = {"pe": False, "act": True, "dve": True, "pool": True, "sp": False}


def _is_psum_key(k):
    name = k[0] if isinstance(k, tuple) else k
    return name in ("PA", "PT", "PTb", "PS", "PO", "PON")


class Prog:
    NDMA = 8

    def __init__(self, nc, stack):
        self.nc = nc
        self.ops = {e: [] for e in ENGS}
        self.sem = {e: stack.enter_context(nc.semaphore("prog_" + e)) for e in ENGS}
        self.cnt = {e: 0 for e in ENGS}
        self.dsem = {e: [stack.enter_context(nc.semaphore("dma_%s_%d" % (e, i))) for i in range(self.NDMA)]
                     for e in ("sp", "pool", "act")}
        self.dcnt = {e: [0] * self.NDMA for e in self.dsem}
        self.dnext = {e: 0 for e in self.dsem}
        self.lastw = {}
        self.readers = {}
        self.waited = {e: {} for e in ENGS}
        self.out_tokens = []
        self.all_dma_tokens = {}
        self.total = 0
        self.maxops = None
        self.log = []

    def _deps(self, eng, reads, writes):
        toks = []
        for r in reads:
            w = self.lastw.get(r)
            if w is not None:
                toks.append(w)
            if _is_psum_key(r):
                toks.extend(self.readers.get(r, ()))
        for w_ in writes:
            w = self.lastw.get(w_)
            if w is not None:
                toks.append(w)
            toks.extend(self.readers.get(w_, ()))
        need = {}
        for (sname, sem, val, teng, is_dma) in toks:
            if (not is_dma) and teng == eng and not SELF_SYNC[eng]:
                continue
            if self.waited[eng].get(sname, 0) >= val:
                continue
            if need.get(sname, (None, 0))[1] < val:
                need[sname] = (sem, val)
        for sname, (sem, val) in need.items():
            self.waited[eng][sname] = val
        return list(need.values())

    def _record(self, tok, reads, writes):
        for r in reads:
            self.readers.setdefault(r, []).append(tok)
        for w in writes:
            self.lastw[w] = tok
            self.readers[w] = []

    def op(self, eng, fn, reads=(), writes=()):
        self.total += 1
        if self.maxops is not None and self.total > self.maxops:
            return None
        self.log.append((self.total, eng, "op", tuple(writes)))
        waits = self._deps(eng, reads, writes)
        self.cnt[eng] += 1
        val = self.cnt[eng]
        sem = self.sem[eng]

        def run(e, fn=fn, waits=waits, sem=sem):
            for (s, v) in waits:
                e.wait_ge(s, v)
            ins = fn(e)
            ins.then_inc(sem, 1)

        self.ops[eng].append(run)
        tok = ("prog_" + eng, sem, val, eng, False)
        self._record(tok, reads, writes)
        return tok

    def dma(self, eng, out, in_, reads=(), writes=(), is_output=False):
        self.total += 1
        if self.maxops is not None and self.total > self.maxops:
            return None
        self.log.append((self.total, eng, "dma", tuple(writes)))
        i = self.dnext[eng]
        self.dnext[eng] = (i + 1) % self.NDMA
        sem = self.dsem[eng][i]
        prev = self.dcnt[eng][i]
        self.dcnt[eng][i] = prev + 16
        val = prev + 16
        sname = "dma_%s_%d" % (eng, i)
        waits = self._deps(eng, reads, writes)
        if prev > 0 and self.waited[eng].get(sname, 0) < prev:
            waits.append((sem, prev))
            self.waited[eng][sname] = prev

        def run(e, waits=waits, sem=sem, out=out, in_=in_):
            for (s, v) in waits:
                e.wait_ge(s, v)
            e.dma_start(out=out, in_=in_).then_inc(sem, 16)

        self.ops[eng].append(run)
        tok = (sname, sem, val, eng, True)
        self._record(tok, reads, writes)
        self.all_dma_tokens[sname] = (sem, val)
        if is_output:
            self.out_tokens.append(tok)
        return tok

    def barrier(self):
        targets = {}
        for en in ENGS:
            if self.cnt[en] > 0:
                targets["prog_" + en] = (self.sem[en], self.cnt[en], en)
        for sname, (sem, val) in self.all_dma_tokens.items():
            targets[sname] = (sem, val, None)
        for en in ENGS:
            waits = []
            for sname, (sem, val, teng) in targets.items():
                if teng == en and not SELF_SYNC[en]:
                    continue
                if self.waited[en].get(sname, 0) >= val:
                    continue
                self.waited[en][sname] = val
                waits.append((sem, val))

            def run(e, waits=waits):
                for (s, v) in waits:
                    e.wait_ge(s, v)

            if waits:
                self.ops[en].append(run)

    def finish(self, block):
        self.maxops = None
        self.barrier()
        fin = {}
        for (sname, sem, val, teng, is_dma) in self.out_tokens:
            if fin.get(sname, (None, 0))[1] < val:
                fin[sname] = (sem, val)

        def run(e, fin=fin):
            for (s, v) in fin.values():
                e.wait_ge(s, v)

        self.ops["sp"].append(run)
        hmap = {"pe": block.tensor, "act": block.scalar, "dve": block.vector, "pool": block.gpsimd, "sp": block.sync}
        for en in ENGS:
            ops = self.ops[en]
            if not ops:
                continue

            def body(e, ops=ops):
                for f in ops:
                    f(e)

            hmap[en](body)


class Arena:
    def __init__(self, nc):
        self.nc = nc
        self.top = SB_BASE
        self.limit = SB_END
        self.n = 0

    def region(self, start, end):
        self.top = start
        self.limit = end

    def alloc(self, name, shape, dt):
        esz = 2 if dt == BF16 else 4
        nbytes = int(np.prod(shape[1:])) * esz
        off = (self.top + 31) // 32 * 32
        assert off + nbytes <= self.limit, ("SBUF overflow", name, off, nbytes, self.limit)
        self.top = off + nbytes
        self.n += 1
        return self.nc.alloc_sbuf_tensor_at("%s_%d" % (name, self.n), list(shape), dt, offset=off)

    def mark(self):
        return self.top

    def release(self, m):
        self.top = m


def ssl(start, count, step):
    return slice(start, start + (count - 1) * step + 1, step)


def _mm(P, calls, reads, writes):
    def fn(e, calls=calls):
        ins = None
        for c in calls:
            (o, l, r, s, t) = c[:5]
            if len(c) > 5 and c[5]:
                ins = e.matmul(o, lhsT=l, rhs=r, start=s, stop=t, skip_group_check=True)
            else:
                ins = e.matmul(o, lhsT=l, rhs=r, start=s, stop=t)
        return ins
    return P.op("pe", fn, reads, writes)


def _tr(P, calls, reads, writes):
    def fn(e, calls=calls):
        ins = None
        for (o, i, idn) in calls:
            ins = e.transpose(o, i, idn)
        return ins
    return P.op("pe", fn, reads, writes)


def _trf(P, calls, reads, writes):
    def fn(e, calls=calls):
        ins = None
        for (o, i, idn) in calls:
            ins = e.matmul(o, lhsT=i, rhs=idn, start=True, stop=True)
        return ins
    return P.op("pe", fn, reads, writes)


def build_program(stop_after=None):
    nc = bass.Bass("TRN2", target_bir_lowering=False)

    def din(name, shape):
        return nc.dram_tensor(name, list(shape), F32, kind="ExternalInput").ap()

    def dout(name, shape):
        return nc.dram_tensor(name, list(shape), F32, kind="ExternalOutput").ap()

    xp = din("xp", [S, D])
    xs = din("xs", [NS, D])
    ck = din("ck", [4, 2048, 1024])
    cv = din("cv", [4, 2048, 1024])
    spst_d = din("spst", [4, 15, 1024])
    w_in = din("w_in", [D, 4096])
    w_pool = din("w_pool", [4, 256, 256])
    w_out = din("w_out", [D, D])
    w_gate = din("w_gate", [D, DFF])
    w_up = din("w_up", [D, DFF])
    w_down = din("w_down", [DFF, D])
    g_mix = din("g_mix", [128, D])
    g_ffn = din("g_ffn", [128, D])
    g_fin = din("g_fin", [128, D])
    pscale = din("pscale", [128, 8])
    btab = din("btab", [8, 128, 2 * 3 * 256])
    sbias_d = din("sbias", [128, 12 * 256])
    ident_d = din("ident", [128, 128])
    invc_d = din("invc", [128, 16])
    selab_d = din("selab", [128, 256])

    yp = dout("yp", [S, D])
    ys = dout("ys", [NS, D])
    pk = dout("pk", [S, 1024])
    pv = dout("pv", [S, 1024])
    ppool = dout("ppool", [15, 1024])
    sk = dout("sk", [NS, 1024])
    sv = dout("sv", [NS, 1024])
    spool = dout("spool", [4, 15, 1024])

    with ExitStack() as st:
        P = Prog(nc, st)
        import os as _os
        if _os.environ.get("PROG_MAXOPS"):
            P.maxops = int(_os.environ["PROG_MAXOPS"])
        A = Arena(nc)
        psum = lambda name, shape, dt: st.enter_context(nc.psum_tensor(name, shape, dt))

        RING = 4
        ring = [A.alloc("ring", [128, 4096], BF16) for _ in range(RING)]
        v16 = lambda t: t[:, :].rearrange("p (k n) -> p k n", k=16)
        v8 = lambda t: t[:, :].rearrange("p (k n) -> p k n", k=8)
        ident_f = A.alloc("identf", [128, 128], F32)
        ident_b = A.alloc("identb", [128, 128], BF16)
        ones_b = A.alloc("onesb", [128, 128], BF16)
        invc = A.alloc("invc", [128, 16], F32)
        psc = A.alloc("psc", [128, 8], F32)
        selab = A.alloc("selab", [128, 256], F32)
        QTs = A.alloc("QTs", [128, 8, 2, NS], BF16)
        KTs = A.alloc("KTs", [128, 8, NS], BF16)
        VTs = A.alloc("VTs", [128, 8, NS], BF16)
        mixA = A.alloc("mixA", [128, 8, NT], BF16)
        m_after_mixA = A.mark()

        PA = [psum("PA%d" % i, [128, 512], F32) for i in range(2)]
        PT = psum("PT", [128, 4, 128], F32)
        PTb = psum("PTb", [128, 8, 128], BF16)
        PS = [psum("PS%d" % i, [128, 512], F32) for i in range(2)]
        PO = [psum("PO%d" % i, [128, 512], F32) for i in range(2)]

        wstate = {"next": 0, "seq": []}
        widx = {}

        def wreg(key, dstf, src):
            widx[key] = len(wstate["seq"])
            wstate["seq"].append((dstf, src))

        def wissue_upto(n):
            while wstate["next"] < min(n, len(wstate["seq"])):
                i = wstate["next"]
                dstf, src = wstate["seq"][i]
                sk_ = wstate.get("scr_keys", {}).get(i)
                P.dma("pool", dstf(ring[i % RING]), src, reads=([sk_] if sk_ is not None else []), writes=[("w", i % RING)])
                wstate["next"] += 1

        def wget(key):
            i = widx[key]
            assert i < wstate["next"], ("weight tile not issued", key)
            return ring[i % RING], ("w", i % RING)

        def wdone(key):
            wissue_upto(widx[key] + RING + 1)

        w_in_v = w_in.rearrange("(kc p) n -> p kc n", p=128)
        w_out_v = w_out.rearrange("(kc p) n -> p kc n", p=128)
        w_gate_v = w_gate.rearrange("(kc p) n -> p kc n", p=128)
        w_up_v = w_up.rearrange("(kc p) n -> p kc n", p=128)
        w_down_v = w_down.rearrange("(j p) n -> p j n", p=128)
        for hp in range(8):
            for g in range(3):
                wreg(("in", hp, g), (lambda t: v16(t)[:, :, 0:128]),
                     w_in_v[:, :, g * 1024 + hp * 128: g * 1024 + (hp + 1) * 128])
        for c in range(8):
            wreg(("u", c), (lambda t: v16(t)[:, :, 0:128]), w_in_v[:, :, 3072 + c * 128: 3072 + (c + 1) * 128])
        PARTS = [(0, 8), (8, 8), (16, 8), (24, 8), (32, 8), (40, 4)]
        NG = 4
        p2tiles = []
        for cb in range(4):
            for half in range(2):
                p2tiles.append((("out", cb, half), (lambda t: t.rearrange("p (k n) -> p k n", k=8)),
                                w_out_v[:, half * 8:(half + 1) * 8, cb * 512:(cb + 1) * 512], 4096))
        for pi, (j0, nj) in enumerate(PARTS):
            for q in range(nj // 2):
                c0 = (j0 + q * 2) * 128
                p2tiles.append((("gate", pi, q), (lambda t: t.rearrange("p (k n) -> p k n", k=16)), w_gate_v[:, :, c0:c0 + 256], 4096))
                p2tiles.append((("up", pi, q), (lambda t: t.rearrange("p (k n) -> p k n", k=16)), w_up_v[:, :, c0:c0 + 256], 4096))
            for cb in range(4):
                p2tiles.append((("down", pi, cb), (lambda t, nj=nj: t[:, 0:nj * 512].rearrange("p (k n) -> p k n", k=nj)),
                                w_down_v[:, j0:j0 + nj, cb * 512:(cb + 1) * 512], nj * 512))
        NP2 = len(p2tiles)
        wscr = nc.dram_tensor("wscr", [NP2, 128, 4096], BF16, kind="Internal").ap()
        for g in range(NG):
            for ti, (ks, vf, src, nval) in enumerate(p2tiles):
                wreg((ks[0], g) + ks[1:], (lambda t, nval=nval: t[:, 0:nval]), wscr[ti][:, 0:nval])
        wstate["scr_keys"] = {}
        for g in range(NG):
            for ti, (ks, vf, src, nval) in enumerate(p2tiles):
                wstate["scr_keys"][widx[(ks[0], g) + ks[1:]]] = ("scr", ti)
        pc_state = {"next": 0}

        def precast(n):
            while n > 0 and pc_state["next"] < NP2:
                ti = pc_state["next"]
                ks, vf, src, nval = p2tiles[ti]
                P.dma("pool", vf(wscr[ti]), src, writes=[("scr", ti)])
                pc_state["next"] += 1
                n -= 1

        with nc.Block() as block:
          for _once in (0,):
              P.dma("sp", ident_f[:], ident_d, writes=["identf"])
              P.dma("sp", invc[:], invc_d, writes=["invc"])
              P.dma("sp", psc[:], pscale, writes=["psc"])
              P.dma("sp", selab[:], selab_d, writes=["selab"])
              P.op("dve", lambda e: e.tensor_copy(out=ident_b[:], in_=ident_f[:]), reads=["identf"], writes=["identb"])
              P.op("pool", lambda e: e.memset(ones_b[:], 1.0), writes=["onesb"])
              wissue_upto(RING)
              P.op("pool", lambda e: e.memset(QTs[64:128, :, 0, :], 0.0), writes=["QTs0a"])
              P.op("pool", lambda e: e.memset(QTs[0:64, :, 1, :], 0.0), writes=["QTs0b"])

              HT_off = (A.mark() + 31) // 32 * 32
              HT = A.alloc("HT", [128, 16, NT], BF16)
              m_after_HT = A.mark()
              xst = [A.alloc("xst", [128, D], F32) for _ in range(2)]
              hbt = [A.alloc("hbt", [128, D], BF16) for _ in range(2)]
              junk = A.alloc("junk", [128, D], BF16)
              gmix_b = A.alloc("gmixb", [128, D], F32)
              ssq = A.alloc("ssq", [128, 20], F32)
              rstd = A.alloc("rstd", [128, 20], F32)
              P.dma("sp", gmix_b[:], g_mix, writes=["gmixb"])
              P.op("pool", lambda e: e.memset(ssq[:], 0.0), writes=[("ssq", t) for t in range(17)])
              xst.append(A.alloc("xst", [128, D], F32))

              def p1a_stage_a(t):
                  n = 128 if t < 16 else NS
                  b = t % 3
                  src = xp[t * 128:(t + 1) * 128, :] if t < 16 else xs[:, :]
                  P.dma("sp", xst[b][0:n, :], src, writes=[("xst", b)])
                  P.op("act", lambda e, b=b, n=n, t=t: e.activation(out=junk[0:n, :], in_=xst[b][0:n, :], func=AF.Square,
                                                                    accum_out=ssq[0:n, t:t + 1]),
                       reads=[("xst", b)], writes=["junk", ("ssq", t)])
                  P.op("act", lambda e, n=n, t=t: e.activation(out=rstd[0:n, t:t + 1], in_=ssq[0:n, t:t + 1], func=AF.Sqrt,
                                                               scale=1.0 / D, bias=EPS),
                       reads=[("ssq", t)], writes=[("rstd", t)])

              def p1a_stage_b(t):
                  n = 128 if t < 16 else NS
                  b = t % 3
                  hb_ = t % 2
                  P.op("dve", lambda e, n=n, t=t: e.reciprocal(out=rstd[0:n, t:t + 1], in_=rstd[0:n, t:t + 1]),
                       reads=[("rstd", t)], writes=[("rstd", t)])
                  P.op("dve", lambda e, b=b, hb_=hb_, n=n, t=t: e.scalar_tensor_tensor(
                      out=hbt[hb_][0:n, :], in0=xst[b][0:n, :], scalar=rstd[0:n, t:t + 1], in1=gmix_b[0:n, :],
                      op0=ALU.mult, op1=ALU.mult), reads=[("xst", b), ("rstd", t), "gmixb"], writes=[("hbt", hb_)])
                  for half in range(2):
                      calls = [(PTb[:, k, 0:n], hbt[hb_][0:n, (half * 8 + k) * 128:(half * 8 + k + 1) * 128], ident_b[0:n, 0:n])
                               for k in range(8)]
                      _tr(P, calls, reads=[("hbt", hb_), "identb"], writes=["PTb"])
                      c0 = t * 128 if t < 16 else S
                      if half == 0:
                          P.op("act", lambda e, half=half, c0=c0, n=n: e.copy(out=HT[:, half * 8:(half + 1) * 8, c0:c0 + n],
                                                                              in_=PTb[:, :, 0:n]),
                               reads=["PTb"], writes=[("HT", t)])
                      else:
                          P.op("dve", lambda e, half=half, c0=c0, n=n: e.tensor_copy(out=HT[:, half * 8:(half + 1) * 8, c0:c0 + n],
                                                                                     in_=PTb[:, :, 0:n]),
                               reads=["PTb"], writes=[("HT", t)])

              p1a_stage_a(0)
              for t in range(17):
                  if t + 1 < 17:
                      p1a_stage_a(t + 1)
                  p1a_stage_b(t)
              P.barrier()
              precast(6)
              if stop_after == "p1a":
                  break
              A.release(m_after_HT)
              HT_ALL = [("HT", t) for t in range(17)]

              TGS = [(i * 512, 512) for i in range(4)] + [(S, NS)]

              QTz = A.alloc("QTz", [128, 2, NT], BF16)
              P.op("pool", lambda e: e.memset(QTz[64:128, 0, :], 0.0), writes=["QTz0a"])
              P.op("pool", lambda e: e.memset(QTz[0:64, 1, :], 0.0), writes=["QTz0b"])
              KT = A.alloc("KT", [128, NT], BF16)
              VT = A.alloc("VT", [128, NT], BF16)
              V1 = A.alloc("V1", [128, 16, 192], BF16)
              VX = [A.alloc("VX", [128, 16, 192], BF16) for _ in range(2)]
              for vi, vb in enumerate((V1, VX[0], VX[1])):
                  P.op("pool", lambda e, vb=vb: e.memset(vb[:, :, 64:128], 1.0), writes=[("Vones", vi)])
              rdt = [A.alloc("rdt", [128, 512], F32) for _ in range(2)]
              ftmp = [A.alloc("ftmp", [128, 512], F32) for _ in range(2)]
              kvst = [A.alloc("kvst", [128, 4, 128], F32) for _ in range(2)]
              ACC = A.alloc("ACC", [128, 2, S], F32)
              BT = [A.alloc("BT", [128, 2, 3, 256], BF16) for _ in range(2)]
              PE_ = [A.alloc("Pexp", [128, 512], BF16) for _ in range(4)]
              PSB = [PS[0], PS[1], PT[:, :, :].rearrange("p a b -> p (a b)")]
              PSK = [("PS", 0), ("PS", 1), "PT"]
              kvs_tmp = A.alloc("kvstmp", [128, 2, NS], F32)
              kvss = [A.alloc("kvss", [NS, 128], F32) for _ in range(4)]
              kvss_i = [0]

              ftmp_i = [0]
              kvst_i = [0]

              for hp in range(8):
                  bt = BT[hp % 2]
                  P.dma("pool", bt[:], btab[hp].rearrange("p (h b q) -> p h b q", h=2, b=3), writes=[("BT", hp % 2)])
                  pa_i = 0
                  for gi, gname in enumerate(("Q", "K", "V")):
                      slot_t, wkey = wget(("in", hp, gi))
                      slot = v16(slot_t)
                      for tgi, (c0, n) in enumerate(TGS):
                          pa = PA[pa_i % 2]
                          pkey = ("PA", pa_i % 2)
                          pa_i += 1
                          calls = [(pa[:, 0:n], slot[:, kc, 0:128], HT[:, kc, c0:c0 + n], kc == 0, kc == 15)
                                   for kc in range(16)]
                          _mm(P, calls, reads=[wkey] + HT_ALL, writes=[pkey])
                          if tgi == len(TGS) - 1:
                              wdone(("in", hp, gi))
                          if gname == "Q":
                              if tgi < 4:
                                  P.op("act", lambda e, pa=pa, c0=c0, n=n: e.mul(out=QTz[0:64, 0, c0:c0 + n], in_=pa[0:64, 0:n], mul=0.125),
                                       reads=[pkey, "QTz0a", "QTz0b"], writes=[("QT", tgi)])
                                  P.op("act", lambda e, pa=pa, c0=c0, n=n: e.mul(out=QTz[64:128, 1, c0:c0 + n], in_=pa[64:128, 0:n],
                                                                                 mul=0.125),
                                       reads=[pkey, "QTz0a", "QTz0b"], writes=[("QT", tgi)])
                              else:
                                  P.op("act", lambda e, pa=pa, hp=hp: e.mul(out=QTs[0:64, hp, 0, :], in_=pa[0:64, 0:NS], mul=0.125),
                                       reads=[pkey, "QTs0a", "QTs0b"], writes=[("QTs", hp)])
                                  P.op("act", lambda e, pa=pa, hp=hp: e.mul(out=QTs[64:128, hp, 1, :], in_=pa[64:128, 0:NS], mul=0.125),
                                       reads=[pkey, "QTs0a", "QTs0b"], writes=[("QTs", hp)])
                              continue
                          Tb = KT if gname == "K" else VT
                          Ts = KTs if gname == "K" else VTs
                          outd = pk if gname == "K" else pv
                          if tgi < 4:
                              f = ftmp[ftmp_i[0] % 2]
                              fkey = ("ftmp", ftmp_i[0] % 2)
                              ftmp_i[0] += 1
                              P.op("dve", lambda e, Tb=Tb, pa=pa, c0=c0, n=n: e.tensor_copy(out=Tb[:, c0:c0 + n], in_=pa[:, 0:n]),
                                   reads=[pkey], writes=[(gname + "T", tgi)])
                              P.op("act", lambda e, f=f, pa=pa, n=n: e.mul(out=f[:, 0:n], in_=pa[:, 0:n], mul=1.0),
                                   reads=[pkey, (gname + "T", tgi)], writes=[fkey])
                              calls = [(PT[:, i, :], f[:, i * 128:(i + 1) * 128], ident_f[:, :]) for i in range(4)]
                              _trf(P, calls, reads=[fkey, "identf"], writes=["PT"])
                              stg = kvst[kvst_i[0] % 2]
                              skey = ("kvst", kvst_i[0] % 2)
                              kvst_i[0] += 1
                              P.op("dve", lambda e, stg=stg: e.tensor_copy(out=stg[:], in_=PT[:]), reads=["PT"], writes=[skey])
                              if gname == "V":
                                  P.op("act", lambda e, tgi=tgi: e.mul(
                                      out=V1[:, tgi * 4:(tgi + 1) * 4, :].rearrange("p t (a c) -> p t a c", a=3)[:, :, 0:3:2, :],
                                      in_=PT[:].rearrange("p t (a c) -> p t a c", a=2), mul=1.0),
                                       reads=["PT", ("Vones", 0)], writes=[("V1", tgi)])
                              dview = outd[tgi * 512:(tgi + 1) * 512, hp * 128:(hp + 1) * 128].rearrange("(t p) c -> p t c", p=128)
                              P.dma("sp", dview, stg[:], reads=[skey], is_output=True)
                          else:
                              stg_s = kvss[kvss_i[0] % 4]
                              sskey = ("kvss", kvss_i[0] % 4)
                              kvss_i[0] += 1
                              outs = sk if gname == "K" else sv
                              gsel = 0 if gname == "K" else 1
                              P.op("dve", lambda e, Ts=Ts, pa=pa, hp=hp: e.tensor_copy(out=Ts[:, hp, :], in_=pa[:, 0:NS]),
                                   reads=[pkey], writes=[(gname + "Ts", hp)])
                              P.op("act", lambda e, pa=pa, gsel=gsel: e.mul(out=kvs_tmp[:, gsel, :], in_=pa[:, 0:NS], mul=1.0),
                                   reads=[pkey], writes=[("kvstmp", gsel)])
                              _trf(P, [(PT[0:NS, 0, :], kvs_tmp[:, gsel, :], ident_f[:, :])], reads=[("kvstmp", gsel), "identf"],
                                  writes=["PT"])
                              P.op("dve", lambda e, stg_s=stg_s: e.tensor_copy(out=stg_s[:, :], in_=PT[0:NS, 0, :]),
                                   reads=["PT"], writes=[sskey])
                              P.dma("sp", outs[:, hp * 128:(hp + 1) * 128], stg_s[:, :], reads=[sskey], is_output=True)

                  if stop_after == "p1b:proj":
                      break
                  QK_R = [("QT", i) for i in range(4)] + [("KT", i) for i in range(4)]
                  tiles = []
                  POB = [PO[0], PO[1], PA[0], PA[1]]
                  POK = [("PO", 0), ("PO", 1), ("PA", 0), ("PA", 1)]
                  for bi, d in enumerate((1, 4, 16)):
                      nb = 16 // d
                      for r in range(d):
                          for j in range(nb):
                              tiles.append((bi, d, nb, r, j))

                  def build_vx(bi, d, buf):
                      nb = 16 // d
                      for q4 in range(2):
                          calls = []
                          for k in range(8):
                              ti = q4 * 8 + k
                              r, j = ti // nb, ti % nb
                              st0 = r + d * 128 * j
                              calls.append((PTb[:, k, :], VT[:, ssl(st0, 128, d)], ident_b[:, :]))
                          _tr(P, calls, reads=[("VT", i) for i in range(4)] + ["identb"], writes=["PTb"])
                          P.op("dve", lambda e, buf=buf, q4=q4: e.tensor_copy(
                              out=VX[buf][:, q4 * 8:(q4 + 1) * 8, :].rearrange("p t (a c) -> p t a c", a=3)[:, :, 0:3:2, :],
                              in_=PTb[:].rearrange("p t (a c) -> p t a c", a=2)),
                               reads=["PTb", ("Vones", buf + 1)], writes=[("VX", buf, q4)])

                  build_vx(1, 4, 0)
                  build_vx(2, 16, 1)
                  if stop_after == "p1b:vx":
                      break

                  def emit_qk(k):
                      bi, d, nb, r, j = tiles[k]
                      nq = 2 if j + 1 < nb else 1
                      W = 128 * nq
                      st0 = r + d * 128 * j
                      psv = PSB[k % 3][:, 0:2 * W].rearrange("p (h w) -> p h w", h=2)
                      calls = [(psv, ident_b[:, :], bt[:, :, bi, 0:W], True, False),
                               (psv, KT[:, ssl(st0, 128, d)], QTz[:, :, ssl(st0, W, d)], False, True)]
                      _mm(P, calls, reads=QK_R + [("BT", hp % 2), "identb"], writes=[PSK[k % 3]])
                      P.op("act", lambda e, psv=psv, W=W, k=k: e.activation(
                          out=PE_[k % 4][:, 0:2 * W].rearrange("p (h w) -> p h w", h=2), in_=psv, func=AF.Exp),
                           reads=[PSK[k % 3]], writes=[("Pexp", k % 4)])

                  def emit_pv(k):
                      bi, d, nb, r, j = tiles[k]
                      nq = 2 if j + 1 < nb else 1
                      ti = r * nb + j
                      st0 = r + d * 128 * j
                      for hh in range(2):
                          if bi == 0:
                              vt = V1[:, ti, hh * 64:hh * 64 + 128]
                              vkeys = [("V1", i) for i in range(4)] + [("Vones", 0)]
                          else:
                              vt = VX[bi - 1][:, ti, hh * 64:hh * 64 + 128]
                              vkeys = [("VX", bi - 1, 0), ("VX", bi - 1, 1), ("Vones", bi)]
                          pe_ = PE_[k % 4][:, 0:256 * nq].rearrange("p (h w) -> p h w", h=2)
                          cb_ = 2 * hh + (k % 2)
                          nb_ = 2 * hh + ((k + 1) % 2)
                          cur, nxt = POB[cb_], POB[nb_]
                          curk, nxtk = POK[cb_], POK[nb_]
                          calls = [(cur[:, 0:128], vt, pe_[:, hh, 0:128], j == 0, True, True)]
                          wr = [curk]
                          if nq == 2:
                              calls += [(nxt[:, 0:128], vt, pe_[:, hh, 128:256], True, False, True)]
                              wr.append(nxtk)
                          _mm(P, calls, reads=[("Pexp", k % 4)] + vkeys, writes=wr)
                          dst = ACC[:, hh, ssl(st0, 128, d)]
                          srcv = cur[:, 0:128]
                          if d == 1:
                              akeys = [("ACC", hh, j // 4)]
                          elif d == 4:
                              akeys = [("ACC", hh, j)]
                          else:
                              akeys = [("ACC", hh, i) for i in range(4)]
                          if bi == 0 and hh == 1:
                              P.op("act", lambda e, dst=dst, srcv=srcv: e.mul(out=dst, in_=srcv, mul=1.0),
                                   reads=[curk], writes=akeys)
                          elif bi == 0:
                              P.op("dve", lambda e, dst=dst, srcv=srcv: e.tensor_copy(out=dst, in_=srcv),
                                   reads=[curk], writes=akeys)
                          else:
                              P.op("dve", lambda e, dst=dst, srcv=srcv: e.tensor_tensor(out=dst, in0=dst, in1=srcv, op=ALU.add),
                                   reads=[curk] + akeys, writes=akeys)

                  LOOK = 3
                  for k in range(min(LOOK, len(tiles))):
                      emit_qk(k)
                  for k in range(len(tiles)):
                      emit_pv(k)
                      if k + LOOK < len(tiles):
                          emit_qk(k + LOOK)
                  if stop_after == "p1b:att":
                      break
                  precast(10)
                  for tg4 in range(4):
                      cs = slice(tg4 * 512, (tg4 + 1) * 512)
                      AKg = [("ACC", 0, tg4), ("ACC", 1, tg4)]
                      pf = POB[2 + tg4 % 2]
                      pfk = POK[2 + tg4 % 2]
                      _mm(P, [(pf[:, 0:512], selab[:, 0:128], ACC[:, 0, cs], True, False),
                              (pf[:, 0:512], selab[:, 128:256], ACC[:, 1, cs], False, True)], reads=AKg + ["selab"], writes=[pfk])
                      rd = rdt[tg4 % 2]
                      P.op("dve", lambda e, rd=rd, pf=pf: e.reciprocal(out=rd[:, :], in_=pf[:, 0:512]), reads=[pfk],
                           writes=[("rdt", tg4 % 2)])
                      P.op("pool", lambda e, rd=rd, cs=cs, hp=hp: e.tensor_tensor(out=mixA[0:64, hp, cs], in0=ACC[0:64, 0, cs],
                                                                                 in1=rd[0:64, :], op=ALU.mult),
                           reads=AKg + [("rdt", tg4 % 2)], writes=[("mixA", hp, tg4, 0)])
                      P.op("pool", lambda e, rd=rd, cs=cs, hp=hp: e.tensor_tensor(out=mixA[64:128, hp, cs], in0=ACC[64:128, 1, cs],
                                                                                 in1=rd[64:128, :], op=ALU.mult),
                           reads=AKg + [("rdt", tg4 % 2)], writes=[("mixA", hp, tg4, 1)])

              precast(NP2)
              if stop_after is not None and stop_after.startswith("p1b:"):
                  break
              P.barrier()
              if stop_after == "p1b":
                  break
              A.release(m_after_HT)

              mixP = A.alloc("mixP", [128, 8, NT], BF16)
              m_after_mixP = A.mark()
              UT = A.alloc("UT", [128, 16 + S], F32)
              S1 = A.alloc("S1", [128, 16 + S], F32)
              S2 = A.alloc("S2", [128, 16 + S], F32)
              pooled = A.alloc("pooled", [128, 2, NT], BF16)
              wp = A.alloc("wp", [128, 4, 2, 256], BF16)
              spst = [A.alloc("spst", [64, 128], F32) for _ in range(2)]
              ucs = [A.alloc("ucs", [128, 4, 19], F32) for _ in range(3)]
              ptmp = A.alloc("ptmp", [128, 16], F32)
              ppst = [A.alloc("ppst", [16, 128], F32) for _ in range(2)]
              usst = [A.alloc("usst", [NS, 128], F32) for _ in range(2)]
              utail = A.alloc("utail", [128, 32], F32)
              P.dma("pool", wp[:], w_pool.rearrange("g (cc p) o -> p g cc o", p=128), writes=["wp"])
              P.op("pool", lambda e: e.memset(UT[:, 0:16], 0.0), writes=["UTpad"])
              P.op("pool", lambda e: e.memset(S1[:, 0:16], 0.0), writes=["S1pad"])
              P.op("pool", lambda e: e.memset(S2[:, 0:16], 0.0), writes=["S2pad"])
              for b in range(4):
                  P.dma("sp", spool[b, 0:11, :], spst_d[b, 4:15, :], is_output=True)
              pa_i = 0
              pa_state = [0]
              pending_pool = []
              PCB = [PA[0], PA[1], PO[0], PO[1]]
              PCK = [("PA", 0), ("PA", 1), ("PO", 0), ("PO", 1)]
              for c in range(8):
                  slot_t, wkey = wget(("u", c))
                  slot = v16(slot_t)
                  g = c // 2
                  w = (2, 4, 8, 16)[g]
                  for tgi, (c0, n) in enumerate(TGS):
                      pa = PCB[pa_i % 4]
                      pkey = PCK[pa_i % 4]
                      pa_i += 1
                      calls = [(pa[:, 0:n], slot[:, kc, 0:128], HT[:, kc, c0:c0 + n], kc == 0, kc == 15)
                               for kc in range(16)]
                      _mm(P, calls, reads=[wkey] + HT_ALL, writes=[pkey])
                      if tgi == len(TGS) - 1:
                          wdone(("u", c))
                      if tgi < 4:
                          P.op("act", lambda e, pa=pa, c0=c0: e.mul(out=UT[:, 16 + c0:16 + c0 + 512], in_=pa[:, :], mul=1.0),
                               reads=[pkey, "UTpad"], writes=[("UT", tgi)])
                      else:
                          P.op("act", lambda e, pa=pa: e.mul(out=ucs[0][:, :, 15:19],
                                                             in_=pa[:, 0:NS].rearrange("p (b t) -> p b t", b=4), mul=1.0),
                               reads=[pkey], writes=["ucs0n"])
                          P.op("dve", lambda e, pa=pa: e.tensor_copy(out=utail[:, 0:NS], in_=pa[:, 0:NS]),
                               reads=[pkey], writes=["utailS"])
                  pa_state[0] = pa_i
                  while pending_pool:
                      pending_pool.pop(0)()
                  pa_i = pa_state[0]
                  UTK = [("UT", i) for i in range(4)]
                  P.dma("sp", spst[c % 2][0:60, :], spst_d.rearrange("b r c -> (b r) c")[:, c * 128:(c + 1) * 128],
                        writes=[("spst", c % 2)])
                  _trf(P, [(PT[:, 0, 0:60], spst[c % 2][0:60, :], ident_f[0:60, 0:60])], reads=[("spst", c % 2), "identf"],
                      writes=["PT"])
                  P.op("dve", lambda e: e.tensor_copy(out=ucs[0][:, :, 0:15], in_=PT[:, 0, 0:60].rearrange("p (b r) -> p b r", b=4)),
                       reads=["PT"], writes=["ucs0s"])
                  P.op("dve", lambda e: e.tensor_copy(out=utail[:, 16:31], in_=UT[:, 16 + S - 15:16 + S]),
                       reads=UTK, writes=["utailP"])
                  _trf(P, [(PT[0:15, 1, :], utail[:, 16:31], ident_f[:, :]), (PT[0:NS, 2, :], utail[:, 0:NS], ident_f[:, :])],
                      reads=["utailP", "utailS", "identf"], writes=["PT"])
                  P.op("dve", lambda e, c=c: e.tensor_copy(out=ppst[c % 2][0:15, :], in_=PT[0:15, 1, :]),
                       reads=["PT"], writes=[("ppst", c % 2)])
                  P.op("dve", lambda e, c=c: e.tensor_copy(out=usst[c % 2][0:NS, :], in_=PT[0:NS, 2, :]),
                       reads=["PT"], writes=[("usst", c % 2)])
                  P.dma("sp", ppool[:, c * 128:(c + 1) * 128], ppst[c % 2][0:15, :], reads=[("ppst", c % 2)], is_output=True)
                  for b in range(4):
                      P.dma("sp", spool[b, 11:15, c * 128:(c + 1) * 128], usst[c % 2][4 * b:4 * b + 4, :],
                            reads=[("usst", c % 2)], is_output=True)
                  bufs = [UT, S1, S2]
                  cur_i = 0
                  steps = int(math.log2(w))
                  src_keys = UTK + ["UTpad"]
                  for si in range(steps):
                      sh = 1 << si
                      dst_i = 1 if cur_i != 1 else 2
                      srcb, dstb = bufs[cur_i], bufs[dst_i]
                      dkey = "S%d" % dst_i
                      P.op("pool", lambda e, srcb=srcb, dstb=dstb, sh=sh: e.tensor_tensor(
                          out=dstb[:, 16:16 + S], in0=srcb[:, 16:16 + S], in1=srcb[:, 16 - sh:16 + S - sh], op=ALU.add),
                          reads=src_keys + [dkey + "pad"], writes=[dkey])
                      cur_i = dst_i
                      src_keys = [dkey, dkey + "pad"]
                  sw = bufs[cur_i]
                  P.op("dve", lambda e, sw=sw, w=w, c=c: e.scalar_tensor_tensor(
                      out=pooled[:, c % 2, 0:S], in0=sw[:, 16:16 + S], scalar=1.0 / w, in1=UT[:, 16:16 + S],
                      op0=ALU.mult, op1=ALU.subtract), reads=src_keys + UTK, writes=[("pooled", c % 2)])
                  P.op("dve", lambda e, sw=sw, w=w: e.tensor_tensor(out=ptmp[:, 0:w - 1], in0=sw[:, 16:16 + w - 1],
                                                                    in1=invc[:, 0:w - 1], op=ALU.mult),
                       reads=src_keys + ["invc"], writes=["ptmp"])
                  P.op("dve", lambda e, w=w, c=c: e.tensor_tensor(out=pooled[:, c % 2, 0:w - 1], in0=ptmp[:, 0:w - 1],
                                                                  in1=UT[:, 16:16 + w - 1], op=ALU.subtract),
                       reads=["ptmp"] + UTK + [("pooled", c % 2)], writes=[("pooled", c % 2)])
                  cur = 0
                  lo = 0
                  rk = ["ucs0n", "ucs0s"]
                  for si in range(steps):
                      sh = 1 << si
                      dst = 1 if cur != 1 else 2
                      nlo = lo + sh
                      wk = "ucs%d" % dst
                      P.op("dve", lambda e, cur=cur, dst=dst, nlo=nlo, sh=sh: e.tensor_tensor(
                          out=ucs[dst][:, :, nlo:19], in0=ucs[cur][:, :, nlo:19], in1=ucs[cur][:, :, nlo - sh:19 - sh], op=ALU.add),
                          reads=rk, writes=[wk])
                      cur, lo, rk = dst, nlo, [wk]
                  P.op("dve", lambda e, cur=cur, w=w, c=c: e.scalar_tensor_tensor(
                      out=pooled[:, c % 2, S:NT].rearrange("p (b t) -> p b t", b=4), in0=ucs[cur][:, :, 15:19], scalar=1.0 / w,
                      in1=ucs[0][:, :, 15:19], op0=ALU.mult, op1=ALU.subtract),
                      reads=rk + ["ucs0n", "ucs0s", ("pooled", c % 2)], writes=[("pooled", c % 2)])
                  if c % 2 == 1:
                      def poolmm(g=g):
                          nonlocal_pa = pa_state
                          for co in range(2):
                              for tgi, (c0, n) in enumerate(TGS):
                                  pa = PCB[nonlocal_pa[0] % 4]
                                  pkey = PCK[nonlocal_pa[0] % 4]
                                  nonlocal_pa[0] += 1
                                  calls = [(pa[:, 0:n], wp[:, g, ci, co * 128:(co + 1) * 128], pooled[:, ci, c0:c0 + n], ci == 0, ci == 1)
                                           for ci in range(2)]
                                  _mm(P, calls, reads=["wp", ("pooled", 0), ("pooled", 1)], writes=[pkey])
                                  ch = 2 * g + co
                                  P.op("dve", lambda e, pa=pa, ch=ch, c0=c0, n=n: e.tensor_scalar_mul(
                                      out=mixP[:, ch, c0:c0 + n], in0=pa[:, 0:n], scalar1=psc[:, ch:ch + 1]),
                                      reads=[pkey, "psc"], writes=[("mixP", ch)])
                      pending_pool.append(poolmm)
              pa_state[0] = pa_i
              while pending_pool:
                  pending_pool.pop(0)()
              pa_i = pa_state[0]
              P.barrier()
              if stop_after == "p1c":
                  break
              R2_start = m_after_mixP
              A.region(HT_off, m_after_HT)

              sbias = A.alloc("sbias", [128, 12, 256], BF16)
              P.dma("pool", sbias[:], sbias_d.rearrange("p (t c) -> p t c", t=12), writes=["sbias"])
              kc_t = [A.alloc("kct", [128, 1024], BF16) for _ in range(4)]
              vc_t = [A.alloc("vct", [128, 1024], BF16) for _ in range(4)]
              kT = [A.alloc("kT", [128, 8, 128], BF16) for _ in range(2)]
              Psm = [A.alloc("Psm", [128, 256], BF16) for _ in range(2)]
              vnew = A.alloc("vnew", [NS, 8, 128], BF16)
              srd = A.alloc("srd", [128, 256], F32)
              PON, POD = PO[0], PO[1]
              _tr(P, [(PTb[0:NS, hp, :], VTs[:, hp, :], ident_b[:, :]) for hp in range(8)],
                  reads=[("VTs", hp) for hp in range(8)] + ["identb"], writes=["PTb"])
              P.op("dve", lambda e: e.tensor_copy(out=vnew[:], in_=PTb[0:NS, :, :]), reads=["PTb"], writes=["vnew"])
              QTS_K = [("QTs", hp) for hp in range(8)]
              kT.append(A.alloc("kT", [128, 8, 128], BF16))
              ttypes = [("A", 0)] + [("B", t) for t in range(4)] + [("C", t) for t in range(4)] + [("N", i) for i in range(3)]
              tl = []
              ld = 0
              for b in range(4):
                  for ti, (kind, t) in enumerate(ttypes):
                      d_ = {"b": b, "ti": ti, "kind": kind, "t": t, "first": ti == 0, "last": ti == len(ttypes) - 1, "k": len(tl)}
                      if kind != "N":
                          d_["sl"] = ld % 4
                          ld += 1
                      tl.append(d_)

              def s1(k):
                  d_ = tl[k]
                  if d_["kind"] == "N":
                      return
                  b, t, sl = d_["b"], d_["t"], d_["sl"]
                  if d_["kind"] == "A":
                      rows = slice(1920, 2048)
                  elif d_["kind"] == "B":
                      rows = slice(1536 + t, 2048, 4)
                  else:
                      rows = slice(t, 2048, 16)
                  P.dma("pool", kc_t[sl][:], ck[b, rows, :], writes=[("kct", sl)])
                  P.dma("pool", vc_t[sl][:], cv[b, rows, :], writes=[("vct", sl)])
                  kt = kT[k % 3]
                  _tr(P, [(PTb[:, hp, :], kc_t[sl][:, hp * 128:(hp + 1) * 128], ident_b[:, :]) for hp in range(8)],
                      reads=[("kct", sl), "identb"], writes=["PTb"])
                  P.op("dve", lambda e, kt=kt: e.tensor_copy(out=kt[:], in_=PTb[:]), reads=["PTb"], writes=[("kT", k % 3)])

              def s2(k):
                  d_ = tl[k]
                  ti = d_["ti"]
                  psb = PS[k % 2]
                  pskey = ("PS", k % 2)
                  psm = Psm[k % 2]
                  pmkey = ("Psm", k % 2)
                  if d_["kind"] != "N":
                      nk = 128
                      kt = kT[k % 3]
                      calls = [(psb[0:nk, 0:256], ident_b[:, :], sbias[:, ti, :], True, False, True)]
                      for hp in range(8):
                          calls.append((psb[0:nk, hp * 32:(hp + 1) * 32], kt[:, hp, :],
                                        QTs[:, hp, :, :].rearrange("p h q -> p (h q)"), False, hp == 7, True))
                      _mm(P, calls, reads=[("kT", k % 3), "sbias", "identb"] + QTS_K, writes=[pskey])
                  else:
                      nk = NS
                      calls = [(psb[0:nk, 0:256], ident_b[0:NS, 0:NS], sbias[0:NS, ti, :], True, False, True)]
                      for hp in range(8):
                          calls.append((psb[0:nk, hp * 32:(hp + 1) * 32], KTs[:, hp, :],
                                        QTs[:, hp, :, :].rearrange("p h q -> p (h q)"), False, hp == 7, True))
                      _mm(P, calls, reads=["sbias", "identb"] + QTS_K + [("KTs", hp) for hp in range(8)], writes=[pskey])
                  P.op("act", lambda e, psm=psm, psb=psb, nk=nk: e.activation(out=psm[0:nk, :], in_=psb[0:nk, 0:256], func=AF.Exp),
                       reads=[pskey], writes=[pmkey])

              def s3(k):
                  d_ = tl[k]
                  b, first, last = d_["b"], d_["first"], d_["last"]
                  psm = Psm[k % 2]
                  pmkey = ("Psm", k % 2)
                  if d_["kind"] != "N":
                      nk = 128
                      sl = d_["sl"]
                      vsrc = lambda hp, sl=sl: vc_t[sl][:, hp * 128:(hp + 1) * 128]
                      vkeys = [("vct", sl)]
                  else:
                      nk = NS
                      vsrc = lambda hp: vnew[0:NS, hp, :]
                      vkeys = ["vnew"]
                  calls = [(PON[:, hp * 32:(hp + 1) * 32], vsrc(hp), psm[0:nk, hp * 32:(hp + 1) * 32], first and hp == 0, last, True)
                           for hp in range(8)]
                  calls.append((POD[:, 0:256], ones_b[0:nk, :], psm[0:nk, :], first, last, True))
                  _mm(P, calls, reads=[pmkey, "onesb"] + vkeys, writes=["PON"])
                  if last:
                      P.op("dve", lambda e: e.reciprocal(out=srd[:], in_=POD[:, 0:256]), reads=["PON"], writes=["srd"])
                      for hh in range(2):
                          hs = slice(hh * 64, (hh + 1) * 64)
                          cs = slice(hh * NS + 4 * b, hh * NS + 4 * b + 4)
                          P.op("dve", lambda e, hs=hs, hh=hh, cs=cs, b=b: e.tensor_tensor(
                              out=mixA[hs, :, S + 4 * b:S + 4 * b + 4],
                              in0=PON[hs, 0:256].rearrange("p (h c) -> p h c", h=8)[:, :, cs],
                              in1=srd[hs, 0:256].rearrange("p (h c) -> p h c", h=8)[:, :, cs], op=ALU.mult),
                              reads=["PON", "srd"], writes=[("mixAs", b, hh)])

              NTL = len(tl)
              s1(0)
              s1(1)
              s2(0)
              for k in range(NTL):
                  if k + 2 < NTL:
                      s1(k + 2)
                  if k + 1 < NTL:
                      s2(k + 1)
                  s3(k)
              P.barrier()
              if stop_after == "p1d":
                  break
              A.region(HT_off, m_after_HT)

              NW = 528
              NJP = 8
              xg = A.alloc("xg", [128, 5, D], F32)
              H2T = A.alloc("H2T", [128, 16, NW], BF16)
              A.region(R2_start, SB_END)
              aT = A.alloc("aT", [128, NJP, NW], BF16)
              gffn_b = A.alloc("gffnb", [128, D], F32)
              gfin_b = A.alloc("gfinb", [128, D], F32)
              h2sb = [A.alloc("h2s", [128, D], BF16) for _ in range(2)]
              sgt = [A.alloc("sgt", [128, NW], F32) for _ in range(2)]
              ssq2 = A.alloc("ssq2", [128, 64], F32)
              rstd2 = A.alloc("rstd2", [128, 64], F32)
              junk2 = A.alloc("junk2", [128, D], BF16)
              P.dma("sp", gffn_b[:], g_ffn, writes=["gffnb"])
              P.dma("sp", gfin_b[:], g_fin, writes=["gfinb"])
              P.op("pool", lambda e: e.memset(ssq2[:], 0.0), writes=[("ssq2", i) for i in range(64)])
              PG, PU, PD = PA, PS, PO
              MIXK = [("mixA", hp, t4, h2) for hp in range(8) for t4 in range(4) for h2 in range(2)] + [("mixAs", b, hh) for b in range(4) for hh in range(2)] + \
                     [("mixP", ch) for ch in range(8)]
              stat_i = 0
              pd_i = 0
              gi_ = 0
              for g in range(NG):
                  subs = [(g * 512 + i * 128, 128, xp[g * 512 + i * 128:g * 512 + (i + 1) * 128, :],
                           yp[g * 512 + i * 128:g * 512 + (i + 1) * 128, :]) for i in range(4)]
                  if g == NG - 1:
                      subs.append((S, NS, xs[:, :], ys[:, :]))
                  ncol = sum(s_[1] for s_ in subs)
                  for si, (c0, n, xsrc, ydst) in enumerate(subs):
                      P.dma("sp", xg[0:n, si, :], xsrc, writes=[("xg", si)])
                  for cb in range(4):
                      t0, k0 = wget(("out", g, cb, 0))
                      t1, k1 = wget(("out", g, cb, 1))
                      wv = (v8(t0), v8(t1))
                      for si, (c0, n, xsrc, ydst) in enumerate(subs):
                          pd = PD[pd_i % 2]
                          pdk = ("PO", pd_i % 2)
                          pd_i += 1
                          calls = []
                          for kc in range(16):
                              src = mixA if kc < 8 else mixP
                              calls.append((pd[0:n, :], src[:, kc % 8, c0:c0 + n], wv[kc // 8][:, kc % 8, :], kc == 0, kc == 15))
                          _mm(P, calls, reads=[k0, k1] + MIXK, writes=[pdk])
                          P.op("dve", lambda e, pd=pd, n=n, si=si, cb=cb: e.tensor_tensor(
                              out=xg[0:n, si, cb * 512:(cb + 1) * 512], in0=xg[0:n, si, cb * 512:(cb + 1) * 512], in1=pd[0:n, :],
                              op=ALU.add), reads=[pdk, ("xg", si)], writes=[("xg", si)])
                      wdone(("out", g, cb, 0))
                      wdone(("out", g, cb, 1))
                  cols_ = []
                  cacc_ = 0
                  for (c0, n, xsrc, ydst) in subs:
                      cols_.append(cacc_)
                      cacc_ += n

                  def h2_stage_a(si):
                      nonlocal_stat = h2_stat
                      c0, n, xsrc, ydst = subs[si]
                      sc = nonlocal_stat[0]
                      nonlocal_stat[0] += 1
                      hb = h2sb[si % 2]
                      hk = ("h2s", si % 2)
                      P.op("act", lambda e, n=n, si=si, sc=sc: e.activation(out=junk2[0:n, :], in_=xg[0:n, si, :], func=AF.Square,
                                                                            accum_out=ssq2[0:n, sc:sc + 1]),
                           reads=[("xg", si)], writes=["junk2", ("ssq2", sc)])
                      P.op("act", lambda e, n=n, sc=sc: e.activation(out=rstd2[0:n, sc:sc + 1], in_=ssq2[0:n, sc:sc + 1], func=AF.Sqrt,
                                                                     scale=1.0 / D, bias=EPS), reads=[("ssq2", sc)], writes=[("rstd2", sc)])
                      P.op("dve", lambda e, n=n, sc=sc: e.reciprocal(out=rstd2[0:n, sc:sc + 1], in_=rstd2[0:n, sc:sc + 1]),
                           reads=[("rstd2", sc)], writes=[("rstd2", sc)])
                      P.op("dve", lambda e, n=n, si=si, sc=sc, hb=hb: e.scalar_tensor_tensor(
                          out=hb[0:n, :], in0=xg[0:n, si, :], scalar=rstd2[0:n, sc:sc + 1], in1=gffn_b[0:n, :],
                          op0=ALU.mult, op1=ALU.mult), reads=[("xg", si), ("rstd2", sc), "gffnb"], writes=[hk])

                  def h2_stage_b(si):
                      c0, n, xsrc, ydst = subs[si]
                      hb = h2sb[si % 2]
                      hk = ("h2s", si % 2)
                      col = cols_[si]
                      for half in range(2):
                          calls = [(PTb[:, k, 0:n], hb[0:n, (half * 8 + k) * 128:(half * 8 + k + 1) * 128], ident_b[0:n, 0:n])
                                   for k in range(8)]
                          _tr(P, calls, reads=[hk, "identb"], writes=["PTb"])
                          if half == 0:
                              P.op("act", lambda e, col=col, n=n: e.copy(out=H2T[:, 0:8, col:col + n], in_=PTb[:, :, 0:n]),
                                   reads=["PTb"], writes=[("H2T", si)])
                          else:
                              P.op("dve", lambda e, col=col, n=n: e.tensor_copy(out=H2T[:, 8:16, col:col + n], in_=PTb[:, :, 0:n]),
                                   reads=["PTb"], writes=[("H2T", si)])

                  h2_stat = [stat_i]
                  h2_stage_a(0)
                  for si in range(len(subs)):
                      if si + 1 < len(subs):
                          h2_stage_a(si + 1)
                      h2_stage_b(si)
                  stat_i = h2_stat[0]
                  H2K = [("H2T", si) for si in range(len(subs))]
                  for pi, (j0, nj) in enumerate(PARTS):
                      for q in range(nj // 2):
                          gt, gkey = wget(("gate", g, pi, q))
                          ut, ukey = wget(("up", g, pi, q))
                          gslot, uslot = v16(gt), v16(ut)
                          for jj in range(2):
                              jl = q * 2 + jj
                              pg = PG[gi_ % 2]
                              pu = PU[gi_ % 2]
                              pgk, puk = ("PA", gi_ % 2), ("PS", gi_ % 2)
                              sg = sgt[gi_ % 2]
                              sgk = ("sgt", gi_ % 2)
                              gi_ += 1
                              for (wslot, wk, pp, ppk) in ((gslot, gkey, pg, pgk), (uslot, ukey, pu, puk)):
                                  calls = [(pp[:, 0:512], wslot[:, kc, jj * 128:(jj + 1) * 128], H2T[:, kc, 0:512], kc == 0, kc == 15)
                                           for kc in range(16)]
                                  _mm(P, calls, reads=[wk] + H2K, writes=[ppk])
                              P.op("act", lambda e, sg=sg, pg=pg: e.activation(out=sg[:, 0:512], in_=pg[:, 0:512], func=AF.Silu),
                                   reads=[pgk], writes=[sgk])
                              P.op("dve", lambda e, sg=sg, pu=pu, jl=jl: e.tensor_tensor(out=aT[:, jl, 0:512], in0=sg[:, 0:512],
                                                                                         in1=pu[:, 0:512], op=ALU.mult),
                                   reads=[sgk, puk], writes=[("aT", jl)])
                              if ncol > 512:
                                  for wi, (wslot, wk) in enumerate(((gslot, gkey), (uslot, ukey))):
                                      calls = [(PT[:, wi, 0:NS], wslot[:, kc, jj * 128:(jj + 1) * 128], H2T[:, kc, 512:NW],
                                                kc == 0, kc == 15, True) for kc in range(16)]
                                      _mm(P, calls, reads=[wk] + H2K, writes=["PT"])
                                  P.op("act", lambda e, sg=sg: e.activation(out=sg[:, 512:NW], in_=PT[:, 0, 0:NS], func=AF.Silu),
                                       reads=["PT"], writes=[sgk])
                                  P.op("dve", lambda e, sg=sg, jl=jl: e.tensor_tensor(out=aT[:, jl, 512:NW], in0=sg[:, 512:NW],
                                                                                      in1=PT[:, 1, 0:NS], op=ALU.mult),
                                       reads=[sgk, "PT"], writes=[("aT", jl)])
                          wdone(("gate", g, pi, q))
                          wdone(("up", g, pi, q))
                      ATK = [("aT", jl) for jl in range(nj)]
                      for cb in range(4):
                          dt_, wkey = wget(("down", g, pi, cb))
                          slot = v8(dt_)
                          col = 0
                          for si, (c0, n, xsrc, ydst) in enumerate(subs):
                              pd = PD[pd_i % 2]
                              pdk = ("PO", pd_i % 2)
                              pd_i += 1
                              calls = [(pd[0:n, :], aT[:, jl, col:col + n], slot[:, jl, :], jl == 0, jl == nj - 1) for jl in range(nj)]
                              _mm(P, calls, reads=[wkey] + ATK, writes=[pdk])
                              P.op("dve", lambda e, pd=pd, n=n, si=si, cb=cb: e.tensor_tensor(
                                  out=xg[0:n, si, cb * 512:(cb + 1) * 512], in0=xg[0:n, si, cb * 512:(cb + 1) * 512], in1=pd[0:n, :],
                                  op=ALU.add), reads=[pdk, ("xg", si)], writes=[("xg", si)])
                              col += n
                          wdone(("down", g, pi, cb))
                  for si, (c0, n, xsrc, ydst) in enumerate(subs):
                      sc = stat_i
                      stat_i += 1
                      P.op("act", lambda e, n=n, si=si, sc=sc: e.activation(out=junk2[0:n, :], in_=xg[0:n, si, :], func=AF.Square,
                                                                            accum_out=ssq2[0:n, sc:sc + 1]),
                           reads=[("xg", si)], writes=["junk2", ("ssq2", sc)])
                      P.op("act", lambda e, n=n, sc=sc: e.activation(out=rstd2[0:n, sc:sc + 1], in_=ssq2[0:n, sc:sc + 1], func=AF.Sqrt,
                                                                     scale=1.0 / D, bias=EPS), reads=[("ssq2", sc)], writes=[("rstd2", sc)])
                      P.op("dve", lambda e, n=n, sc=sc: e.reciprocal(out=rstd2[0:n, sc:sc + 1], in_=rstd2[0:n, sc:sc + 1]),
                           reads=[("rstd2", sc)], writes=[("rstd2", sc)])
                      P.op("dve", lambda e, n=n, si=si, sc=sc: e.scalar_tensor_tensor(
                          out=xg[0:n, si, :], in0=xg[0:n, si, :], scalar=rstd2[0:n, sc:sc + 1], in1=gfin_b[0:n, :],
                          op0=ALU.mult, op1=ALU.mult), reads=[("xg", si), ("rstd2", sc), "gfinb"], writes=[("xg", si)])
                      P.dma("sp", ydst, xg[0:n, si, :], reads=[("xg", si)], is_output=True)
          P.finish(block)
        if _os.environ.get("PROG_LOG"):
            with open(_os.environ["PROG_LOG"], "w") as fh:
                for rec in P.log:
                    fh.write(repr(rec) + "\n")
    return nc


def _t5_bucket(dist):
    dist = np.asarray(dist)
    df = np.maximum(dist, 1).astype(np.float32)
    large = 16 + (np.log(df / np.float32(16)) / np.float32(math.log(2048 / 16)) * np.float32(16)).astype(np.int32)
    large = np.minimum(large, 31)
    return np.where(dist < 16, dist, large)


def _bias_tables(rel_bias):
    rb = np.asarray(rel_bias, dtype=np.float32)
    ki = np.arange(128)[:, None]
    qc = np.arange(256)[None, :]
    dist = np.where(qc < 128, qc - ki, 128 + (qc - 128) - ki)
    valid = (dist >= 0) & (dist <= 128)
    distc = np.clip(dist, 0, 128)
    btab = np.empty((8, 128, 2, 3, 256), np.float32)
    for bi, d in enumerate((1, 4, 16)):
        bucket = _t5_bucket(d * distc)
        for h in range(16):
            vals = rb[bucket, h]
            btab[h // 2, :, h % 2, bi, :] = np.where(valid, vals, np.float32(NEG))
    btab = btab.reshape(8, 128, 2 * 3 * 256)
    sb = np.full((12, 128, 8, 2, 4, 4), NEG, np.float32)
    i = np.arange(128)
    for h in range(16):
        hp, hh = h // 2, h % 2
        for t in range(4):
            dA = 128 + t - i
            sb[0, :, hp, hh, :, t] = np.where(i >= t, rb[_t5_bucket(np.clip(dA, 0, 128)), h], np.float32(NEG))[:, None]
            sb[1 + t, :, hp, hh, :, t] = rb[_t5_bucket(4 * (128 - i)), h][:, None]
            sb[5 + t, :, hp, hh, :, t] = rb[_t5_bucket(16 * (128 - i)), h][:, None]
            for bq in range(4):
                for tp in range(t + 1):
                    sb[9, 4 * bq + tp, hp, hh, bq, t] = rb[_t5_bucket(np.array(t - tp)), h]
                sb[10, 4 * bq + t, hp, hh, bq, t] = rb[0, h]
                sb[11, 4 * bq + t, hp, hh, bq, t] = rb[0, h]
    sbias = np.ascontiguousarray(sb.reshape(12, 128, 256).transpose(1, 0, 2)).reshape(128, 12 * 256)
    return btab, sbias


def _selab():
    sel = np.zeros((128, 256), np.float32)
    for m in range(64):
        sel[m + 64, m] = 1.0
    for m in range(64, 128):
        sel[m - 64, 128 + m] = 1.0
    return sel


_NC_CACHE = {}


def kernel(x_prompt, x_sample, cache_k, cache_v, state_pool, rel_bias, norm_mix, w_in, w_pool, pool_scale,
           w_out, norm_ffn, w_gate, w_up, w_down, norm_final):
    f32 = lambda a: np.ascontiguousarray(np.asarray(a, dtype=np.float32))
    x_prompt, x_sample = f32(x_prompt), f32(x_sample)
    cache_k, cache_v, state_pool = f32(cache_k), f32(cache_v), f32(state_pool)
    btab, sbias = _bias_tables(rel_bias)
    shared = {
        "w_in": f32(w_in)[0], "w_pool": f32(w_pool)[0], "w_out": f32(w_out)[0], "w_gate": f32(w_gate)[0],
        "w_up": f32(w_up)[0], "w_down": f32(w_down)[0],
        "g_mix": np.ascontiguousarray(np.broadcast_to(f32(norm_mix).reshape(1, D), (128, D))),
        "g_ffn": np.ascontiguousarray(np.broadcast_to(f32(norm_ffn).reshape(1, D), (128, D))),
        "g_fin": np.ascontiguousarray(np.broadcast_to(f32(norm_final).reshape(1, D), (128, D))),
        "pscale": np.ascontiguousarray(f32(pool_scale).reshape(8, 128).T),
        "btab": btab, "sbias": sbias,
        "ident": np.eye(128, dtype=np.float32),
        "selab": _selab(),
        "invc": np.ascontiguousarray(np.broadcast_to((1.0 / np.arange(1, 17, dtype=np.float32))[None, :], (128, 16))),
    }
    in_maps = []
    for c in range(NCORES):
        m = dict(shared)
        m["xp"] = x_prompt[c]
        m["xs"] = np.ascontiguousarray(x_sample[4 * c:4 * c + 4].reshape(NS, D))
        m["ck"] = np.ascontiguousarray(cache_k[0, 4 * c:4 * c + 4].reshape(4, 2048, 1024))
        m["cv"] = np.ascontiguousarray(cache_v[0, 4 * c:4 * c + 4].reshape(4, 2048, 1024))
        m["spst"] = np.ascontiguousarray(state_pool[0, 4 * c:4 * c + 4])
        in_maps.append(m)
    if "nc" not in _NC_CACHE:
        _NC_CACHE["nc"] = build_program()
    nc = _NC_CACHE["nc"]
    res = run_bass_kernel_spmd(nc, in_maps, core_ids=list(range(NCORES)))
    R = res.results
    y_prompt = np.stack([R[c]["yp"] for c in range(NCORES)], 0)
    y_sample = np.concatenate([R[c]["ys"].reshape(4, 4, D) for c in range(NCORES)], 0)
    prompt_k = np.stack([R[c]["pk"].reshape(S, 16, 64) for c in range(NCORES)], 0)[None]
    prompt_v = np.stack([R[c]["pv"].reshape(S, 16, 64) for c in range(NCORES)], 0)[None]
    prompt_pool = np.stack([R[c]["ppool"] for c in range(NCORES)], 0)[None]
    sample_k = np.concatenate([R[c]["sk"].reshape(4, 4, 16, 64) for c in range(NCORES)], 0)[None]
    sample_v = np.concatenate([R[c]["sv"].reshape(4, 4, 16, 64) for c in range(NCORES)], 0)[None]
    sample_pool = np.concatenate([R[c]["spool"] for c in range(NCORES)], 0)[None]
    return (y_prompt.astype(np.float32), y_sample.astype(np.float32), prompt_k.astype(np.float32),
            prompt_v.astype(np.float32), prompt_pool.astype(np.float32), sample_k.astype(np.float32),
            sample_v.astype(np.float32), sample_pool.astype(np.float32))
```
